# Optimizing a Trainium2 kernel written in Bass

```python
import math
import jax, jax.numpy as jnp
from jax import lax
import numpy as np

D_MODEL = 1024
BATCH = 16
SEQ = 2048
DEPTH = 1
DEC_BATCH = 32
DEC_SEQ = 32
PAST_LEN = 2048

CHUNK = 64
N_META = 16
ATT_HEADS = 8
HEAD_DIM = 64
ATT_WIDTH = ATT_HEADS * HEAD_DIM
CONV_CH = D_MODEL - ATT_WIDTH
CONV_K = 31
D_FF = 4 * D_MODEL
Q_BLOCK = 128
IN_COLS = 3 * ATT_WIDTH + 2 * CONV_CH
ALPHA = (2.0 * DEPTH) ** 0.25
BETA_INIT = (8.0 * DEPTH) ** -0.25
LN_EPS = 1e-5

kernel_name = "hymba_stickbreak_conformer_stream_step"


def layer_norm(x, g, b):
    xf = x.astype(jnp.float32)
    mu = jnp.mean(xf, axis=-1, keepdims=True)
    var = jnp.mean(jnp.square(xf - mu), axis=-1, keepdims=True)
    y = (xf - mu) * lax.rsqrt(var + LN_EPS) * g.astype(jnp.float32) + b.astype(jnp.float32)
    return y.astype(x.dtype)


def project_in(h, w):
    B, T, _ = h.shape
    p = h @ w
    q, k, v, a, gate = jnp.split(
        p, [ATT_WIDTH, 2 * ATT_WIDTH, 3 * ATT_WIDTH, 3 * ATT_WIDTH + CONV_CH], axis=-1)
    shp = (B, T, ATT_HEADS, HEAD_DIM)
    return q.reshape(shp), k.reshape(shp), v.reshape(shp), a, gate


def stick_breaking_block(q, k, v, q_pos, k_pos):
    z = jnp.einsum('bqhd,bshd->bhqs', q, k).astype(jnp.float32) / math.sqrt(HEAD_DIM)
    mask = k_pos[None, :] < q_pos[:, None]
    log_beta = jax.nn.log_sigmoid(z)
    log_1m = jnp.where(mask, jax.nn.log_sigmoid(-z), 0.0)
    suffix = lax.cumsum(log_1m, axis=3, reverse=True) - log_1m
    a = jnp.where(mask, jnp.exp(log_beta + suffix), 0.0)
    return jnp.einsum('bhqs,bshd->bqhd', a.astype(v.dtype), v)


def stick_breaking_prompt(q, k, v):
    B, L, H, dh = q.shape
    nb = -(-L // Q_BLOCK)
    lp = nb * Q_BLOCK
    qp = jnp.pad(q, ((0, 0), (0, lp - L), (0, 0), (0, 0)))
    q_blocks = qp.reshape(B, nb, Q_BLOCK, H, dh).transpose(1, 0, 2, 3, 4)
    pos_blocks = jnp.arange(lp, dtype=jnp.int32).reshape(nb, Q_BLOCK)
    k_pos = jnp.arange(L, dtype=jnp.int32)
    out = lax.map(lambda qb: stick_breaking_block(qb[0], k, v, qb[1], k_pos),
                  (q_blocks, pos_blocks))
    return out.transpose(1, 0, 2, 3, 4).reshape(B, lp, H, dh)[:, :L]


def conv_module(a, gate, buf, w_dw, b_dw, g_c, b_c):
    u = a * jax.nn.sigmoid(gate)
    xp = jnp.concatenate([buf, u], axis=1)
    c = lax.conv_general_dilated(
        xp, w_dw[:, None, :], window_strides=(1,), padding='VALID',
        dimension_numbers=('NWC', 'WIO', 'NWC'), feature_group_count=CONV_CH)
    c = jax.nn.silu(layer_norm(c + b_dw, g_c, b_c))
    return c, xp[:, -(CONV_K - 1):]


def layer_tail(h, att, conv, w_out, g1, b1, w_ff1, w_ff2, g2, b2):
    B, T, _ = h.shape
    mix = jnp.concatenate([att.reshape(B, T, ATT_WIDTH), conv], axis=-1) @ w_out
    h1 = layer_norm(ALPHA * h + mix, g1, b1)
    f = jnp.square(jax.nn.relu(h1 @ w_ff1)) @ w_ff2
    return layer_norm(ALPHA * h1 + f, g2, b2)


def setup_inputs(seed: int = 0) -> dict:
    key = jax.random.key(seed)
    ks = jax.random.split(key, 20)
    f32 = jnp.float32
    nrm = lambda k, s, sc: jax.random.normal(k, s, f32) * sc
    return {
        "x_prompt": nrm(ks[0], (BATCH, SEQ, D_MODEL), 1.0),
        "x_sample": nrm(ks[1], (DEC_BATCH, DEC_SEQ, D_MODEL), 1.0),
        "cache_k": nrm(ks[2], (DEPTH, DEC_BATCH, PAST_LEN, ATT_HEADS, HEAD_DIM), 1.0),
        "cache_v": nrm(ks[3], (DEPTH, DEC_BATCH, PAST_LEN, ATT_HEADS, HEAD_DIM), 1.0),
        "state_conv": nrm(ks[4], (DEPTH, DEC_BATCH, CONV_K - 1, CONV_CH), 0.5),
        "meta": nrm(ks[5], (N_META, D_MODEL), 1.0),
        "g_in": 1.0 + nrm(ks[6], (D_MODEL,), 0.02),
        "b_in": nrm(ks[7], (D_MODEL,), 0.02),
        "w_in": nrm(ks[8], (DEPTH, D_MODEL, IN_COLS), D_MODEL ** -0.5),
        "w_dw": nrm(ks[9], (DEPTH, CONV_K, CONV_CH), CONV_K ** -0.5),
        "b_dw": nrm(ks[10], (DEPTH, CONV_CH), 0.02),
        "g_conv": 1.0 + nrm(ks[11], (DEPTH, CONV_CH), 0.02),
        "b_conv": nrm(ks[12], (DEPTH, CONV_CH), 0.02),
        "w_out": nrm(ks[13], (DEPTH, D_MODEL, D_MODEL), D_MODEL ** -0.5 * BETA_INIT),
        "g_ln1": 1.0 + nrm(ks[14], (DEPTH, D_MODEL), 0.02),
        "b_ln1": nrm(ks[15], (DEPTH, D_MODEL), 0.02),
        "w_ff1": nrm(ks[16], (DEPTH, D_MODEL, D_FF), D_MODEL ** -0.5),
        "w_ff2": nrm(ks[17], (DEPTH, D_FF, D_MODEL), D_FF ** -0.5 * BETA_INIT),
        "g_ln2": 1.0 + nrm(ks[18], (DEPTH, D_MODEL), 0.02),
        "b_ln2": nrm(ks[19], (DEPTH, D_MODEL), 0.02),
    }


def reference(x_prompt, x_sample, cache_k, cache_v, state_conv, meta, g_in, b_in, w_in,
              w_dw, b_dw, g_conv, b_conv, w_out, g_ln1, b_ln1, w_ff1, w_ff2, g_ln2, b_ln2):
    B = x_prompt.shape[0]
    DB, n = x_sample.shape[0], x_sample.shape[1]
    P = cache_k.shape[2]
    meta_b = jnp.broadcast_to(meta[None].astype(x_prompt.dtype), (B, N_META, D_MODEL))
    hp = layer_norm(jnp.concatenate([meta_b, x_prompt], axis=1), g_in, b_in)
    hs = layer_norm(x_sample, g_in, b_in)
    q_pos_s = P + jnp.arange(n, dtype=jnp.int32)
    k_pos_s = jnp.arange(P + n, dtype=jnp.int32)
    kp_l, vp_l, cp_l, ks_l, vs_l, cs_l = [], [], [], [], [], []
    for l in range(DEPTH):
        qp, kp, vp, ap, gp = project_in(hp, w_in[l])
        att_p = stick_breaking_prompt(qp, kp, vp)
        buf0 = jnp.zeros((B, CONV_K - 1, CONV_CH), ap.dtype)
        conv_p, cst_p = conv_module(ap, gp, buf0, w_dw[l], b_dw[l], g_conv[l], b_conv[l])
        hp = layer_tail(hp, att_p, conv_p, w_out[l], g_ln1[l], b_ln1[l],
                        w_ff1[l], w_ff2[l], g_ln2[l], b_ln2[l])
        kp_l.append(kp); vp_l.append(vp); cp_l.append(cst_p)
        qs, kn, vn, an, gn = project_in(hs, w_in[l])
        k_all = jnp.concatenate([cache_k[l].astype(kn.dtype), kn], axis=1)
        v_all = jnp.concatenate([cache_v[l].astype(vn.dtype), vn], axis=1)
        att_s = stick_breaking_block(qs, k_all, v_all, q_pos_s, k_pos_s)
        conv_s, cst_s = conv_module(an, gn, state_conv[l].astype(an.dtype),
                                    w_dw[l], b_dw[l], g_conv[l], b_conv[l])
        hs = layer_tail(hs, att_s, conv_s, w_out[l], g_ln1[l], b_ln1[l],
                        w_ff1[l], w_ff2[l], g_ln2[l], b_ln2[l])
        ks_l.append(kn); vs_l.append(vn); cs_l.append(cst_s)
    y_prompt = hp[:, N_META:]
    return (y_prompt, hs, jnp.stack(kp_l), jnp.stack(vp_l), jnp.stack(cp_l),
            jnp.stack(ks_l), jnp.stack(vs_l), jnp.stack(cs_l))
```

```python
import numpy as np
from contextlib import ExitStack
import concourse.bass as bass
import concourse.mybir as mybir
from concourse.bass_utils import run_bass_kernel_spmd

F32 = mybir.dt.float32
BF16 = mybir.dt.bfloat16
ALU = mybir.AluOpType
AF = mybir.ActivationFunctionType

LN_EPS = 1e-5
BIG = 30000.0
NCONST = 6


class Cfg:
    def __init__(s, D=1024, H=8, DFF=4096, SEQ=2048, NPS=2, NSS=4, DSEQ=32, PAST=2048,
                 NMETA=16, CK=31, NHALF=2, depth=1, stop=None, merge=True):
        s.stop = stop
        s.D, s.H, s.DFF, s.SEQ, s.NPS, s.NSS, s.DSEQ, s.PAST = D, H, DFF, SEQ, NPS, NSS, DSEQ, PAST
        s.NMETA, s.CK, s.NHALF = NMETA, CK, NHALF
        s.AW = H * 64
        s.CC = D - s.AW
        s.KC = D // 128
        s.HP = H // 2
        s.CT = s.CC // 128
        s.NFC = DFF // 512
        s.NF = SEQ // 128
        s.LP = NMETA + SEQ
        s.INC = 3 * s.AW + 2 * s.CC
        s.ALPHA = (2.0 * depth) ** 0.25
        s.OCW = min(512, D)
        s.NOC = D // s.OCW
        assert s.AW <= 512 and s.CC <= 512 and s.KC * 512 <= 4096
        assert s.NF % NHALF == 0
        s.TPH = s.NF // NHALF
        s.MERGE = merge
        s.TPMAX = NMETA + s.TPH * 128
        s.TMAX = max(s.TPMAX, s.TPH * 128 + NSS * DSEQ if merge else NSS * DSEQ)
        s.NTMAX = s.TPH + 1
        s.LMAX = max(s.LP, PAST + DSEQ)
        s.NKB = max(1 + s.NF, PAST // 128 + 1)
        s.HW = CK - 1


class Ctx:
    def __init__(s, nc, stack):
        s.nc, s.stack = nc, stack
        s.sems = {}
        s.dcnt = {}

    def sem(s, key):
        if key not in s.sems:
            s.sems[key] = s.stack.enter_context(s.nc.semaphore("s%d" % len(s.sems)))
            s.dcnt[key] = 0
        return s.sems[key]


class Eng:
    def __init__(s, ctx, name, e):
        s.ctx, s.name, s.e = ctx, name, e
        s.key = ("eng", name)
        ctx.sem(s.key)
        s.cnt = 0
        s.seen = {}

    def wait(s, k, v):
        if s.seen.get(k, 0) >= v:
            return
        s.e.wait_ge(s.ctx.sems[k], v)
        s.seen[k] = v

    def sig(s, inst):
        inst.then_inc(s.ctx.sems[s.key], 1)
        s.cnt += 1
        return (s.key, s.cnt)


class Dep:
    def __init__(s):
        s.w = {}
        s.r = {}

    def pre(s, eng, reads, writes):
        if eng.name == "pe":
            return s.pre_pe(eng, reads, writes)
        for k in reads:
            for sk, v in s.w.get(k, {}).items():
                eng.wait(sk, v)
            if isinstance(k, tuple) and k[0] == "PS":
                for sk, v in s.r.get(k, {}).items():
                    if sk != eng.key:
                        eng.wait(sk, v)
        for k in writes:
            for sk, v in s.w.get(k, {}).items():
                eng.wait(sk, v)
            for sk, v in s.r.get(k, {}).items():
                eng.wait(sk, v)

    def pre_pe(s, eng, reads, writes):
        for k in reads:
            for sk, v in s.w.get(k, {}).items():
                if sk != eng.key:
                    eng.wait(sk, v)
        for k in writes:
            for d in (s.w.get(k, {}), s.r.get(k, {})):
                for sk, v in d.items():
                    if sk != eng.key:
                        eng.wait(sk, v)

    def post(s, ev, reads, writes):
        for k in reads:
            d = s.r.setdefault(k, {})
            d[ev[0]] = max(d.get(ev[0], 0), ev[1])
        for k in writes:
            d = s.w.setdefault(k, {})
            d[ev[0]] = max(d.get(ev[0], 0), ev[1])


def build(cfg):
    c = cfg
    nc = bass.Bass("TRN2", target_bir_lowering=False)
    D, KC, HP, CT, AW, CC, H = c.D, c.KC, c.HP, c.CT, c.AW, c.CC, c.H
    HWc = c.HW

    def din(name, shape):
        return nc.dram_tensor(name, list(shape), F32, kind="ExternalInput").ap()

    def dout(name, shape):
        return nc.dram_tensor(name, list(shape), F32, kind="ExternalOutput").ap()

    xp = din("xp", [c.NPS, c.SEQ, D])
    xs = din("xs", [c.NSS * c.DSEQ, D])
    ck = din("ck", [c.NSS, c.PAST, AW])
    cv = din("cv", [c.NSS, c.PAST, AW])
    sc = din("sc", [c.NSS, HWc, CC])
    meta = din("meta", [c.NMETA, D])
    lnp = din("lnp", [6, D])
    w_in = din("w_in", [D, c.INC])
    w_dw = din("w_dw", [c.CK, CC])
    cvp = din("cvp", [3, CC])
    w_out = din("w_out", [D, D])
    w_ff1 = din("w_ff1", [D, c.DFF])
    w_ff2 = din("w_ff2", [c.DFF, D])
    consts = din("consts", [128, NCONST * 128])

    yp = dout("yp", [c.NPS, c.SEQ, D])
    ys = dout("ys", [c.NSS * c.DSEQ, D])
    kp = dout("kp", [c.NPS, c.LP, AW])
    vp = dout("vp", [c.NPS, c.LP, AW])
    cp = dout("cp", [c.NPS, HWc, CC])
    ksm = dout("ksm", [c.NSS * c.DSEQ, AW])
    vsm = dout("vsm", [c.NSS * c.DSEQ, AW])
    cs = dout("cs", [c.NSS, HWc, CC])

    stack = ExitStack()
    with stack:
        ctx = Ctx(nc, stack)
        dep = Dep()
        PE = Eng(ctx, "pe", nc.tensor)
        ACT = Eng(ctx, "act", nc.scalar)
        DVE = Eng(ctx, "dve", nc.vector)
        POOL = Eng(ctx, "pool", nc.gpsimd)
        SP = Eng(ctx, "sp", nc.sync)

        def sb(name, shape, dt):
            return stack.enter_context(nc.sbuf_tensor(name, list(shape), dt))

        def op(eng, fn, reads=(), writes=()):
            dep.pre(eng, reads, writes)
            inst = fn()
            ev = eng.sig(inst)
            dep.post(ev, reads, writes)
            return inst

        def pe(fns, reads=(), writes=()):
            dep.pre(PE, reads, writes)
            inst = None
            for f in fns:
                inst = f()
            ev = PE.sig(inst)
            dep.post(ev, reads, writes)

        def dma(q, out, in_, semkey, reads=(), writes=()):
            dmas(q, [(out, in_)], semkey, reads, writes)

        def dmas(q, pairs, semkey, reads=(), writes=()):
            dep.pre(q, reads, writes)
            sem = ctx.sem(("dma", semkey))
            for out, in_ in pairs:
                q.e.dma_start(out=out, in_=in_).then_inc(sem, 16)
                ctx.dcnt[("dma", semkey)] += 16
            ev = (("dma", semkey), ctx.dcnt[("dma", semkey)])
            dep.post(ev, reads, writes)

        cst_f = sb("cst_f", [128, 2 * 128], F32)
        cst_b = sb("cst_b", [128, NCONST * 128], BF16)
        ident_f = cst_f[:, 0:128]
        ones_f = cst_f[:, 128:256]
        ident_b = cst_b[:, 0:128]
        ones_b = cst_b[:, 128:256]
        tri_b = cst_b[:, 256:384]
        mbig_b = cst_b[:, 384:512]
        e0_b = cst_b[:, 512:640]
        u_b = cst_b[:, 640:768]

        gtile = sb("gtile", [128, D], F32)
        btile = sb("btile", [128, D], F32)
        wdw_c = sb("wdw_c", [128, CT, c.CK], F32)
        cvp_c = sb("cvp_c", [128, 3, CT], F32)
        diag = [sb("diag%d" % i, [128, c.CK, 128], BF16) for i in range(2)]
        dg_rr = [0]
        hid_rr = [0]
        tmpH = sb("tmpH", [128, CT, HWc], BF16)

        Hs = sb("Hs", [128, c.NTMAX, D], F32)
        XT = sb("XT", [128, KC, c.TMAX], BF16)
        qz = sb("qz", [128, H, c.TMAX], BF16)
        kTn = sb("kTn", [128, HP, c.LMAX], BF16)
        vh = sb("vh", [128, c.NKB, AW], BF16)
        UBS = HWc + c.TPMAX
        UW = UBS + c.NSS * (HWc + c.DSEQ)
        ub = sb("ub", [128, CT, UW], BF16)
        NRING = 4
        ring = [sb("ring%d" % i, [128, 4096], BF16) for i in range(NRING)]
        cfh = sb("cfh", [128, max(CT * 1024, 4096)], BF16)
        hidT = [cfh[:, i * 2048:(i + 1) * 2048].rearrange("p (k t) -> p k t", k=4) for i in range(2)]
        cf = cfh[:, 0:CT * 1024].bitcast(F32).rearrange("p (k t) -> p k t", k=CT)
        e_t = [sb("e_t%d" % i, [128, 512], F32) for i in range(2)]
        sp_t = [sb("sp_t%d" % i, [128, 512], BF16) for i in range(2)]
        a_t = [sb("a_t%d" % i, [128, 512], BF16) for i in range(2)]
        NSTG = 3
        stg = [sb("stg%d" % i, [128, 512], F32) for i in range(NSTG)]
        tmpA = sb("tmpA", [128, 512], F32)
        tmpB = sb("tmpB", [128, 512], F32)
        rs_t = sb("rs_t", [128, 512], F32)
        mn_t = sb("mn_t", [128, 512], F32)
        st6 = sb("st6", [128, 2, 6], F32)
        mv = sb("mv", [128, 2], F32)
        rstd = sb("rstd", [128, 1], F32)
        nbias = sb("nbias", [128, 1], F32)
        kTnew = sb("kTnew", [128, HP, 128], BF16)
        vnew = sb("vnew", [128, AW], BF16)

        cf32 = cfh[:, 0:CT * 1024].bitcast(F32)
        kstage = [(stg[i], [("stg", i)]) for i in range(NSTG)]
        kstage += [(cf32[:, i * 512:(i + 1) * 512], [("cf", i), "cfh"]) for i in range(CT)]
        kst_rr = [0]
        PS = [stack.enter_context(nc.psum_tensor("ps%d" % i, [128, 512], F32)) for i in range(8)]
        ps_rr = [0]

        def psum_next(lo=0, hi=8):
            i = lo + ps_rr[0] % (hi - lo)
            ps_rr[0] += 1
            return i

        dma(SP, cst_f[:, :], consts[:, 0:256], "cst", writes=["cst_f"])
        dma(POOL, cst_b[:, :], consts[:, :], "cstb", writes=["cst_b"])
        prw, prc = stg[0], stg[1]
        dma(SP, prw[0:c.CK, 0:CC], w_dw[:, :], "prm_w", writes=[("stg", 0)])
        dma(SP, prc[0:3, 0:CC], cvp[:, :], "prm_c", writes=[("stg", 1)])
        for ct in range(CT):
            b = psum_next()
            pe([lambda: nc.tensor.transpose(out=PS[b][:, 0:c.CK], in_=prw[0:c.CK, ct * 128:(ct + 1) * 128],
                                            identity=ident_f[0:c.CK, 0:c.CK]),
                lambda: nc.tensor.transpose(out=PS[b][:, 32:35], in_=prc[0:3, ct * 128:(ct + 1) * 128],
                                            identity=ident_f[0:3, 0:3])],
               reads=[("stg", 0), ("stg", 1), "cst_f"], writes=[("PS", b)])
            op(DVE, lambda: nc.vector.tensor_copy(out=wdw_c[:, ct, :], in_=PS[b][:, 0:c.CK]), reads=[("PS", b)],
               writes=["wdw_c"])
            op(DVE, lambda: nc.vector.tensor_copy(out=cvp_c[:, :, ct], in_=PS[b][:, 32:35]), reads=[("PS", b)],
               writes=["cvp_c"])
        eps_c = sb("eps_c", [128, 1], F32)
        op(POOL, lambda: nc.gpsimd.memset(eps_c[:, :], LN_EPS), writes=["eps_c"])
        op(POOL, lambda: nc.gpsimd.memset(qz[:, :, :], 0.0), writes=["qz_init"])

        lnp_loaded = [None]

        def load_ln(which):
            if lnp_loaded[0] == which:
                return
            lnp_loaded[0] = which
            dma(ACT, gtile[:, :], lnp[2 * which, :].partition_broadcast(128), "gt", writes=["gtile"])
            dma(ACT, btile[:, :], lnp[2 * which + 1, :].partition_broadcast(128), "bt", writes=["btile"])

        chunks = []
        wstate = {"issued": 0, "cur": 0}

        def wsrc(desc):
            kind = desc[0]
            if kind in ("q", "k", "v"):
                c0 = {"q": 0, "k": AW, "v": 2 * AW}[kind]
                return [(0, KC, AW, w_in[:, c0:c0 + AW].rearrange("(k p) c -> p k c", p=128))]
            if kind == "ag":
                j = desc[1]
                cts = [t for t in (2 * j, 2 * j + 1) if t < CT]
                n = len(cts) * 128
                a0 = 3 * AW + cts[0] * 128
                g0 = 3 * AW + CC + cts[0] * 128
                return [("ag", n, w_in[:, a0:a0 + n].rearrange("(k p) c -> p k c", p=128),
                         w_in[:, g0:g0 + n].rearrange("(k p) c -> p k c", p=128))]
            if kind == "o":
                j = desc[1]
                return [(0, KC, c.OCW, w_out[:, j * c.OCW:(j + 1) * c.OCW].rearrange("(k p) c -> p k c", p=128))]
            if kind == "f1":
                j = desc[1]
                return [(0, KC, 512, w_ff1[:, j * 512:(j + 1) * 512].rearrange("(k p) c -> p k c", p=128))]
            if kind == "f2":
                j = desc[1]
                return [(0, 4, D, w_ff2[j * 512:(j + 1) * 512, :].rearrange("(k p) c -> p k c", p=128))]
            raise ValueError(kind)

        def wview(slot, a, b):
            return ring[slot][:, 0:a * b].rearrange("p (k c) -> p k c", k=a)

        def issue_chunk(m):
            slot = m % NRING
            desc = chunks[m]
            src = wsrc(desc)[0]
            key = ("W", slot)
            if src[0] == "ag":
                n = src[1]
                v = wview(slot, KC, 2 * n)
                dmas(POOL, [(v[:, :, 0:n], src[2]), (v[:, :, n:2 * n], src[3])], ("W", slot), writes=[key])
            else:
                _, a, b, ap = src
                dma(POOL, wview(slot, a, b), ap, ("W", slot), writes=[key])

        def wget():
            m = wstate["cur"]
            wstate["cur"] += 1
            while wstate["issued"] < min(len(chunks), m + NRING - 1):
                issue_chunk(wstate["issued"])
                wstate["issued"] += 1
            return m % NRING, ("W", m % NRING)

        passes = []
        for sq in range(c.NPS):
            for hf in range(c.NHALF):
                tiles = []
                col = 0
                if hf == 0:
                    tiles.append(dict(n=c.NMETA, c0=0, kb=0, pos=0, src=meta[:, :], y=None))
                    col = c.NMETA
                for f in range(hf * c.TPH, (hf + 1) * c.TPH):
                    tiles.append(dict(n=128, c0=col, kb=1 + f, pos=c.NMETA + 128 * f,
                                      src=xp[sq, 128 * f:128 * (f + 1), :], y=yp[sq, 128 * f:128 * (f + 1), :]))
                    col += 128
                groups = []
                i0 = 0
                if hf == 0:
                    groups.append([0])
                    i0 = 1
                fr = list(range(i0, len(tiles)))
                for g in range(0, len(fr), 4):
                    groups.append(fr[g:g + 4])
                passes.append(dict(kind="p", seq=sq, half=hf, tiles=tiles, groups=groups, T=col,
                                   last=(hf == c.NHALF - 1)))
        if c.NSS > 0:
            n = c.NSS * c.DSEQ
            stile = dict(n=n, c0=0, kb=None, pos=None, src=xs[:, :], y=ys[:, :], kind="s")
            if c.MERGE and passes:
                lp = passes[-1]
                stile["c0"] = lp["T"]
                lp["tiles"].append(stile)
                lp["groups"].append([len(lp["tiles"]) - 1])
                lp["Ttot"] = lp["T"] + n
            else:
                passes.append(dict(kind="s", tiles=[stile], groups=[[0]], T=0, Ttot=n, last=True, half=0, seq=0))
        for p_ in passes:
            p_.setdefault("Ttot", p_["T"])
        for p in passes:
            chunks.extend([("q",), ("k",), ("v",)])
            chunks.extend([("ag", j) for j in range((CT + 1) // 2)])
            chunks.extend([("o", j) for j in range(c.NOC)])
            for j in range(c.NFC):
                chunks.extend([("f1", j), ("f2", j)])

        def gcols(p, g):
            ts = [p["tiles"][i] for i in g]
            c0 = ts[0]["c0"]
            W = sum(t["n"] for t in ts)
            return c0, W

        def layer_norm(i, n):
            hk = ("H", i)
            for j in range(D // 512 if D >= 512 else 1):
                w = min(512, D)
                op(DVE, lambda j=j, w=w: nc.vector.bn_stats(out=st6[:n, j, :], in_=Hs[:n, i, j * w:(j + 1) * w]),
                   reads=[hk], writes=[("st6", j)])
            nj = D // 512 if D >= 512 else 1
            op(DVE, lambda: nc.vector.bn_aggr(out=mv[:n, :], in_=st6[:n, 0:nj, :]),
               reads=[("st6", j) for j in range(nj)], writes=["mv"])
            op(ACT, lambda: nc.scalar.activation(out=rstd[:n, :], in_=mv[:n, 1:2], func=AF.Ln, bias=eps_c[:n, :], scale=1.0),
               reads=["mv", "eps_c"], writes=["rstd"])
            op(ACT, lambda: nc.scalar.activation(out=rstd[:n, :], in_=rstd[:n, :], func=AF.Exp, scale=-0.5),
               reads=["rstd"], writes=["rstd"])
            op(DVE, lambda: nc.vector.scalar_tensor_tensor(out=nbias[:n, :], in0=mv[:n, 0:1], scalar=-1.0,
                                                           in1=rstd[:n, :], op0=ALU.mult, op1=ALU.mult),
               reads=["mv", "rstd"], writes=["nbias"])
            op(ACT, lambda: nc.scalar.activation(out=Hs[:n, i, :], in_=Hs[:n, i, :], func=AF.Identity,
                                                 bias=nbias[:n, :], scale=rstd[:n, :]),
               reads=[hk, "rstd", "nbias"], writes=[hk])
            op(DVE, lambda: nc.vector.tensor_tensor(out=Hs[:n, i, :], in0=Hs[:n, i, :], in1=gtile[:n, :], op=ALU.mult),
               reads=[hk, "gtile"], writes=[hk])
            op(DVE, lambda: nc.vector.tensor_tensor(out=Hs[:n, i, :], in0=Hs[:n, i, :], in1=btile[:n, :], op=ALU.add),
               reads=[hk, "btile"], writes=[hk])

        st6a = sb("st6a", [128, c.NTMAX, 2, 6], F32)
        mva = sb("mva", [128, c.NTMAX, 2], F32)
        rsa = sb("rsa", [128, c.NTMAX], F32)
        nba = sb("nba", [128, c.NTMAX], F32)
        op(DVE, lambda: nc.vector.memset(mva[:, :, :], 1.0), writes=["mva"])

        def ln_multi(p, tiles_all, ids):
            i0, NT = ids[0], ids[-1] + 1
            tiles = [(i, tiles_all[i]) for i in ids]
            nj = D // 512 if D >= 512 else 1
            w = min(512, D)
            for i, t in tiles:
                n = t["n"]
                for j in range(nj):
                    op(DVE, lambda j=j: nc.vector.bn_stats(out=st6a[:n, i, j, :], in_=Hs[:n, i, j * w:(j + 1) * w]),
                       reads=[("H", i)], writes=[("st6a", i, j)])
                op(DVE, lambda: nc.vector.bn_aggr(out=mva[:n, i, :], in_=st6a[:n, i, 0:nj, :]),
                   reads=[("st6a", i, j) for j in range(nj)], writes=["mva"])
            op(ACT, lambda: nc.scalar.activation(out=rsa[:, i0:NT], in_=mva[:, i0:NT, 1], func=AF.Ln, bias=eps_c[:, :], scale=1.0),
               reads=["mva", "eps_c"], writes=["rsa"])
            op(ACT, lambda: nc.scalar.activation(out=rsa[:, i0:NT], in_=rsa[:, i0:NT], func=AF.Exp, scale=-0.5),
               reads=["rsa"], writes=["rsa"])
            for i, t in tiles:
                n = t["n"]
                hk = ("H", i)
                op(DVE, lambda: nc.vector.scalar_tensor_tensor(out=Hs[:n, i, :], in0=Hs[:n, i, :], scalar=mva[:n, i, 0:1],
                                                               in1=gtile[:n, :], op0=ALU.subtract, op1=ALU.mult),
                   reads=[hk, "mva", "gtile"], writes=[hk])
                op(DVE, lambda: nc.vector.scalar_tensor_tensor(out=Hs[:n, i, :], in0=Hs[:n, i, :], scalar=rsa[:n, i:i + 1],
                                                               in1=btile[:n, :], op0=ALU.mult, op1=ALU.add),
                   reads=[hk, "rsa", "btile"], writes=[hk])

        def transpose_tile(p, i, n, c0):
            for k0 in range(0, KC, 4):
                kn = min(4, KC - k0)
                b = psum_next()
                pk = ("PS", b)
                pe([lambda j=j: nc.tensor.transpose(out=PS[b][:, j * 128:j * 128 + n],
                                                    in_=Hs[:n, i, (k0 + j) * 128:(k0 + j + 1) * 128],
                                                    identity=ident_f[:n, :n]) for j in range(kn)],
                   reads=[("H", i)], writes=[pk])
                src = PS[b][:, 0:kn * 128].rearrange("p (k t) -> p k t", k=kn)[:, :, 0:n]
                op(ACT, lambda src=src, k0=k0, kn=kn: nc.scalar.copy(out=XT[:, k0:k0 + kn, c0:c0 + n], in_=src),
                   reads=[pk], writes=[("XT", i)])

        stg_rr = [0]

        def stg_next():
            i = stg_rr[0] % NSTG
            stg_rr[0] += 1
            return i

        def run_pass(p):
            tiles, groups = p["tiles"], p["groups"]
            sq = p.get("seq", 0)
            T = p["T"]
            isS = lambda i: tiles[i].get("kind") == "s"
            gS = lambda g: isS(g[0])
            has_s = any(isS(i) for i in range(len(tiles)))
            has_p = any(not isS(i) for i in range(len(tiles)))
            SW = HWc + c.DSEQ
            xtk = lambda g: [("XT", i) for i in g]

            for i, t in enumerate(tiles):
                if not t.get("preloaded"):
                    dma(SP, Hs[:t["n"], i, :], t["src"], ("Hld", i), writes=[("H", i)])
            load_ln(0)
            for g in groups:
                ln_multi(p, tiles, g)

            if c.stop == "ln":
                return
            slot, wk = wget()
            Wt = wview(slot, KC, AW)
            for g in groups:
                c0, W = gcols(p, g)
                for i in g:
                    transpose_tile(p, i, tiles[i]["n"], tiles[i]["c0"])
                for hp in range(HP):
                    b = psum_next()
                    pe([lambda kc=kc: nc.tensor.matmul(PS[b][:, 0:W], lhsT=Wt[:, kc, hp * 128:(hp + 1) * 128],
                                                       rhs=XT[:, kc, c0:c0 + W], start=(kc == 0), stop=(kc == KC - 1))
                        for kc in range(KC)], reads=[wk] + xtk(g), writes=[("PS", b)])
                    op(ACT, lambda: nc.scalar.copy(out=qz[0:64, 2 * hp, c0:c0 + W], in_=PS[b][0:64, 0:W]),
                       reads=[("PS", b), "qz_init"], writes=[("qz", 2 * hp, g[0])])
                    op(DVE, lambda: nc.vector.tensor_copy(out=qz[64:128, 2 * hp + 1, c0:c0 + W], in_=PS[b][64:128, 0:W]),
                       reads=[("PS", b), "qz_init"], writes=[("qz", 2 * hp + 1, g[0])])
            if c.stop == "q":
                return
            slot, wk = wget()
            Wt = wview(slot, KC, AW)
            for g in groups:
                c0, W = gcols(p, g)
                for hp in range(HP):
                    b = psum_next()
                    pe([lambda kc=kc: nc.tensor.matmul(PS[b][:, 0:W], lhsT=Wt[:, kc, hp * 128:(hp + 1) * 128],
                                                       rhs=XT[:, kc, c0:c0 + W], start=(kc == 0), stop=(kc == KC - 1))
                        for kc in range(KC)], reads=[wk] + xtk(g), writes=[("PS", b)])
                    if gS(g):
                        dst, dk = kTnew[:, hp, 0:W], [("kTnew",)]
                    else:
                        pos = tiles[g[0]]["pos"]
                        dst, dk = kTn[:, hp, pos:pos + W], [("kT", tiles[i_]["kb"]) for i_ in g]
                    op(ACT, lambda dst=dst: nc.scalar.mul(out=dst, in_=PS[b][:, 0:W], mul=-0.125),
                       reads=[("PS", b)], writes=dk)
                for i in g:
                    t = tiles[i]
                    n = t["n"]
                    b = psum_next()
                    pe([lambda kc=kc: nc.tensor.matmul(PS[b][:n, 0:AW], lhsT=XT[:, kc, t["c0"]:t["c0"] + n],
                                                       rhs=Wt[:, kc, :], start=(kc == 0), stop=(kc == KC - 1))
                        for kc in range(KC)], reads=[wk, ("XT", i)], writes=[("PS", b)])
                    s_ = stg_next()
                    op(DVE, lambda: nc.vector.tensor_copy(out=stg[s_][:n, 0:AW], in_=PS[b][:n, 0:AW]),
                       reads=[("PS", b)], writes=[("stg", s_)])
                    dst = ksm[0:n, :] if isS(i) else kp[sq, t["pos"]:t["pos"] + n, :]
                    dma(SP, dst, stg[s_][:n, 0:AW], ("stg", s_), reads=[("stg", s_)])
            if c.stop == "k":
                return
            slot, wk = wget()
            Wt = wview(slot, KC, AW)
            for g in groups:
                for i in g:
                    t = tiles[i]
                    n = t["n"]
                    b = psum_next()
                    pe([lambda kc=kc: nc.tensor.matmul(PS[b][:n, 0:AW], lhsT=XT[:, kc, t["c0"]:t["c0"] + n],
                                                       rhs=Wt[:, kc, :], start=(kc == 0), stop=(kc == KC - 1))
                        for kc in range(KC)], reads=[wk, ("XT", i)], writes=[("PS", b)])
                    s_ = stg_next()
                    op(DVE, lambda: nc.vector.tensor_copy(out=stg[s_][:n, 0:AW], in_=PS[b][:n, 0:AW]),
                       reads=[("PS", b)], writes=[("stg", s_)])
                    dst = vsm[0:n, :] if isS(i) else vp[sq, t["pos"]:t["pos"] + n, :]
                    dma(SP, dst, stg[s_][:n, 0:AW], ("stg", s_), reads=[("stg", s_)])
                    if isS(i):
                        op(ACT, lambda: nc.scalar.copy(out=vnew[:n, :], in_=PS[b][:n, 0:AW]),
                           reads=[("PS", b)], writes=["vnew"])
                    else:
                        op(ACT, lambda: nc.scalar.copy(out=vh[:n, t["kb"], :], in_=PS[b][:n, 0:AW]),
                           reads=[("PS", b)], writes=[("vh", t["kb"])])
            if c.stop == "v":
                return
            tail_tiles = []
            ptiles = [i for i in range(len(tiles)) if not isS(i)]
            if has_p and p["last"]:
                tail_tiles.append(ptiles[-1])
            tail_tiles += [i for i in range(len(tiles)) if isS(i)]
            if has_p and p["half"] == 0:
                op(POOL, lambda: nc.gpsimd.memset(ub[:, :, 0:HWc], 0.0), writes=[("ubh",)])
            if has_s:
                for s in range(c.NSS):
                    s_ = stg_next()
                    dma(SP, stg[s_][:HWc, 0:CC], sc[s, :, :], ("stgl", s_), writes=[("stg", s_)])
                    for ct in range(CT):
                        b = psum_next()
                        pe([lambda: nc.tensor.transpose(out=PS[b][:, 0:HWc], in_=stg[s_][:HWc, ct * 128:(ct + 1) * 128],
                                                        identity=ident_f[:HWc, :HWc])],
                           reads=[("stg", s_)], writes=[("PS", b)])
                        op(ACT, lambda: nc.scalar.copy(out=ub[:, ct, UBS + s * SW:UBS + s * SW + HWc], in_=PS[b][:, 0:HWc]),
                           reads=[("PS", b)], writes=[("ubhs",)])
            ust = {}
            for j in range((CT + 1) // 2):
                slot, wk = wget()
                cts = [t_ for t_ in (2 * j, 2 * j + 1) if t_ < CT]
                nn = len(cts) * 128
                Wt = wview(slot, KC, 2 * nn)
                for g in groups:
                    c0, W = gcols(p, g)
                    for ci, ct in enumerate(cts):
                        ba = psum_next()
                        pe([lambda kc=kc: nc.tensor.matmul(PS[ba][:, 0:W], lhsT=Wt[:, kc, ci * 128:(ci + 1) * 128],
                                                           rhs=XT[:, kc, c0:c0 + W], start=(kc == 0), stop=(kc == KC - 1))
                            for kc in range(KC)], reads=[wk] + xtk(g), writes=[("PS", ba)])
                        bg = psum_next()
                        pe([lambda kc=kc: nc.tensor.matmul(PS[bg][:, 0:W], lhsT=Wt[:, kc, nn + ci * 128:nn + (ci + 1) * 128],
                                                           rhs=XT[:, kc, c0:c0 + W], start=(kc == 0), stop=(kc == KC - 1))
                            for kc in range(KC)], reads=[wk] + xtk(g), writes=[("PS", bg)])
                        op(ACT, lambda: nc.scalar.activation(out=tmpA[:, 0:W], in_=PS[bg][:, 0:W], func=AF.Sigmoid),
                           reads=[("PS", bg)], writes=["tmpA"])
                        if gS(g):
                            dst = ub[:, ct, UBS:UBS + c.NSS * SW].rearrange("p (s j) -> p s j", j=SW)[:, :, HWc:SW]
                            src0 = PS[ba][:, 0:W].rearrange("p (s j) -> p s j", j=c.DSEQ)
                            src1 = tmpA[:, 0:W].rearrange("p (s j) -> p s j", j=c.DSEQ)
                        else:
                            dst = ub[:, ct, HWc + c0:HWc + c0 + W]
                            src0, src1 = PS[ba][:, 0:W], tmpA[:, 0:W]
                        op(DVE, lambda dst=dst, src0=src0, src1=src1: nc.vector.tensor_tensor(out=dst, in0=src0, in1=src1,
                                                                                             op=ALU.mult),
                           reads=[("PS", ba), "tmpA", ("ubh",), ("ubhs",)], writes=[("ub", ct, g[0])])
                for i in tail_tiles:
                    t = tiles[i]
                    n = t["n"]
                    if i not in ust:
                        ust[i] = stg_next()
                    ba = psum_next()
                    pe([lambda kc=kc: nc.tensor.matmul(PS[ba][:n, 0:nn], lhsT=XT[:, kc, t["c0"]:t["c0"] + n],
                                                       rhs=Wt[:, kc, 0:nn], start=(kc == 0), stop=(kc == KC - 1))
                        for kc in range(KC)], reads=[wk, ("XT", i)], writes=[("PS", ba)])
                    bg = psum_next()
                    pe([lambda kc=kc: nc.tensor.matmul(PS[bg][:n, 0:nn], lhsT=XT[:, kc, t["c0"]:t["c0"] + n],
                                                       rhs=Wt[:, kc, nn:2 * nn], start=(kc == 0), stop=(kc == KC - 1))
                        for kc in range(KC)], reads=[wk, ("XT", i)], writes=[("PS", bg)])
                    op(ACT, lambda: nc.scalar.activation(out=tmpA[:n, 0:nn], in_=PS[bg][:n, 0:nn], func=AF.Sigmoid),
                       reads=[("PS", bg)], writes=["tmpA"])
                    cc0 = cts[0] * 128
                    op(DVE, lambda: nc.vector.tensor_tensor(out=stg[ust[i]][:n, cc0:cc0 + nn], in0=PS[ba][:n, 0:nn],
                                                            in1=tmpA[:n, 0:nn], op=ALU.mult),
                       reads=[("PS", ba), "tmpA"], writes=[("stg", ust[i])])
            for i in tail_tiles:
                u_ = ust[i]
                if isS(i):
                    dmas(SP, [(cs[s, :, :], stg[u_][s * c.DSEQ + (c.DSEQ - HWc):(s + 1) * c.DSEQ, 0:CC])
                              for s in range(c.NSS)], ("stg", u_), reads=[("stg", u_)])
                else:
                    n = tiles[i]["n"]
                    dma(SP, cp[sq, :, :], stg[u_][n - HWc:n, 0:CC], ("stg", u_), reads=[("stg", u_)])

            if c.stop == "inproj":
                return
            def attn(streams, kbl, mode="all", next_prologue=None):
                nkb = len(kbl)
                ofirst = {}
                xt = [tmpA, tmpB]
                xk = ["tmpA", "tmpB"]
                eb = [[e_t[0], rs_t], [e_t[1], mn_t]]
                ek = [[("e", 0), "rs_t"], [("e", 1), "mn_t"]]

                def ctxs(bi, st):
                    kb = kbl[bi]
                    si = st["si"]
                    return kb, si, kb["nk"], kb["lo"], bi == 0, bi == nkb - 1, st["W"], st["segs"], st["qk"]

                def emit_Z(bi):
                    for st in streams:
                        kb, si, nk, lo, first, last, Wd, segs, rq = ctxs(bi, st)
                        zb = si
                        fns = []
                        for (h, qc0, zc0, w) in segs:
                            lhs = kb["kT"](h // 2)
                            fns.append(lambda lhs=lhs, h=h, qc0=qc0, zc0=zc0, w=w, nk=nk, lo=lo, kb=kb, zb=zb:
                                       nc.tensor.matmul(PS[zb][:nk, zc0 + lo:zc0 + w], lhsT=lhs,
                                                        rhs=qz[:, h, qc0 + lo:qc0 + w], start=True, stop=(kb["dw"] == 0)))
                            if kb["dw"]:
                                fns.append(lambda zc0=zc0, dw=kb["dw"], nk=nk, lo=lo, zb=zb:
                                           nc.tensor.matmul(PS[zb][:nk, zc0 + lo:zc0 + lo + dw], lhsT=ident_b[:nk, :nk],
                                                            rhs=mbig_b[:nk, 0:dw], start=False, stop=True))
                        pe(fns, reads=rq + kb["rk"] + ["cst_b"], writes=[("PS", zb)])

                def emit_e(bi):
                    for st in streams:
                        kb, si, nk, lo, first, last, Wd, segs, rq = ctxs(bi, st)
                        et, etk = eb[si][bi % 2], ek[si][bi % 2]
                        op(ACT, lambda: nc.scalar.activation(out=et[:nk, lo:Wd], in_=PS[si][:nk, lo:Wd], func=AF.Exp,
                                                             scale=-1.0), reads=[("PS", si)], writes=[etk])

                def emit_sp(bi):
                    for st in streams:
                        kb, si, nk, lo, first, last, Wd, segs, rq = ctxs(bi, st)
                        et, etk = eb[si][bi % 2], ek[si][bi % 2]
                        op(ACT, lambda: nc.scalar.activation(out=sp_t[si][:nk, lo:Wd], in_=et[:nk, lo:Wd], func=AF.Ln,
                                                             bias=1.0), reads=[etk], writes=[("sp", si)])

                def emit_tri(bi):
                    for st in streams:
                        kb, si, nk, lo, first, last, Wd, segs, rq = ctxs(bi, st)
                        rb = 2 + si
                        pe([lambda: nc.tensor.matmul(PS[rb][:, lo:Wd], lhsT=tri_b[:nk, :], rhs=sp_t[si][:nk, lo:Wd],
                                                     start=first, stop=True, skip_group_check=True)],
                           reads=[("sp", si), "cst_b"], writes=[("PS", rb)])

                def emit_X(bi):
                    for st in streams:
                        kb, si, nk, lo, first, last, Wd, segs, rq = ctxs(bi, st)
                        rb = 2 + si
                        op(ACT, lambda: nc.scalar.activation(out=xt[si][:nk, lo:Wd], in_=PS[rb][:nk, lo:Wd], func=AF.Exp,
                                                             scale=-1.0), reads=[("PS", rb)], writes=[xk[si]])

                def emit_U(bi):
                    for st in streams:
                        kb, si, nk, lo, first, last, Wd, segs, rq = ctxs(bi, st)
                        rb = 2 + si
                        if not last:
                            pe([lambda: nc.tensor.matmul(PS[rb][:, lo:Wd], lhsT=u_b[:nk, :], rhs=sp_t[si][:nk, lo:Wd],
                                                         start=False, stop=True, skip_group_check=True)],
                               reads=[("sp", si), "cst_b"], writes=[("PS", rb)])

                def emit_a(bi):
                    for st in streams:
                        kb, si, nk, lo, first, last, Wd, segs, rq = ctxs(bi, st)
                        et, etk = eb[si][bi % 2], ek[si][bi % 2]
                        op(DVE, lambda: nc.vector.tensor_tensor(out=a_t[si][:nk, lo:Wd], in0=et[:nk, lo:Wd],
                                                                in1=xt[si][:nk, lo:Wd], op=ALU.mult),
                           reads=[etk, xk[si]], writes=[("a", si)])

                def emit_O(bi):
                    for st in streams:
                        kb, si, nk, lo, first, last, Wd, segs, rq = ctxs(bi, st)
                        fns = []
                        wr = []
                        for sgi, (h, qc0, zc0, w) in enumerate(segs):
                            ob, oc0 = st["o"][sgi]
                            vl = kb["v"](h // 2)
                            stt = ob not in ofirst
                            ofirst[ob] = True
                            if ("PS", ob) not in wr:
                                wr.append(("PS", ob))
                            fns.append(lambda vl=vl, zc0=zc0, w=w, ob=ob, oc0=oc0, stt=stt, nk=nk, lo=lo, si=si:
                                       nc.tensor.matmul(PS[ob][:, oc0 + lo:oc0 + w], lhsT=vl, rhs=a_t[si][:nk, zc0 + lo:zc0 + w],
                                                        start=stt, stop=True, skip_group_check=True))
                        pe(fns, reads=[("a", si)] + kb["vk"], writes=wr)

                if mode in ("all", "prologue"):
                    emit_Z(0)
                    emit_e(0)
                    emit_sp(0)
                if mode == "prologue":
                    return
                for bi in range(nkb):
                    emit_tri(bi)
                    emit_X(bi)
                    if bi + 1 < nkb:
                        emit_Z(bi + 1)
                        emit_e(bi + 1)
                    emit_a(bi)
                    emit_U(bi)
                    if bi + 1 < nkb:
                        emit_sp(bi + 1)
                    elif next_prologue is not None:
                        next_prologue()
                    emit_O(bi)

            def attn1(st, kbl, hook=None):
                nkb = len(kbl)
                Wd = st["W"]
                segs = st["segs"]
                rq = st["qk"]
                assert Wd <= 256
                E4 = [(e_t[0], ("e", 0)), (rs_t, "rs_t"), (e_t[1], ("e", 1)), (mn_t, "mn_t")]
                XA = [(tmpA, "tmpA"), (tmpB, "tmpB")]
                ofirst = {}

                def spv(bi):
                    j = bi % 4
                    return sp_t[j // 2][:, (j % 2) * 256:(j % 2) * 256 + Wd], ("sp1", j)

                def front(bi):
                    kb = kbl[bi]
                    nk = kb["nk"]
                    zb = bi % 2
                    fns = []
                    for (h, qc0, zc0, w) in segs:
                        lhs = kb["kT"](h // 2)
                        fns.append(lambda lhs=lhs, h=h, qc0=qc0, zc0=zc0, w=w:
                                   nc.tensor.matmul(PS[zb][:nk, zc0:zc0 + w], lhsT=lhs, rhs=qz[:, h, qc0:qc0 + w],
                                                    start=True, stop=(kb["dw"] == 0)))
                        if kb["dw"]:
                            fns.append(lambda zc0=zc0, dw=kb["dw"]:
                                       nc.tensor.matmul(PS[zb][:nk, zc0:zc0 + dw], lhsT=ident_b[:nk, :nk],
                                                        rhs=mbig_b[:nk, 0:dw], start=False, stop=True))
                    pe(fns, reads=rq + kb["rk"] + ["cst_b"], writes=[("PS", zb)])
                    et, etk = E4[bi % 4]
                    spa, spk = spv(bi)
                    op(ACT, lambda: nc.scalar.activation(out=et[:nk, 0:Wd], in_=PS[zb][:nk, 0:Wd], func=AF.Exp, scale=-1.0),
                       reads=[("PS", zb)], writes=[etk])
                    op(ACT, lambda: nc.scalar.activation(out=spa[:nk, :], in_=et[:nk, 0:Wd], func=AF.Ln, bias=1.0),
                       reads=[etk], writes=[spk, ("sp", 0), ("sp", 1)])

                def emit_O(bi):
                    kb = kbl[bi]
                    nk = kb["nk"]
                    at = a_t[bi % 2]
                    fns, wr = [], []
                    for sgi, (h, qc0, zc0, w) in enumerate(segs):
                        ob, oc0 = st["o"][sgi]
                        vl = kb["v"](h // 2)
                        stt = ob not in ofirst
                        ofirst[ob] = True
                        if ("PS", ob) not in wr:
                            wr.append(("PS", ob))
                        fns.append(lambda vl=vl, zc0=zc0, w=w, ob=ob, oc0=oc0, stt=stt:
                                   nc.tensor.matmul(PS[ob][:, oc0:oc0 + w], lhsT=vl, rhs=at[:nk, zc0:zc0 + w],
                                                    start=stt, stop=True, skip_group_check=True))
                    pe(fns, reads=[("a", bi % 2)] + kb["vk"], writes=wr)

                front(0)
                if nkb > 1:
                    front(1)
                for bi in range(nkb):
                    kb = kbl[bi]
                    nk = kb["nk"]
                    first, last = bi == 0, bi == nkb - 1
                    spa, spk = spv(bi)
                    et, etk = E4[bi % 4]
                    xa, xak = XA[bi % 2]
                    pe([lambda: nc.tensor.matmul(PS[2][:, 0:Wd], lhsT=tri_b[:nk, :], rhs=spa[:nk, :], start=first, stop=True,
                                                 skip_group_check=True)], reads=[spk, "cst_b"], writes=[("PS", 2)])
                    if bi > 0:
                        emit_O(bi - 1)
                    op(ACT, lambda: nc.scalar.activation(out=xa[:nk, 0:Wd], in_=PS[2][:nk, 0:Wd], func=AF.Exp, scale=-1.0),
                       reads=[("PS", 2)], writes=[xak])
                    if bi + 2 < nkb:
                        front(bi + 2)
                    op(DVE, lambda: nc.vector.tensor_tensor(out=a_t[bi % 2][:nk, 0:Wd], in0=et[:nk, 0:Wd], in1=xa[:nk, 0:Wd],
                                                            op=ALU.mult), reads=[etk, xak], writes=[("a", bi % 2)])
                    if not last:
                        pe([lambda: nc.tensor.matmul(PS[2][:, 0:Wd], lhsT=u_b[:nk, :], rhs=spa[:nk, :], start=False, stop=True,
                                                     skip_group_check=True)], reads=[spk, "cst_b"], writes=[("PS", 2)])
                    if hook is not None:
                        hook(bi)
                emit_O(nkb - 1)

            def prompt_attn(g):
                c0, W = gcols(p, g)
                kb_hi = tiles[g[-1]]["kb"]
                kb_lo = tiles[g[0]]["kb"]
                kbl = []
                for kbx in range(kb_hi, -1, -1):
                    nk = c.NMETA if kbx == 0 else 128
                    kpos = 0 if kbx == 0 else c.NMETA + 128 * (kbx - 1)
                    if kbx >= kb_lo:
                        lo, dw = (kbx - kb_lo) * 128, nk
                    else:
                        lo, dw = 0, 0
                    kbl.append(dict(nk=nk, lo=lo, dw=dw,
                                    kT=(lambda hp, kpos=kpos, nk=nk: kTn[:, hp, kpos:kpos + nk]),
                                    v=(lambda hp, kbx=kbx, nk=nk: vh[:nk, kbx, hp * 128:(hp + 1) * 128]),
                                    rk=[("kT", kbx)], vk=[("vh", kbx)]))
                for hp in range(HP):
                    streams = [dict(si=si, segs=[(2 * hp + si, c0, 0, W)], W=W, qk=[("qz", 2 * hp + si, g[0])],
                                    o=[(6 + si, 0)]) for si in range(2)]

                    def evac(hp=hp):
                        op(ACT, lambda: nc.scalar.copy(out=XT[0:64, hp, c0:c0 + W], in_=PS[6][0:64, 0:W]),
                           reads=[("PS", 6)], writes=[("XT", i) for i in g])
                        op(DVE, lambda: nc.vector.tensor_copy(out=XT[64:128, hp, c0:c0 + W], in_=PS[7][64:128, 0:W]),
                           reads=[("PS", 7)], writes=[("XT", i) for i in g])
                    attn_jobs.append((streams, kbl, evac))

            def sample_attn(g):
                ti = g[0]
                cs0 = tiles[ti]["c0"]
                NCB = c.PAST // 128
                DS = c.DSEQ
                VG = 4 if NCB % 4 == 0 else 1
                allk = [("kT", x) for x in range(c.NKB + 1)]

                def prep_new(s):
                    dma(SP, vh[0:DS, NCB, :], vnew[s * DS:(s + 1) * DS, :], ("vnl",), reads=["vnew"],
                        writes=[("vh", NCB)])

                def prep_block(s, kb):
                    if kb % VG == 0:
                        dma(POOL, vh[:, kb:kb + VG, :], cv[s, kb * 128:(kb + VG) * 128, :].rearrange("(k p) c -> p k c", p=128),
                            ("cvl", (kb // VG) % 4), writes=[("vh", x) for x in range(kb, kb + VG)])
                    ks_ = kst_rr[0] % len(kstage)
                    kst_rr[0] += 1
                    kbuf, kkey = kstage[ks_]
                    dma(SP, kbuf[:, 0:AW], ck[s, kb * 128:(kb + 1) * 128, :], ("kstl", ks_), writes=kkey)
                    b = psum_next(4, 6)
                    pe([lambda hp=hp: nc.tensor.transpose(out=PS[b][:, hp * 128:(hp + 1) * 128],
                                                          in_=kbuf[:, hp * 128:(hp + 1) * 128], identity=ident_f[:, :])
                        for hp in range(HP)], reads=kkey + ["cst_f"], writes=[("PS", b)])
                    op(ACT, lambda: nc.scalar.mul(out=kTn[:, 0:HP, kb * 128:(kb + 1) * 128],
                                                  in_=PS[b][:, 0:HP * 128].rearrange("p (k t) -> p k t", k=HP), mul=-0.125),
                       reads=[("PS", b)], writes=[("kTs", kb)] + (allk if s == 0 else []))

                prep_new(0)
                for kb in range(NCB - 1, -1, -1):
                    prep_block(0, kb)
                for s in range(c.NSS):
                    kbl = [dict(nk=DS, lo=0, dw=DS, kT=(lambda hp, s=s: kTnew[:, hp, s * DS:(s + 1) * DS]),
                                v=(lambda hp: vh[:DS, NCB, hp * 128:(hp + 1) * 128]), rk=[("kTnew",)], vk=[("vh", NCB)])]
                    for kb in range(NCB - 1, -1, -1):
                        kbl.append(dict(nk=128, lo=0, dw=0, kT=(lambda hp, kb=kb: kTn[:, hp, kb * 128:(kb + 1) * 128]),
                                        v=(lambda hp, kb=kb: vh[:, kb, hp * 128:(hp + 1) * 128]),
                                        rk=[("kTs", kb)], vk=[("vh", kb)]))
                    st = dict(si=0, segs=[(h, cs0 + s * DS, h * DS, DS) for h in range(H)], W=H * DS,
                              qk=[("qz", h, ti) for h in range(H)], o=[(6 + (h % 2), (h // 2) * DS) for h in range(H)])
                    todo = list(range(NCB - 1, -1, -1)) if s + 1 < c.NSS else []
                    state = {"new": s + 1 < c.NSS}

                    def hook(bi, s=s, todo=todo, state=state):
                        if state["new"] and bi >= 2:
                            prep_new(s + 1)
                            state["new"] = False
                        while todo:
                            kb = todo[0]
                            need = (NCB - (kb - (kb % VG))) + 1 if VG > 1 else (NCB - kb) + 1
                            if bi < need:
                                break
                            prep_block(s + 1, todo.pop(0))
                    attn1(st, kbl, hook)
                    for kb in todo[:]:
                        prep_block(s + 1, todo.pop(0))
                    if state["new"]:
                        prep_new(s + 1)
                    op(ACT, lambda: nc.scalar.copy(out=XT[0:64, 0:HP, cs0 + s * DS:cs0 + (s + 1) * DS],
                                                   in_=PS[6][0:64, 0:HP * DS].rearrange("p (k t) -> p k t", k=HP)),
                       reads=[("PS", 6)], writes=[("XT", ti)])
                    op(DVE, lambda: nc.vector.tensor_copy(out=XT[64:128, 0:HP, cs0 + s * DS:cs0 + (s + 1) * DS],
                                                          in_=PS[7][64:128, 0:HP * DS].rearrange("p (k t) -> p k t", k=HP)),
                       reads=[("PS", 7)], writes=[("XT", ti)])

            lgroups = [g for g in groups if any(tiles[i]["y"] is not None for i in g)]
            attn_jobs = []
            for g in lgroups:
                if not gS(g):
                    prompt_attn(g)
            for n_, (st_, kb_, ev_) in enumerate(attn_jobs):
                if n_ == 0:
                    attn(st_, kb_, mode="prologue")
                nxt_ = attn_jobs[n_ + 1] if n_ + 1 < len(attn_jobs) else None
                attn(st_, kb_, mode="body",
                     next_prologue=(lambda nxt_=nxt_: attn(nxt_[0], nxt_[1], mode="prologue")) if nxt_ else None)
                ev_()
            for g in lgroups:
                if gS(g):
                    sample_attn(g)

            if c.stop == "attn":
                return
            def conv_group(g):
                c0, W = gcols(p, g)
                if gS(g):
                    csegs = [(UBS + s * SW, c.DSEQ, s * c.DSEQ) for s in range(c.NSS)]
                else:
                    csegs = [(c0, W, 0)]
                s1b, s2b = psum_next(), psum_next()
                while s2b == s1b:
                    s2b = psum_next()
                ubk = [("ub", ct_, g_[0]) for ct_ in range(CT) for g_ in groups] + [("ubh",), ("ubhs",)]
                pend_stats = [None]
                for ct in range(CT):
                    db = dg_rr[0] % 2
                    dg_rr[0] += 1
                    op(DVE, lambda: nc.vector.tensor_tensor(
                        out=diag[db][:, :, :], in0=ident_b.unsqueeze(1).to_broadcast([128, c.CK, 128]),
                        in1=wdw_c[:, ct, :].unsqueeze(2).to_broadcast([128, c.CK, 128]), op=ALU.mult),
                       reads=["cst_b", "wdw_c"], writes=[("diag", db, 0)])
                    b = psum_next()
                    while b in (s1b, s2b):
                        b = psum_next()
                    fns = []
                    for (u0, w, o0) in csegs:
                        for k in range(c.CK):
                            fns.append(lambda u0=u0, w=w, o0=o0, k=k:
                                       nc.tensor.matmul(PS[b][:, o0:o0 + w], lhsT=diag[db][:, k, :], rhs=ub[:, ct, u0 + k:u0 + k + w],
                                                        start=(k == 0), stop=(k == c.CK - 1)))
                    pe(fns, reads=[("diag", db, 0)] + ubk, writes=[("PS", b)])
                    op(ACT, lambda: nc.scalar.activation(out=cf[:, ct, 0:W], in_=PS[b][:, 0:W], func=AF.Identity,
                                                         bias=cvp_c[:, 0, ct:ct + 1], scale=1.0),
                       reads=[("PS", b), "cvp_c"], writes=[("cf", ct), "cfh"])
                    sqb = [(e_t[0], ("e", 0)), (e_t[1], ("e", 1))][ct % 2]
                    op(ACT, lambda: nc.scalar.activation(out=sqb[0][:, 0:W], in_=cf[:, ct, 0:W], func=AF.Square),
                       reads=[("cf", ct), "cfh"], writes=[sqb[1]])
                    def stats(ct=ct, sqb=sqb):
                        pe([lambda: nc.tensor.matmul(PS[s1b][:, 0:W], lhsT=ones_f[:, :], rhs=cf[:, ct, 0:W], start=(ct == 0),
                                                     stop=(ct == CT - 1), skip_group_check=True)],
                           reads=[("cf", ct), "cfh", "cst_f"], writes=[("PS", s1b)])
                        pe([lambda: nc.tensor.matmul(PS[s2b][:, 0:W], lhsT=ones_f[:, :], rhs=sqb[0][:, 0:W], start=(ct == 0),
                                                     stop=(ct == CT - 1), skip_group_check=True)],
                           reads=[sqb[1], "cst_f"], writes=[("PS", s2b)])
                    if pend_stats[0] is not None:
                        pend_stats[0]()
                    pend_stats[0] = stats
                pend_stats[0]()
                pend_stats[0] = None
                if has_p and g is [g_ for g_ in groups if not gS(g_)][-1] and not p["last"]:
                    op(POOL, lambda: nc.gpsimd.tensor_copy(out=tmpH[:, :, :], in_=ub[:, :, T:T + HWc]),
                       reads=ubk, writes=["tmpH"])
                    op(POOL, lambda: nc.gpsimd.tensor_copy(out=ub[:, :, 0:HWc], in_=tmpH[:, :, :]),
                       reads=["tmpH"], writes=[("ubh",)])
                inv = 1.0 / CC
                op(DVE, lambda: nc.vector.tensor_scalar(out=mn_t[:, 0:W], in0=PS[s1b][:, 0:W], scalar1=inv, scalar2=None,
                                                        op0=ALU.mult), reads=[("PS", s1b)], writes=["mn_t"])
                op(DVE, lambda: nc.vector.tensor_tensor(out=tmpB[:, 0:W], in0=mn_t[:, 0:W], in1=mn_t[:, 0:W], op=ALU.mult),
                   reads=["mn_t"], writes=["tmpB"])
                op(DVE, lambda: nc.vector.scalar_tensor_tensor(out=rs_t[:, 0:W], in0=PS[s2b][:, 0:W], scalar=inv,
                                                               in1=tmpB[:, 0:W], op0=ALU.mult, op1=ALU.subtract),
                   reads=[("PS", s2b), "tmpB"], writes=["rs_t"])
                op(ACT, lambda: nc.scalar.activation(out=rs_t[:, 0:W], in_=rs_t[:, 0:W], func=AF.Ln, bias=eps_c[:, :], scale=1.0),
                   reads=["rs_t", "eps_c"], writes=["rs_t"])
                op(ACT, lambda: nc.scalar.activation(out=rs_t[:, 0:W], in_=rs_t[:, 0:W], func=AF.Exp, scale=-0.5),
                   reads=["rs_t"], writes=["rs_t"])
                for ct in range(CT):
                    op(DVE, lambda: nc.vector.tensor_tensor(out=tmpA[:, 0:W], in0=cf[:, ct, 0:W], in1=mn_t[:, 0:W],
                                                            op=ALU.subtract), reads=[("cf", ct), "cfh", "mn_t"], writes=["tmpA"])
                    op(DVE, lambda: nc.vector.tensor_tensor(out=tmpA[:, 0:W], in0=tmpA[:, 0:W], in1=rs_t[:, 0:W],
                                                            op=ALU.mult), reads=["tmpA", "rs_t"], writes=["tmpA"])
                    op(ACT, lambda: nc.scalar.activation(out=XT[:, HP + ct, c0:c0 + W], in_=tmpA[:, 0:W], func=AF.Silu,
                                                         bias=cvp_c[:, 2, ct:ct + 1], scale=cvp_c[:, 1, ct:ct + 1]),
                       reads=["tmpA", "cvp_c"], writes=[("XT", i) for i in g])

            load_ln(1)
            osl = [wget() for j in range(c.NOC)]

            def outproj_group(g):
                for i in g:
                    t = tiles[i]
                    n = t["n"]
                    for j in range(c.NOC):
                        slot, wk = osl[j]
                        Wt = wview(slot, KC, c.OCW)
                        b = psum_next()
                        pe([lambda kc=kc: nc.tensor.matmul(PS[b][:n, 0:c.OCW], lhsT=XT[:, kc, t["c0"]:t["c0"] + n],
                                                           rhs=Wt[:, kc, :], start=(kc == 0), stop=(kc == KC - 1))
                            for kc in range(KC)], reads=[wk, ("XT", i)], writes=[("PS", b)])
                        hsl = Hs[:n, i, j * c.OCW:(j + 1) * c.OCW]
                        op(DVE, lambda: nc.vector.scalar_tensor_tensor(out=hsl, in0=hsl, scalar=c.ALPHA,
                                                                       in1=PS[b][:n, 0:c.OCW], op0=ALU.mult, op1=ALU.add),
                           reads=[("PS", b), ("H", i)], writes=[("H", i)])
                ln_multi(p, tiles, g)
                for i in g:
                    transpose_tile(p, i, tiles[i]["n"], tiles[i]["c0"])

            conv_group(lgroups[0])
            for gi, g in enumerate(lgroups):
                if gi + 1 < len(lgroups):
                    conv_group(lgroups[gi + 1])
                outproj_group(g)

            if c.stop == "out":
                return
            load_ln(2)
            for j in range(c.NFC):
                s1, wk1 = wget()
                s2, wk2 = wget()
                W1 = wview(s1, KC, 512)
                W2 = wview(s2, 4, D)

                def ffn1(g, W1=W1, wk1=wk1):
                    c0, W = gcols(p, g)
                    hb = hid_rr[0] % 2
                    hid_rr[0] += 1
                    for fc in range(4):
                        b = psum_next()
                        pe([lambda kc=kc: nc.tensor.matmul(PS[b][:, 0:W], lhsT=W1[:, kc, fc * 128:(fc + 1) * 128],
                                                           rhs=XT[:, kc, c0:c0 + W], start=(kc == 0), stop=(kc == KC - 1))
                            for kc in range(KC)], reads=[wk1] + xtk(g), writes=[("PS", b)])
                        op(ACT, lambda: nc.scalar.activation(out=e_t[fc % 2][:, 0:W], in_=PS[b][:, 0:W], func=AF.Relu),
                           reads=[("PS", b)], writes=[("e", fc % 2)])
                        op(DVE, lambda: nc.vector.tensor_tensor(out=hidT[hb][:, fc, 0:W], in0=e_t[fc % 2][:, 0:W],
                                                                in1=e_t[fc % 2][:, 0:W], op=ALU.mult),
                           reads=[("e", fc % 2)], writes=[("hid", hb), "cfh"])
                    return hb

                def ffn2(g, hb, W2=W2, wk2=wk2, j=j):
                    c0, W = gcols(p, g)
                    for i in g:
                        t = tiles[i]
                        n = t["n"]
                        off = t["c0"] - c0
                        for oc in range(c.NOC):
                            b = psum_next()
                            pe([lambda fc=fc: nc.tensor.matmul(PS[b][:n, 0:c.OCW], lhsT=hidT[hb][:, fc, off:off + n],
                                                               rhs=W2[:, fc, oc * c.OCW:(oc + 1) * c.OCW], start=(fc == 0),
                                                               stop=(fc == 3)) for fc in range(4)],
                               reads=[wk2, ("hid", hb), "cfh"], writes=[("PS", b)])
                            hsl = Hs[:n, i, oc * c.OCW:(oc + 1) * c.OCW]
                            if j == 0:
                                op(DVE, lambda: nc.vector.scalar_tensor_tensor(out=hsl, in0=hsl, scalar=c.ALPHA,
                                                                               in1=PS[b][:n, 0:c.OCW], op0=ALU.mult, op1=ALU.add),
                                   reads=[("PS", b), ("H", i)], writes=[("H", i)])
                            else:
                                op(DVE, lambda: nc.vector.tensor_tensor(out=hsl, in0=hsl, in1=PS[b][:n, 0:c.OCW], op=ALU.add),
                                   reads=[("PS", b), ("H", i)], writes=[("H", i)])
                    if j == c.NFC - 1:
                        ln_multi(p, tiles, g)
                        for i in g:
                            t = tiles[i]
                            n = t["n"]
                            if t["y"] is not None:
                                dma(SP, t["y"], Hs[:n, i, :], ("Hst", i), reads=[("H", i)])
                            nxt = p.get("next")
                            if nxt is not None and i < len(nxt["tiles"]):
                                tn = nxt["tiles"][i]
                                dma(SP, Hs[:tn["n"], i, :], tn["src"], ("Hld", i), writes=[("H", i)])
                                tn["preloaded"] = True

                hbs = {}
                hbs[0] = ffn1(lgroups[0])
                for gi, g in enumerate(lgroups):
                    if gi + 1 < len(lgroups):
                        hbs[gi + 1] = ffn1(lgroups[gi + 1])
                    ffn2(g, hbs[gi])
            nxt = p.get("next")
            for g in groups:
                if g not in lgroups:
                    for i in g:
                        if nxt is not None and i < len(nxt["tiles"]) and not nxt["tiles"][i].get("preloaded"):
                            tn = nxt["tiles"][i]
                            dma(SP, Hs[:tn["n"], i, :], tn["src"], ("Hld", i), writes=[("H", i)])
                            tn["preloaded"] = True

        for pi_, p in enumerate(passes):
            p["next"] = passes[pi_ + 1] if pi_ + 1 < len(passes) else None
        for p in passes:
            run_pass(p)
        assert c.stop or wstate["cur"] == len(chunks)
        for k, v in ctx.dcnt.items():
            if v:
                SP.wait(k, v)
        SP.e.nop() if hasattr(SP.e, "nop") else None
    return nc


def make_consts():
    cst = np.zeros((128, NCONST * 128), np.float32)
    j = np.arange(128)[:, None]
    s = np.arange(128)[None, :]
    cst[:, 0:128] = (j == s)
    cst[:, 128:256] = 1.0
    cst[:, 256:384] = (j >= s)
    cst[:, 384:512] = BIG * (j >= s)
    cst[0, 512:640] = 1.0
    cst[:, 640:768] = (j < s)
    return cst


def core_inputs(c, core, inp):
    f = lambda a: np.ascontiguousarray(a, dtype=np.float32)
    ps = slice(core * c.NPS, (core + 1) * c.NPS)
    ss = slice(core * c.NSS, (core + 1) * c.NSS)
    return {
        "xp": f(inp["x_prompt"][ps]),
        "xs": f(inp["x_sample"][ss].reshape(c.NSS * c.DSEQ, c.D)),
        "ck": f(inp["cache_k"][0, ss].reshape(c.NSS, c.PAST, c.AW)),
        "cv": f(inp["cache_v"][0, ss].reshape(c.NSS, c.PAST, c.AW)),
        "sc": f(inp["state_conv"][0, ss]),
        "meta": f(inp["meta"]),
        "lnp": f(np.stack([inp["g_in"], inp["b_in"], inp["g_ln1"][0], inp["b_ln1"][0], inp["g_ln2"][0], inp["b_ln2"][0]])),
        "w_in": f(inp["w_in"][0]),
        "w_dw": f(inp["w_dw"][0]),
        "cvp": f(np.stack([inp["b_dw"][0], inp["g_conv"][0], inp["b_conv"][0]])),
        "w_out": f(inp["w_out"][0]),
        "w_ff1": f(inp["w_ff1"][0]),
        "w_ff2": f(inp["w_ff2"][0]),
        "consts": make_consts(),
    }


def assemble(c, res, ncores):
    cat = lambda k: np.concatenate([np.asarray(r[k]) for r in res], axis=0)
    B = ncores * c.NPS
    DB = ncores * c.NSS
    yp = cat("yp").reshape(B, c.SEQ, c.D)
    ys = cat("ys").reshape(DB, c.DSEQ, c.D)
    kp = cat("kp").reshape(1, B, c.LP, c.H, 64)
    vp = cat("vp").reshape(1, B, c.LP, c.H, 64)
    cp = cat("cp").reshape(1, B, c.HW, c.CC)
    ksm = cat("ksm").reshape(1, DB, c.DSEQ, c.H, 64)
    vsm = cat("vsm").reshape(1, DB, c.DSEQ, c.H, 64)
    cs = cat("cs").reshape(1, DB, c.HW, c.CC)
    return tuple(np.ascontiguousarray(a, dtype=np.float32) for a in (yp, ys, kp, vp, cp, ksm, vsm, cs))


_NC_CACHE = {}


def kernel(**inputs):
    ncores = 8
    c = Cfg()
    if "nc" not in _NC_CACHE:
        _NC_CACHE["nc"] = build(c)
    nc = _NC_CACHE["nc"]
    in_maps = [core_inputs(c, i, inputs) for i in range(ncores)]
    res = run_bass_kernel_spmd(nc, in_maps, core_ids=list(range(ncores)))
    return assemble(c, res.results, ncores)
```

```python
import numpy as np
from contextlib import ExitStack
import concourse.bass as bass
import concourse.mybir as mybir
from concourse.bass_utils import run_bass_kernel_spmd

F32 = mybir.dt.float32
BF16 = mybir.dt.bfloat16
ALU = mybir.AluOpType
AF = mybir.ActivationFunctionType

LN_EPS = 1e-5
BIG = 30000.0
NCONST = 6


class Cfg:
    def __init__(s, D=1024, H=8, DFF=4096, SEQ=2048, NPS=2, NSS=4, DSEQ=32, PAST=2048,
                 NMETA=16, CK=31, NHALF=2, depth=1, stop=None, merge=True):
        s.stop = stop
        s.D, s.H, s.DFF, s.SEQ, s.NPS, s.NSS, s.DSEQ, s.PAST = D, H, DFF, SEQ, NPS, NSS, DSEQ, PAST
        s.NMETA, s.CK, s.NHALF = NMETA, CK, NHALF
        s.AW = H * 64
        s.CC = D - s.AW
        s.KC = D // 128
        s.HP = H // 2
        s.CT = s.CC // 128
        s.NFC = DFF // 512
        s.NF = SEQ // 128
        s.LP = NMETA + SEQ
        s.INC = 3 * s.AW + 2 * s.CC
        s.ALPHA = (2.0 * depth) ** 0.25
        s.OCW = min(512, D)
        s.NOC = D // s.OCW
        assert s.AW <= 512 and s.CC <= 512 and s.KC * 512 <= 4096
        assert s.NF % NHALF == 0
        s.TPH = s.NF // NHALF
        s.MERGE = merge
        s.TPMAX = NMETA + s.TPH * 128
        s.TMAX = max(s.TPMAX, s.TPH * 128 + NSS * DSEQ if merge else NSS * DSEQ)
        s.NTMAX = s.TPH + 1
        s.LMAX = max(s.LP, PAST + DSEQ)
        s.NKB = max(1 + s.NF, PAST // 128 + 1)
        s.HW = CK - 1


class Ctx:
    def __init__(s, nc, stack):
        s.nc, s.stack = nc, stack
        s.sems = {}
        s.dcnt = {}

    def sem(s, key):
        if key not in s.sems:
            s.sems[key] = s.stack.enter_context(s.nc.semaphore("s%d" % len(s.sems)))
            s.dcnt[key] = 0
        return s.sems[key]


class Eng:
    def __init__(s, ctx, name, e):
        s.ctx, s.name, s.e = ctx, name, e
        s.key = ("eng", name)
        ctx.sem(s.key)
        s.cnt = 0
        s.seen = {}

    def wait(s, k, v):
        if s.seen.get(k, 0) >= v:
            return
        s.e.wait_ge(s.ctx.sems[k], v)
        s.seen[k] = v

    def sig(s, inst):
        inst.then_inc(s.ctx.sems[s.key], 1)
        s.cnt += 1
        return (s.key, s.cnt)


class Dep:
    def __init__(s):
        s.w = {}
        s.r = {}

    def pre(s, eng, reads, writes):
        if eng.name == "pe":
            return s.pre_pe(eng, reads, writes)
        for k in reads:
            for sk, v in s.w.get(k, {}).items():
                eng.wait(sk, v)
            if isinstance(k, tuple) and k[0] == "PS":
                for sk, v in s.r.get(k, {}).items():
                    if sk != eng.key:
                        eng.wait(sk, v)
        for k in writes:
            for sk, v in s.w.get(k, {}).items():
                eng.wait(sk, v)
            for sk, v in s.r.get(k, {}).items():
                eng.wait(sk, v)

    def pre_pe(s, eng, reads, writes):
        for k in reads:
            for sk, v in s.w.get(k, {}).items():
                if sk != eng.key:
                    eng.wait(sk, v)
        for k in writes:
            for d in (s.w.get(k, {}), s.r.get(k, {})):
                for sk, v in d.items():
                    if sk != eng.key:
                        eng.wait(sk, v)

    def post(s, ev, reads, writes):
        for k in reads:
            d = s.r.setdefault(k, {})
            d[ev[0]] = max(d.get(ev[0], 0), ev[1])
        for k in writes:
            d = s.w.setdefault(k, {})
            d[ev[0]] = max(d.get(ev[0], 0), ev[1])


def build(cfg):
    c = cfg
    nc = bass.Bass("TRN2", target_bir_lowering=False)
    D, KC, HP, CT, AW, CC, H = c.D, c.KC, c.HP, c.CT, c.AW, c.CC, c.H
    HWc = c.HW

    def din(name, shape):
        return nc.dram_tensor(name, list(shape), F32, kind="ExternalInput").ap()

    def dout(name, shape):
        return nc.dram_tensor(name, list(shape), F32, kind="ExternalOutput").ap()

    xp = din("xp", [c.NPS, c.SEQ, D])
    xs = din("xs", [c.NSS * c.DSEQ, D])
    ck = din("ck", [c.NSS, c.PAST, AW])
    cv = din("cv", [c.NSS, c.PAST, AW])
    sc = din("sc", [c.NSS, HWc, CC])
    meta = din("meta", [c.NMETA, D])
    lnp = din("lnp", [6, D])
    w_in = din("w_in", [D, c.INC])
    w_dw = din("w_dw", [c.CK, CC])
    cvp = din("cvp", [3, CC])
    w_out = din("w_out", [D, D])
    w_ff1 = din("w_ff1", [D, c.DFF])
    w_ff2 = din("w_ff2", [c.DFF, D])
    consts = din("consts", [128, NCONST * 128])

    yp = dout("yp", [c.NPS, c.SEQ, D])
    ys = dout("ys", [c.NSS * c.DSEQ, D])
    kp = dout("kp", [c.NPS, c.LP, AW])
    vp = dout("vp", [c.NPS, c.LP, AW])
    cp = dout("cp", [c.NPS, HWc, CC])
    ksm = dout("ksm", [c.NSS * c.DSEQ, AW])
    vsm = dout("vsm", [c.NSS * c.DSEQ, AW])
    cs = dout("cs", [c.NSS, HWc, CC])

    stack = ExitStack()
    with stack:
        ctx = Ctx(nc, stack)
        dep = Dep()
        PE = Eng(ctx, "pe", nc.tensor)
        ACT = Eng(ctx, "act", nc.scalar)
        DVE = Eng(ctx, "dve", nc.vector)
        POOL = Eng(ctx, "pool", nc.gpsimd)
        SP = Eng(ctx, "sp", nc.sync)

        def sb(name, shape, dt):
            return stack.enter_context(nc.sbuf_tensor(name, list(shape), dt))

        def op(eng, fn, reads=(), writes=()):
            dep.pre(eng, reads, writes)
            inst = fn()
            ev = eng.sig(inst)
            dep.post(ev, reads, writes)
            return inst

        def pe(fns, reads=(), writes=()):
            dep.pre(PE, reads, writes)
            inst = None
            for f in fns:
                inst = f()
            ev = PE.sig(inst)
            dep.post(ev, reads, writes)

        def dma(q, out, in_, semkey, reads=(), writes=()):
            dmas(q, [(out, in_)], semkey, reads, writes)

        def dmas(q, pairs, semkey, reads=(), writes=()):
            dep.pre(q, reads, writes)
            sem = ctx.sem(("dma", semkey))
            for out, in_ in pairs:
                q.e.dma_start(out=out, in_=in_).then_inc(sem, 16)
                ctx.dcnt[("dma", semkey)] += 16
            ev = (("dma", semkey), ctx.dcnt[("dma", semkey)])
            dep.post(ev, reads, writes)

        cst_f = sb("cst_f", [128, 2 * 128], F32)
        cst_b = sb("cst_b", [128, NCONST * 128], BF16)
        ident_f = cst_f[:, 0:128]
        ones_f = cst_f[:, 128:256]
        ident_b = cst_b[:, 0:128]
        ones_b = cst_b[:, 128:256]
        tri_b = cst_b[:, 256:384]
        mbig_b = cst_b[:, 384:512]
        e0_b = cst_b[:, 512:640]
        u_b = cst_b[:, 640:768]

        gtile = sb("gtile", [128, D], F32)
        btile = sb("btile", [128, D], F32)
        wdw_c = sb("wdw_c", [128, CT, c.CK], F32)
        cvp_c = sb("cvp_c", [128, 3, CT], F32)
        diag = [sb("diag%d" % i, [128, c.CK, 128], BF16) for i in range(2)]
        dg_rr = [0]
        hid_rr = [0]
        tmpH = sb("tmpH", [128, CT, HWc], BF16)

        Hs = sb("Hs", [128, c.NTMAX, D], F32)
        XT = sb("XT", [128, KC, c.TMAX], BF16)
        qz = sb("qz", [128, H, c.TMAX], BF16)
        kTn = sb("kTn", [128, HP, c.LMAX], BF16)
        vh = sb("vh", [128, c.NKB, AW], BF16)
        UBS = HWc + c.TPMAX
        UW = UBS + c.NSS * (HWc + c.DSEQ)
        ub = sb("ub", [128, CT, UW], BF16)
        NRING = 4
        ring = [sb("ring%d" % i, [128, 4096], BF16) for i in range(NRING)]
        cfh = sb("cfh", [128, max(CT * 1024, 4096)], BF16)
        hidT = [cfh[:, i * 2048:(i + 1) * 2048].rearrange("p (k t) -> p k t", k=4) for i in range(2)]
        cf = cfh[:, 0:CT * 1024].bitcast(F32).rearrange("p (k t) -> p k t", k=CT)
        e_t = [sb("e_t%d" % i, [128, 512], F32) for i in range(2)]
        sp_t = [sb("sp_t%d" % i, [128, 512], BF16) for i in range(2)]
        a_t = [sb("a_t%d" % i, [128, 512], BF16) for i in range(2)]
        NSTG = 3
        stg = [sb("stg%d" % i, [128, 512], F32) for i in range(NSTG)]
        tmpA = sb("tmpA", [128, 512], F32)
        tmpB = sb("tmpB", [128, 512], F32)
        rs_t = sb("rs_t", [128, 512], F32)
        mn_t = sb("mn_t", [128, 512], F32)
        st6 = sb("st6", [128, 2, 6], F32)
        mv = sb("mv", [128, 2], F32)
        rstd = sb("rstd", [128, 1], F32)
        nbias = sb("nbias", [128, 1], F32)
        kTnew = sb("kTnew", [128, HP, 128], BF16)
        vnew = sb("vnew", [128, AW], BF16)

        cf32 = cfh[:, 0:CT * 1024].bitcast(F32)
        kstage = [(stg[i], [("stg", i)]) for i in range(NSTG)]
        kstage += [(cf32[:, i * 512:(i + 1) * 512], [("cf", i), "cfh"]) for i in range(CT)]
        kst_rr = [0]
        PS = [stack.enter_context(nc.psum_tensor("ps%d" % i, [128, 512], F32)) for i in range(8)]
        ps_rr = [0]

        def psum_next(lo=0, hi=8):
            i = lo + ps_rr[0] % (hi - lo)
            ps_rr[0] += 1
            return i

        dma(SP, cst_f[:, :], consts[:, 0:256], "cst", writes=["cst_f"])
        dma(POOL, cst_b[:, :], consts[:, :], "cstb", writes=["cst_b"])
        prw, prc = stg[0], stg[1]
        dma(SP, prw[0:c.CK, 0:CC], w_dw[:, :], "prm_w", writes=[("stg", 0)])
        dma(SP, prc[0:3, 0:CC], cvp[:, :], "prm_c", writes=[("stg", 1)])
        for ct in range(CT):
            b = psum_next()
            pe([lambda: nc.tensor.transpose(out=PS[b][:, 0:c.CK], in_=prw[0:c.CK, ct * 128:(ct + 1) * 128],
                                            identity=ident_f[0:c.CK, 0:c.CK]),
                lambda: nc.tensor.transpose(out=PS[b][:, 32:35], in_=prc[0:3, ct * 128:(ct + 1) * 128],
                                            identity=ident_f[0:3, 0:3])],
               reads=[("stg", 0), ("stg", 1), "cst_f"], writes=[("PS", b)])
            op(DVE, lambda: nc.vector.tensor_copy(out=wdw_c[:, ct, :], in_=PS[b][:, 0:c.CK]), reads=[("PS", b)],
               writes=["wdw_c"])
            op(DVE, lambda: nc.vector.tensor_copy(out=cvp_c[:, :, ct], in_=PS[b][:, 32:35]), reads=[("PS", b)],
               writes=["cvp_c"])
        eps_c = sb("eps_c", [128, 1], F32)
        op(POOL, lambda: nc.gpsimd.memset(eps_c[:, :], LN_EPS), writes=["eps_c"])

        lnp_loaded = [None]

        def load_ln(which):
            if lnp_loaded[0] == which:
                return
            lnp_loaded[0] = which
            dma(ACT, gtile[:, :], lnp[2 * which, :].partition_broadcast(128), "gt", writes=["gtile"])
            dma(ACT, btile[:, :], lnp[2 * which + 1, :].partition_broadcast(128), "bt", writes=["btile"])

        chunks = []
        wstate = {"issued": 0, "cur": 0}

        def wsrc(desc):
            kind = desc[0]
            if kind in ("q", "k", "v"):
                c0 = {"q": 0, "k": AW, "v": 2 * AW}[kind]
                return [(0, KC, AW, w_in[:, c0:c0 + AW].rearrange("(k p) c -> p k c", p=128))]
            if kind == "ag":
                j = desc[1]
                cts = [t for t in (2 * j, 2 * j + 1) if t < CT]
                n = len(cts) * 128
                a0 = 3 * AW + cts[0] * 128
                g0 = 3 * AW + CC + cts[0] * 128
                return [("ag", n, w_in[:, a0:a0 + n].rearrange("(k p) c -> p k c", p=128),
                         w_in[:, g0:g0 + n].rearrange("(k p) c -> p k c", p=128))]
            if kind == "o":
                j = desc[1]
                return [(0, KC, c.OCW, w_out[:, j * c.OCW:(j + 1) * c.OCW].rearrange("(k p) c -> p k c", p=128))]
            if kind == "f1":
                j = desc[1]
                return [(0, KC, 512, w_ff1[:, j * 512:(j + 1) * 512].rearrange("(k p) c -> p k c", p=128))]
            if kind == "f2":
                j = desc[1]
                return [(0, 4, D, w_ff2[j * 512:(j + 1) * 512, :].rearrange("(k p) c -> p k c", p=128))]
            raise ValueError(kind)

        def wview(slot, a, b):
            return ring[slot][:, 0:a * b].rearrange("p (k c) -> p k c", k=a)

        def issue_chunk(m):
            slot = m % NRING
            desc = chunks[m]
            src = wsrc(desc)[0]
            key = ("W", slot)
            if src[0] == "ag":
                n = src[1]
                v = wview(slot, KC, 2 * n)
                dmas(POOL, [(v[:, :, 0:n], src[2]), (v[:, :, n:2 * n], src[3])], ("W", slot), writes=[key])
            else:
                _, a, b, ap = src
                dma(POOL, wview(slot, a, b), ap, ("W", slot), writes=[key])

        def wget():
            m = wstate["cur"]
            wstate["cur"] += 1
            while wstate["issued"] < min(len(chunks), m + NRING - 1):
                issue_chunk(wstate["issued"])
                wstate["issued"] += 1
            return m % NRING, ("W", m % NRING)

        passes = []
        for sq in range(c.NPS):
            for hf in range(c.NHALF):
                tiles = []
                col = 0
                if hf == 0:
                    tiles.append(dict(n=c.NMETA, c0=0, kb=0, pos=0, src=meta[:, :], y=None))
                    col = c.NMETA
                for f in range(hf * c.TPH, (hf + 1) * c.TPH):
                    tiles.append(dict(n=128, c0=col, kb=1 + f, pos=c.NMETA + 128 * f,
                                      src=xp[sq, 128 * f:128 * (f + 1), :], y=yp[sq, 128 * f:128 * (f + 1), :]))
                    col += 128
                groups = []
                i0 = 0
                if hf == 0:
                    groups.append([0])
                    i0 = 1
                fr = list(range(i0, len(tiles)))
                for g in range(0, len(fr), 4):
                    groups.append(fr[g:g + 4])
                passes.append(dict(kind="p", seq=sq, half=hf, tiles=tiles, groups=groups, T=col,
                                   last=(hf == c.NHALF - 1)))
        if c.NSS > 0:
            n = c.NSS * c.DSEQ
            stile = dict(n=n, c0=0, kb=None, pos=None, src=xs[:, :], y=ys[:, :], kind="s")
            if c.MERGE and passes:
                lp = passes[-1]
                stile["c0"] = lp["T"]
                lp["tiles"].append(stile)
                lp["groups"].append([len(lp["tiles"]) - 1])
                lp["Ttot"] = lp["T"] + n
            else:
                passes.append(dict(kind="s", tiles=[stile], groups=[[0]], T=0, Ttot=n, last=True, half=0, seq=0))
        for p_ in passes:
            p_.setdefault("Ttot", p_["T"])
        for p in passes:
            chunks.extend([("q",), ("k",), ("v",)])
            chunks.extend([("ag", j) for j in range((CT + 1) // 2)])
            chunks.extend([("o", j) for j in range(c.NOC)])
            for j in range(c.NFC):
                chunks.extend([("f1", j), ("f2", j)])

        while wstate["issued"] < min(len(chunks), NRING - 1):
            issue_chunk(wstate["issued"])
            wstate["issued"] += 1
        op(POOL, lambda: nc.gpsimd.memset(qz[:, :, :], 0.0), writes=["qz_init"])

        def gcols(p, g):
            ts = [p["tiles"][i] for i in g]
            c0 = ts[0]["c0"]
            W = sum(t["n"] for t in ts)
            return c0, W

        def layer_norm(i, n):
            hk = ("H", i)
            for j in range(D // 512 if D >= 512 else 1):
                w = min(512, D)
                op(DVE, lambda j=j, w=w: nc.vector.bn_stats(out=st6[:n, j, :], in_=Hs[:n, i, j * w:(j + 1) * w]),
                   reads=[hk], writes=[("st6", j)])
            nj = D // 512 if D >= 512 else 1
            op(DVE, lambda: nc.vector.bn_aggr(out=mv[:n, :], in_=st6[:n, 0:nj, :]),
               reads=[("st6", j) for j in range(nj)], writes=["mv"])
            op(ACT, lambda: nc.scalar.activation(out=rstd[:n, :], in_=mv[:n, 1:2], func=AF.Ln, bias=eps_c[:n, :], scale=1.0),
               reads=["mv", "eps_c"], writes=["rstd"])
            op(ACT, lambda: nc.scalar.activation(out=rstd[:n, :], in_=rstd[:n, :], func=AF.Exp, scale=-0.5),
               reads=["rstd"], writes=["rstd"])
            op(DVE, lambda: nc.vector.scalar_tensor_tensor(out=nbias[:n, :], in0=mv[:n, 0:1], scalar=-1.0,
                                                           in1=rstd[:n, :], op0=ALU.mult, op1=ALU.mult),
               reads=["mv", "rstd"], writes=["nbias"])
            op(ACT, lambda: nc.scalar.activation(out=Hs[:n, i, :], in_=Hs[:n, i, :], func=AF.Identity,
                                                 bias=nbias[:n, :], scale=rstd[:n, :]),
               reads=[hk, "rstd", "nbias"], writes=[hk])
            op(DVE, lambda: nc.vector.tensor_tensor(out=Hs[:n, i, :], in0=Hs[:n, i, :], in1=gtile[:n, :], op=ALU.mult),
               reads=[hk, "gtile"], writes=[hk])
            op(DVE, lambda: nc.vector.tensor_tensor(out=Hs[:n, i, :], in0=Hs[:n, i, :], in1=btile[:n, :], op=ALU.add),
               reads=[hk, "btile"], writes=[hk])

        st6a = sb("st6a", [128, c.NTMAX, 2, 6], F32)
        mva = sb("mva", [128, c.NTMAX, 2], F32)
        rsa = sb("rsa", [128, c.NTMAX], F32)
        nba = sb("nba", [128, c.NTMAX], F32)
        op(DVE, lambda: nc.vector.memset(mva[:, :, :], 1.0), writes=["mva"])

        def ln_multi(p, tiles_all, ids):
            i0, NT = ids[0], ids[-1] + 1
            tiles = [(i, tiles_all[i]) for i in ids]
            nj = D // 512 if D >= 512 else 1
            w = min(512, D)
            for i, t in tiles:
                n = t["n"]
                for j in range(nj):
                    op(DVE, lambda j=j: nc.vector.bn_stats(out=st6a[:n, i, j, :], in_=Hs[:n, i, j * w:(j + 1) * w]),
                       reads=[("H", i)], writes=[("st6a", i, j)])
                op(DVE, lambda: nc.vector.bn_aggr(out=mva[:n, i, :], in_=st6a[:n, i, 0:nj, :]),
                   reads=[("st6a", i, j) for j in range(nj)], writes=["mva"])
            op(ACT, lambda: nc.scalar.activation(out=rsa[:, i0:NT], in_=mva[:, i0:NT, 1], func=AF.Ln, bias=eps_c[:, :], scale=1.0),
               reads=["mva", "eps_c"], writes=["rsa"])
            op(ACT, lambda: nc.scalar.activation(out=rsa[:, i0:NT], in_=rsa[:, i0:NT], func=AF.Exp, scale=-0.5),
               reads=["rsa"], writes=["rsa"])
            for i, t in tiles:
                n = t["n"]
                hk = ("H", i)
                op(DVE, lambda: nc.vector.scalar_tensor_tensor(out=Hs[:n, i, :], in0=Hs[:n, i, :], scalar=mva[:n, i, 0:1],
                                                               in1=gtile[:n, :], op0=ALU.subtract, op1=ALU.mult),
                   reads=[hk, "mva", "gtile"], writes=[hk])
                op(DVE, lambda: nc.vector.scalar_tensor_tensor(out=Hs[:n, i, :], in0=Hs[:n, i, :], scalar=rsa[:n, i:i + 1],
                                                               in1=btile[:n, :], op0=ALU.mult, op1=ALU.add),
                   reads=[hk, "rsa", "btile"], writes=[hk])

        def transpose_tile(p, i, n, c0):
            for k0 in range(0, KC, 4):
                kn = min(4, KC - k0)
                b = psum_next()
                pk = ("PS", b)
                pe([lambda j=j: nc.tensor.transpose(out=PS[b][:, j * 128:j * 128 + n],
                                                    in_=Hs[:n, i, (k0 + j) * 128:(k0 + j + 1) * 128],
                                                    identity=ident_f[:n, :n]) for j in range(kn)],
                   reads=[("H", i)], writes=[pk])
                src = PS[b][:, 0:kn * 128].rearrange("p (k t) -> p k t", k=kn)[:, :, 0:n]
                op(ACT, lambda src=src, k0=k0, kn=kn: nc.scalar.copy(out=XT[:, k0:k0 + kn, c0:c0 + n], in_=src),
                   reads=[pk], writes=[("XT", i)])

        stg_rr = [0]

        def stg_next():
            i = stg_rr[0] % NSTG
            stg_rr[0] += 1
            return i

        def run_pass(p):
            tiles, groups = p["tiles"], p["groups"]
            sq = p.get("seq", 0)
            T = p["T"]
            isS = lambda i: tiles[i].get("kind") == "s"
            gS = lambda g: isS(g[0])
            has_s = any(isS(i) for i in range(len(tiles)))
            has_p = any(not isS(i) for i in range(len(tiles)))
            SW = HWc + c.DSEQ
            xtk = lambda g: [("XT", i) for i in g]

            for i, t in enumerate(tiles):
                if not t.get("preloaded"):
                    dma(SP, Hs[:t["n"], i, :], t["src"], ("Hld", i), writes=[("H", i)])
            load_ln(0)
            for g in groups:
                ln_multi(p, tiles, g)

            if c.stop == "ln":
                return
            slot, wk = wget()
            Wt = wview(slot, KC, AW)
            for g in groups:
                c0, W = gcols(p, g)
                for i in g:
                    transpose_tile(p, i, tiles[i]["n"], tiles[i]["c0"])
                for hp in range(HP):
                    b = psum_next()
                    pe([lambda kc=kc: nc.tensor.matmul(PS[b][:, 0:W], lhsT=Wt[:, kc, hp * 128:(hp + 1) * 128],
                                                       rhs=XT[:, kc, c0:c0 + W], start=(kc == 0), stop=(kc == KC - 1))
                        for kc in range(KC)], reads=[wk] + xtk(g), writes=[("PS", b)])
                    op(ACT, lambda: nc.scalar.copy(out=qz[0:64, 2 * hp, c0:c0 + W], in_=PS[b][0:64, 0:W]),
                       reads=[("PS", b), "qz_init"], writes=[("qz", 2 * hp, g[0])])
                    op(DVE, lambda: nc.vector.tensor_copy(out=qz[64:128, 2 * hp + 1, c0:c0 + W], in_=PS[b][64:128, 0:W]),
                       reads=[("PS", b), "qz_init"], writes=[("qz", 2 * hp + 1, g[0])])
            if c.stop == "q":
                return
            slot, wk = wget()
            Wt = wview(slot, KC, AW)
            for g in groups:
                c0, W = gcols(p, g)
                for hp in range(HP):
                    b = psum_next()
                    pe([lambda kc=kc: nc.tensor.matmul(PS[b][:, 0:W], lhsT=Wt[:, kc, hp * 128:(hp + 1) * 128],
                                                       rhs=XT[:, kc, c0:c0 + W], start=(kc == 0), stop=(kc == KC - 1))
                        for kc in range(KC)], reads=[wk] + xtk(g), writes=[("PS", b)])
                    if gS(g):
                        dst, dk = kTnew[:, hp, 0:W], [("kTnew",)]
                    else:
                        pos = tiles[g[0]]["pos"]
                        dst, dk = kTn[:, hp, pos:pos + W], [("kT", tiles[i_]["kb"]) for i_ in g]
                    op(ACT, lambda dst=dst: nc.scalar.mul(out=dst, in_=PS[b][:, 0:W], mul=-0.125),
                       reads=[("PS", b)], writes=dk)
                for i in g:
                    t = tiles[i]
                    n = t["n"]
                    b = psum_next()
                    pe([lambda kc=kc: nc.tensor.matmul(PS[b][:n, 0:AW], lhsT=XT[:, kc, t["c0"]:t["c0"] + n],
                                                       rhs=Wt[:, kc, :], start=(kc == 0), stop=(kc == KC - 1))
                        for kc in range(KC)], reads=[wk, ("XT", i)], writes=[("PS", b)])
                    s_ = stg_next()
                    op(DVE, lambda: nc.vector.tensor_copy(out=stg[s_][:n, 0:AW], in_=PS[b][:n, 0:AW]),
                       reads=[("PS", b)], writes=[("stg", s_)])
                    dst = ksm[0:n, :] if isS(i) else kp[sq, t["pos"]:t["pos"] + n, :]
                    dma(SP, dst, stg[s_][:n, 0:AW], ("stg", s_), reads=[("stg", s_)])
            if c.stop == "k":
                return
            slot, wk = wget()
            Wt = wview(slot, KC, AW)
            for g in groups:
                for i in g:
                    t = tiles[i]
                    n = t["n"]
                    b = psum_next()
                    pe([lambda kc=kc: nc.tensor.matmul(PS[b][:n, 0:AW], lhsT=XT[:, kc, t["c0"]:t["c0"] + n],
                                                       rhs=Wt[:, kc, :], start=(kc == 0), stop=(kc == KC - 1))
                        for kc in range(KC)], reads=[wk, ("XT", i)], writes=[("PS", b)])
                    s_ = stg_next()
                    op(DVE, lambda: nc.vector.tensor_copy(out=stg[s_][:n, 0:AW], in_=PS[b][:n, 0:AW]),
                       reads=[("PS", b)], writes=[("stg", s_)])
                    dst = vsm[0:n, :] if isS(i) else vp[sq, t["pos"]:t["pos"] + n, :]
                    dma(SP, dst, stg[s_][:n, 0:AW], ("stg", s_), reads=[("stg", s_)])
                    if isS(i):
                        op(ACT, lambda: nc.scalar.copy(out=vnew[:n, :], in_=PS[b][:n, 0:AW]),
                           reads=[("PS", b)], writes=["vnew"])
                    else:
                        op(ACT, lambda: nc.scalar.copy(out=vh[:n, t["kb"], :], in_=PS[b][:n, 0:AW]),
                           reads=[("PS", b)], writes=[("vh", t["kb"])])
            if c.stop == "v":
                return
            tail_tiles = []
            ptiles = [i for i in range(len(tiles)) if not isS(i)]
            if has_p and p["last"]:
                tail_tiles.append(ptiles[-1])
            tail_tiles += [i for i in range(len(tiles)) if isS(i)]
            if has_p and p["half"] == 0:
                op(POOL, lambda: nc.gpsimd.memset(ub[:, :, 0:HWc], 0.0), writes=[("ubh",)])
            if has_s:
                for s in range(c.NSS):
                    s_ = stg_next()
                    dma(SP, stg[s_][:HWc, 0:CC], sc[s, :, :], ("stgl", s_), writes=[("stg", s_)])
                    for ct in range(CT):
                        b = psum_next()
                        pe([lambda: nc.tensor.transpose(out=PS[b][:, 0:HWc], in_=stg[s_][:HWc, ct * 128:(ct + 1) * 128],
                                                        identity=ident_f[:HWc, :HWc])],
                           reads=[("stg", s_)], writes=[("PS", b)])
                        op(ACT, lambda: nc.scalar.copy(out=ub[:, ct, UBS + s * SW:UBS + s * SW + HWc], in_=PS[b][:, 0:HWc]),
                           reads=[("PS", b)], writes=[("ubhs",)])
            ust = {}
            for j in range((CT + 1) // 2):
                slot, wk = wget()
                cts = [t_ for t_ in (2 * j, 2 * j + 1) if t_ < CT]
                nn = len(cts) * 128
                Wt = wview(slot, KC, 2 * nn)
                for g in groups:
                    c0, W = gcols(p, g)
                    for ci, ct in enumerate(cts):
                        ba = psum_next()
                        pe([lambda kc=kc: nc.tensor.matmul(PS[ba][:, 0:W], lhsT=Wt[:, kc, ci * 128:(ci + 1) * 128],
                                                           rhs=XT[:, kc, c0:c0 + W], start=(kc == 0), stop=(kc == KC - 1))
                            for kc in range(KC)], reads=[wk] + xtk(g), writes=[("PS", ba)])
                        bg = psum_next()
                        pe([lambda kc=kc: nc.tensor.matmul(PS[bg][:, 0:W], lhsT=Wt[:, kc, nn + ci * 128:nn + (ci + 1) * 128],
                                                           rhs=XT[:, kc, c0:c0 + W], start=(kc == 0), stop=(kc == KC - 1))
                            for kc in range(KC)], reads=[wk] + xtk(g), writes=[("PS", bg)])
                        op(ACT, lambda: nc.scalar.activation(out=tmpA[:, 0:W], in_=PS[bg][:, 0:W], func=AF.Sigmoid),
                           reads=[("PS", bg)], writes=["tmpA"])
                        if gS(g):
                            dst = ub[:, ct, UBS:UBS + c.NSS * SW].rearrange("p (s j) -> p s j", j=SW)[:, :, HWc:SW]
                            src0 = PS[ba][:, 0:W].rearrange("p (s j) -> p s j", j=c.DSEQ)
                            src1 = tmpA[:, 0:W].rearrange("p (s j) -> p s j", j=c.DSEQ)
                        else:
                            dst = ub[:, ct, HWc + c0:HWc + c0 + W]
                            src0, src1 = PS[ba][:, 0:W], tmpA[:, 0:W]
                        op(DVE, lambda dst=dst, src0=src0, src1=src1: nc.vector.tensor_tensor(out=dst, in0=src0, in1=src1,
                                                                                             op=ALU.mult),
                           reads=[("PS", ba), "tmpA", ("ubh",), ("ubhs",)], writes=[("ub", ct, g[0])])
                for i in tail_tiles:
                    t = tiles[i]
                    n = t["n"]
                    if i not in ust:
                        ust[i] = stg_next()
                    ba = psum_next()
                    pe([lambda kc=kc: nc.tensor.matmul(PS[ba][:n, 0:nn], lhsT=XT[:, kc, t["c0"]:t["c0"] + n],
                                                       rhs=Wt[:, kc, 0:nn], start=(kc == 0), stop=(kc == KC - 1))
                        for kc in range(KC)], reads=[wk, ("XT", i)], writes=[("PS", ba)])
                    bg = psum_next()
                    pe([lambda kc=kc: nc.tensor.matmul(PS[bg][:n, 0:nn], lhsT=XT[:, kc, t["c0"]:t["c0"] + n],
                                                       rhs=Wt[:, kc, nn:2 * nn], start=(kc == 0), stop=(kc == KC - 1))
                        for kc in range(KC)], reads=[wk, ("XT", i)], writes=[("PS", bg)])
                    op(ACT, lambda: nc.scalar.activation(out=tmpA[:n, 0:nn], in_=PS[bg][:n, 0:nn], func=AF.Sigmoid),
                       reads=[("PS", bg)], writes=["tmpA"])
                    cc0 = cts[0] * 128
                    op(DVE, lambda: nc.vector.tensor_tensor(out=stg[ust[i]][:n, cc0:cc0 + nn], in0=PS[ba][:n, 0:nn],
                                                            in1=tmpA[:n, 0:nn], op=ALU.mult),
                       reads=[("PS", ba), "tmpA"], writes=[("stg", ust[i])])
            for i in tail_tiles:
                u_ = ust[i]
                if isS(i):
                    dmas(SP, [(cs[s, :, :], stg[u_][s * c.DSEQ + (c.DSEQ - HWc):(s + 1) * c.DSEQ, 0:CC])
                              for s in range(c.NSS)], ("stg", u_), reads=[("stg", u_)])
                else:
                    n = tiles[i]["n"]
                    dma(SP, cp[sq, :, :], stg[u_][n - HWc:n, 0:CC], ("stg", u_), reads=[("stg", u_)])

            if c.stop == "inproj":
                return
            def attn(streams, kbl, mode="all", next_prologue=None):
                nkb = len(kbl)
                ofirst = {}
                xt = [tmpA, tmpB]
                xk = ["tmpA", "tmpB"]
                eb = [[e_t[0], rs_t], [e_t[1], mn_t]]
                ek = [[("e", 0), "rs_t"], [("e", 1), "mn_t"]]

                def ctxs(bi, st):
                    kb = kbl[bi]
                    si = st["si"]
                    return kb, si, kb["nk"], kb["lo"], bi == 0, bi == nkb - 1, st["W"], st["segs"], st["qk"]

                def emit_Z(bi):
                    for st in streams:
                        kb, si, nk, lo, first, last, Wd, segs, rq = ctxs(bi, st)
                        zb = si
                        fns = []
                        for (h, qc0, zc0, w) in segs:
                            lhs = kb["kT"](h // 2)
                            fns.append(lambda lhs=lhs, h=h, qc0=qc0, zc0=zc0, w=w, nk=nk, lo=lo, kb=kb, zb=zb:
                                       nc.tensor.matmul(PS[zb][:nk, zc0 + lo:zc0 + w], lhsT=lhs,
                                                        rhs=qz[:, h, qc0 + lo:qc0 + w], start=True, stop=(kb["dw"] == 0)))
                            if kb["dw"]:
                                fns.append(lambda zc0=zc0, dw=kb["dw"], nk=nk, lo=lo, zb=zb:
                                           nc.tensor.matmul(PS[zb][:nk, zc0 + lo:zc0 + lo + dw], lhsT=ident_b[:nk, :nk],
                                                            rhs=mbig_b[:nk, 0:dw], start=False, stop=True))
                        pe(fns, reads=rq + kb["rk"] + ["cst_b"], writes=[("PS", zb)])

                def emit_e(bi):
                    for st in streams:
                        kb, si, nk, lo, first, last, Wd, segs, rq = ctxs(bi, st)
                        et, etk = eb[si][bi % 2], ek[si][bi % 2]
                        op(ACT, lambda: nc.scalar.activation(out=et[:nk, lo:Wd], in_=PS[si][:nk, lo:Wd], func=AF.Exp,
                                                             scale=-1.0), reads=[("PS", si)], writes=[etk])

                def emit_sp(bi):
                    for st in streams:
                        kb, si, nk, lo, first, last, Wd, segs, rq = ctxs(bi, st)
                        et, etk = eb[si][bi % 2], ek[si][bi % 2]
                        op(ACT, lambda: nc.scalar.activation(out=sp_t[si][:nk, lo:Wd], in_=et[:nk, lo:Wd], func=AF.Ln,
                                                             bias=1.0), reads=[etk], writes=[("sp", si)])

                def emit_tri(bi):
                    for st in streams:
                        kb, si, nk, lo, first, last, Wd, segs, rq = ctxs(bi, st)
                        rb = 2 + si
                        pe([lambda: nc.tensor.matmul(PS[rb][:, lo:Wd], lhsT=tri_b[:nk, :], rhs=sp_t[si][:nk, lo:Wd],
                                                     start=first, stop=True, skip_group_check=True)],
                           reads=[("sp", si), "cst_b"], writes=[("PS", rb)])

                def emit_X(bi):
                    for st in streams:
                        kb, si, nk, lo, first, last, Wd, segs, rq = ctxs(bi, st)
                        rb = 2 + si
                        op(ACT, lambda: nc.scalar.activation(out=xt[si][:nk, lo:Wd], in_=PS[rb][:nk, lo:Wd], func=AF.Exp,
                                                             scale=-1.0), reads=[("PS", rb)], writes=[xk[si]])

                def emit_U(bi):
                    for st in streams:
                        kb, si, nk, lo, first, last, Wd, segs, rq = ctxs(bi, st)
                        rb = 2 + si
                        if not last:
                            pe([lambda: nc.tensor.matmul(PS[rb][:, lo:Wd], lhsT=u_b[:nk, :], rhs=sp_t[si][:nk, lo:Wd],
                                                         start=False, stop=True, skip_group_check=True)],
                               reads=[("sp", si), "cst_b"], writes=[("PS", rb)])

                def emit_a(bi):
                    for st in streams:
                        kb, si, nk, lo, first, last, Wd, segs, rq = ctxs(bi, st)
                        et, etk = eb[si][bi % 2], ek[si][bi % 2]
                        op(DVE, lambda: nc.vector.tensor_tensor(out=a_t[si][:nk, lo:Wd], in0=et[:nk, lo:Wd],
                                                                in1=xt[si][:nk, lo:Wd], op=ALU.mult),
                           reads=[etk, xk[si]], writes=[("a", si)])

                def emit_O(bi):
                    for st in streams:
                        kb, si, nk, lo, first, last, Wd, segs, rq = ctxs(bi, st)
                        fns = []
                        wr = []
                        for sgi, (h, qc0, zc0, w) in enumerate(segs):
                            ob, oc0 = st["o"][sgi]
                            vl = kb["v"](h // 2)
                            stt = ob not in ofirst
                            ofirst[ob] = True
                            if ("PS", ob) not in wr:
                                wr.append(("PS", ob))
                            fns.append(lambda vl=vl, zc0=zc0, w=w, ob=ob, oc0=oc0, stt=stt, nk=nk, lo=lo, si=si:
                                       nc.tensor.matmul(PS[ob][:, oc0 + lo:oc0 + w], lhsT=vl, rhs=a_t[si][:nk, zc0 + lo:zc0 + w],
                                                        start=stt, stop=True, skip_group_check=True))
                        pe(fns, reads=[("a", si)] + kb["vk"], writes=wr)

                if mode in ("all", "prologue"):
                    emit_Z(0)
                    emit_e(0)
                    emit_sp(0)
                if mode == "prologue":
                    return
                for bi in range(nkb):
                    emit_tri(bi)
                    emit_X(bi)
                    if bi + 1 < nkb:
                        emit_Z(bi + 1)
                        emit_e(bi + 1)
                    emit_a(bi)
                    emit_U(bi)
                    if bi + 1 < nkb:
                        emit_sp(bi + 1)
                    elif next_prologue is not None:
                        next_prologue()
                    emit_O(bi)

            def attn1(st, kbl, hook=None):
                nkb = len(kbl)
                Wd = st["W"]
                segs = st["segs"]
                rq = st["qk"]
                assert Wd <= 256
                E4 = [(e_t[0], ("e", 0)), (rs_t, "rs_t"), (e_t[1], ("e", 1)), (mn_t, "mn_t")]
                XA = [(tmpA, "tmpA"), (tmpB, "tmpB")]
                ofirst = {}

                def spv(bi):
                    j = bi % 4
                    return sp_t[j // 2][:, (j % 2) * 256:(j % 2) * 256 + Wd], ("sp1", j)

                def front(bi):
                    kb = kbl[bi]
                    nk = kb["nk"]
                    zb = bi % 2
                    fns = []
                    for (h, qc0, zc0, w) in segs:
                        lhs = kb["kT"](h // 2)
                        fns.append(lambda lhs=lhs, h=h, qc0=qc0, zc0=zc0, w=w:
                                   nc.tensor.matmul(PS[zb][:nk, zc0:zc0 + w], lhsT=lhs, rhs=qz[:, h, qc0:qc0 + w],
                                                    start=True, stop=(kb["dw"] == 0)))
                        if kb["dw"]:
                            fns.append(lambda zc0=zc0, dw=kb["dw"]:
                                       nc.tensor.matmul(PS[zb][:nk, zc0:zc0 + dw], lhsT=ident_b[:nk, :nk],
                                                        rhs=mbig_b[:nk, 0:dw], start=False, stop=True))
                    pe(fns, reads=rq + kb["rk"] + ["cst_b"], writes=[("PS", zb)])
                    et, etk = E4[bi % 4]
                    spa, spk = spv(bi)
                    op(ACT, lambda: nc.scalar.activation(out=et[:nk, 0:Wd], in_=PS[zb][:nk, 0:Wd], func=AF.Exp, scale=-1.0),
                       reads=[("PS", zb)], writes=[etk])
                    op(ACT, lambda: nc.scalar.activation(out=spa[:nk, :], in_=et[:nk, 0:Wd], func=AF.Ln, bias=1.0),
                       reads=[etk], writes=[spk, ("sp", 0), ("sp", 1)])

                def emit_O(bi):
                    kb = kbl[bi]
                    nk = kb["nk"]
                    at = a_t[bi % 2]
                    fns, wr = [], []
                    for sgi, (h, qc0, zc0, w) in enumerate(segs):
                        ob, oc0 = st["o"][sgi]
                        vl = kb["v"](h // 2)
                        stt = ob not in ofirst
                        ofirst[ob] = True
                        if ("PS", ob) not in wr:
                            wr.append(("PS", ob))
                        fns.append(lambda vl=vl, zc0=zc0, w=w, ob=ob, oc0=oc0, stt=stt:
                                   nc.tensor.matmul(PS[ob][:, oc0:oc0 + w], lhsT=vl, rhs=at[:nk, zc0:zc0 + w],
                                                    start=stt, stop=True, skip_group_check=True))
                    pe(fns, reads=[("a", bi % 2)] + kb["vk"], writes=wr)

                front(0)
                if nkb > 1:
                    front(1)
                for bi in range(nkb):
                    kb = kbl[bi]
                    nk = kb["nk"]
                    first, last = bi == 0, bi == nkb - 1
                    spa, spk = spv(bi)
                    et, etk = E4[bi % 4]
                    xa, xak = XA[bi % 2]
                    pe([lambda: nc.tensor.matmul(PS[2][:, 0:Wd], lhsT=tri_b[:nk, :], rhs=spa[:nk, :], start=first, stop=True,
                                                 skip_group_check=True)], reads=[spk, "cst_b"], writes=[("PS", 2)])
                    if bi > 0:
                        emit_O(bi - 1)
                    op(ACT, lambda: nc.scalar.activation(out=xa[:nk, 0:Wd], in_=PS[2][:nk, 0:Wd], func=AF.Exp, scale=-1.0),
                       reads=[("PS", 2)], writes=[xak])
                    if bi + 2 < nkb:
                        front(bi + 2)
                    op(DVE, lambda: nc.vector.tensor_tensor(out=a_t[bi % 2][:nk, 0:Wd], in0=et[:nk, 0:Wd], in1=xa[:nk, 0:Wd],
                                                            op=ALU.mult), reads=[etk, xak], writes=[("a", bi % 2)])
                    if not last:
                        pe([lambda: nc.tensor.matmul(PS[2][:, 0:Wd], lhsT=u_b[:nk, :], rhs=spa[:nk, :], start=False, stop=True,
                                                     skip_group_check=True)], reads=[spk, "cst_b"], writes=[("PS", 2)])
                    if hook is not None:
                        hook(bi)
                emit_O(nkb - 1)

            def prompt_attn(g):
                c0, W = gcols(p, g)
                kb_hi = tiles[g[-1]]["kb"]
                kb_lo = tiles[g[0]]["kb"]
                kbl = []
                for kbx in range(kb_hi, -1, -1):
                    nk = c.NMETA if kbx == 0 else 128
                    kpos = 0 if kbx == 0 else c.NMETA + 128 * (kbx - 1)
                    if kbx >= kb_lo:
                        lo, dw = (kbx - kb_lo) * 128, nk
                    else:
                        lo, dw = 0, 0
                    kbl.append(dict(nk=nk, lo=lo, dw=dw,
                                    kT=(lambda hp, kpos=kpos, nk=nk: kTn[:, hp, kpos:kpos + nk]),
                                    v=(lambda hp, kbx=kbx, nk=nk: vh[:nk, kbx, hp * 128:(hp + 1) * 128]),
                                    rk=[("kT", kbx)], vk=[("vh", kbx)]))
                for hp in range(HP):
                    streams = [dict(si=si, segs=[(2 * hp + si, c0, 0, W)], W=W, qk=[("qz", 2 * hp + si, g[0])],
                                    o=[(6 + si, 0)]) for si in range(2)]

                    def evac(hp=hp):
                        op(ACT, lambda: nc.scalar.copy(out=XT[0:64, hp, c0:c0 + W], in_=PS[6][0:64, 0:W]),
                           reads=[("PS", 6)], writes=[("XT", i) for i in g])
                        op(DVE, lambda: nc.vector.tensor_copy(out=XT[64:128, hp, c0:c0 + W], in_=PS[7][64:128, 0:W]),
                           reads=[("PS", 7)], writes=[("XT", i) for i in g])
                    attn_jobs.append((streams, kbl, evac))

            def sample_attn(g):
                ti = g[0]
                cs0 = tiles[ti]["c0"]
                NCB = c.PAST // 128
                DS = c.DSEQ
                VG = 4 if NCB % 4 == 0 else 1
                allk = [("kT", x) for x in range(c.NKB + 1)]

                def prep_new(s):
                    dma(SP, vh[0:DS, NCB, :], vnew[s * DS:(s + 1) * DS, :], ("vnl",), reads=["vnew"],
                        writes=[("vh", NCB)])

                def prep_block(s, kb):
                    if kb % VG == 0:
                        dma(POOL, vh[:, kb:kb + VG, :], cv[s, kb * 128:(kb + VG) * 128, :].rearrange("(k p) c -> p k c", p=128),
                            ("cvl", (kb // VG) % 4), writes=[("vh", x) for x in range(kb, kb + VG)])
                    ks_ = kst_rr[0] % len(kstage)
                    kst_rr[0] += 1
                    kbuf, kkey = kstage[ks_]
                    dma(SP, kbuf[:, 0:AW], ck[s, kb * 128:(kb + 1) * 128, :], ("kstl", ks_), writes=kkey)
                    b = psum_next(4, 6)
                    pe([lambda hp=hp: nc.tensor.transpose(out=PS[b][:, hp * 128:(hp + 1) * 128],
                                                          in_=kbuf[:, hp * 128:(hp + 1) * 128], identity=ident_f[:, :])
                        for hp in range(HP)], reads=kkey + ["cst_f"], writes=[("PS", b)])
                    op(ACT, lambda: nc.scalar.mul(out=kTn[:, 0:HP, kb * 128:(kb + 1) * 128],
                                                  in_=PS[b][:, 0:HP * 128].rearrange("p (k t) -> p k t", k=HP), mul=-0.125),
                       reads=[("PS", b)], writes=[("kTs", kb)] + (allk if s == 0 else []))

                prep_new(0)
                for kb in range(NCB - 1, -1, -1):
                    prep_block(0, kb)
                for s in range(c.NSS):
                    kbl = [dict(nk=DS, lo=0, dw=DS, kT=(lambda hp, s=s: kTnew[:, hp, s * DS:(s + 1) * DS]),
                                v=(lambda hp: vh[:DS, NCB, hp * 128:(hp + 1) * 128]), rk=[("kTnew",)], vk=[("vh", NCB)])]
                    for kb in range(NCB - 1, -1, -1):
                        kbl.append(dict(nk=128, lo=0, dw=0, kT=(lambda hp, kb=kb: kTn[:, hp, kb * 128:(kb + 1) * 128]),
                                        v=(lambda hp, kb=kb: vh[:, kb, hp * 128:(hp + 1) * 128]),
                                        rk=[("kTs", kb)], vk=[("vh", kb)]))
                    st = dict(si=0, segs=[(h, cs0 + s * DS, h * DS, DS) for h in range(H)], W=H * DS,
                              qk=[("qz", h, ti) for h in range(H)], o=[(6 + (h % 2), (h // 2) * DS) for h in range(H)])
                    todo = list(range(NCB - 1, -1, -1)) if s + 1 < c.NSS else []
                    state = {"new": s + 1 < c.NSS}

                    def hook(bi, s=s, todo=todo, state=state):
                        if state["new"] and bi >= 2:
                            prep_new(s + 1)
                            state["new"] = False
                        while todo:
                            kb = todo[0]
                            need = (NCB - (kb - (kb % VG))) + 1 if VG > 1 else (NCB - kb) + 1
                            if bi < need:
                                break
                            prep_block(s + 1, todo.pop(0))
                    attn1(st, kbl, hook)
                    for kb in todo[:]:
                        prep_block(s + 1, todo.pop(0))
                    if state["new"]:
                        prep_new(s + 1)
                    op(ACT, lambda: nc.scalar.copy(out=XT[0:64, 0:HP, cs0 + s * DS:cs0 + (s + 1) * DS],
                                                   in_=PS[6][0:64, 0:HP * DS].rearrange("p (k t) -> p k t", k=HP)),
                       reads=[("PS", 6)], writes=[("XT", ti)])
                    op(DVE, lambda: nc.vector.tensor_copy(out=XT[64:128, 0:HP, cs0 + s * DS:cs0 + (s + 1) * DS],
                                                          in_=PS[7][64:128, 0:HP * DS].rearrange("p (k t) -> p k t", k=HP)),
                       reads=[("PS", 7)], writes=[("XT", ti)])

            lgroups = [g for g in groups if any(tiles[i]["y"] is not None for i in g)]
            attn_jobs = []
            for g in lgroups:
                if not gS(g):
                    prompt_attn(g)
            for n_, (st_, kb_, ev_) in enumerate(attn_jobs):
                if n_ == 0:
                    attn(st_, kb_, mode="prologue")
                nxt_ = attn_jobs[n_ + 1] if n_ + 1 < len(attn_jobs) else None
                attn(st_, kb_, mode="body",
                     next_prologue=(lambda nxt_=nxt_: attn(nxt_[0], nxt_[1], mode="prologue")) if nxt_ else None)
                ev_()
            for g in lgroups:
                if gS(g):
                    sample_attn(g)

            if c.stop == "attn":
                return
            def conv_group(g):
                c0, W = gcols(p, g)
                if gS(g):
                    csegs = [(UBS + s * SW, c.DSEQ, s * c.DSEQ) for s in range(c.NSS)]
                else:
                    csegs = [(c0, W, 0)]
                s1b, s2b = psum_next(), psum_next()
                while s2b == s1b:
                    s2b = psum_next()
                ubk = [("ub", ct_, g_[0]) for ct_ in range(CT) for g_ in groups] + [("ubh",), ("ubhs",)]
                pend_stats = [None]
                for ct in range(CT):
                    db = dg_rr[0] % 2
                    dg_rr[0] += 1
                    op(DVE, lambda: nc.vector.tensor_tensor(
                        out=diag[db][:, :, :], in0=ident_b.unsqueeze(1).to_broadcast([128, c.CK, 128]),
                        in1=wdw_c[:, ct, :].unsqueeze(2).to_broadcast([128, c.CK, 128]), op=ALU.mult),
                       reads=["cst_b", "wdw_c"], writes=[("diag", db, 0)])
                    b = psum_next()
                    while b in (s1b, s2b):
                        b = psum_next()
                    fns = []
                    for (u0, w, o0) in csegs:
                        for k in range(c.CK):
                            fns.append(lambda u0=u0, w=w, o0=o0, k=k:
                                       nc.tensor.matmul(PS[b][:, o0:o0 + w], lhsT=diag[db][:, k, :], rhs=ub[:, ct, u0 + k:u0 + k + w],
                                                        start=(k == 0), stop=(k == c.CK - 1)))
                    pe(fns, reads=[("diag", db, 0)] + ubk, writes=[("PS", b)])
                    op(ACT, lambda: nc.scalar.activation(out=cf[:, ct, 0:W], in_=PS[b][:, 0:W], func=AF.Identity,
                                                         bias=cvp_c[:, 0, ct:ct + 1], scale=1.0),
                       reads=[("PS", b), "cvp_c"], writes=[("cf", ct), "cfh"])
                    sqb = [(e_t[0], ("e", 0)), (e_t[1], ("e", 1))][ct % 2]
                    op(ACT, lambda: nc.scalar.activation(out=sqb[0][:, 0:W], in_=cf[:, ct, 0:W], func=AF.Square),
                       reads=[("cf", ct), "cfh"], writes=[sqb[1]])
                    def stats(ct=ct, sqb=sqb):
                        pe([lambda: nc.tensor.matmul(PS[s1b][:, 0:W], lhsT=ones_f[:, :], rhs=cf[:, ct, 0:W], start=(ct == 0),
                                                     stop=(ct == CT - 1), skip_group_check=True)],
                           reads=[("cf", ct), "cfh", "cst_f"], writes=[("PS", s1b)])
                        pe([lambda: nc.tensor.matmul(PS[s2b][:, 0:W], lhsT=ones_f[:, :], rhs=sqb[0][:, 0:W], start=(ct == 0),
                                                     stop=(ct == CT - 1), skip_group_check=True)],
                           reads=[sqb[1], "cst_f"], writes=[("PS", s2b)])
                    if pend_stats[0] is not None:
                        pend_stats[0]()
                    pend_stats[0] = stats
                pend_stats[0]()
                pend_stats[0] = None
                if has_p and g is [g_ for g_ in groups if not gS(g_)][-1] and not p["last"]:
                    op(POOL, lambda: nc.gpsimd.tensor_copy(out=tmpH[:, :, :], in_=ub[:, :, T:T + HWc]),
                       reads=ubk, writes=["tmpH"])
                    op(POOL, lambda: nc.gpsimd.tensor_copy(out=ub[:, :, 0:HWc], in_=tmpH[:, :, :]),
                       reads=["tmpH"], writes=[("ubh",)])
                inv = 1.0 / CC
                op(DVE, lambda: nc.vector.tensor_scalar(out=mn_t[:, 0:W], in0=PS[s1b][:, 0:W], scalar1=inv, scalar2=None,
                                                        op0=ALU.mult), reads=[("PS", s1b)], writes=["mn_t"])
                op(DVE, lambda: nc.vector.tensor_tensor(out=tmpB[:, 0:W], in0=mn_t[:, 0:W], in1=mn_t[:, 0:W], op=ALU.mult),
                   reads=["mn_t"], writes=["tmpB"])
                op(DVE, lambda: nc.vector.scalar_tensor_tensor(out=rs_t[:, 0:W], in0=PS[s2b][:, 0:W], scalar=inv,
                                                               in1=tmpB[:, 0:W], op0=ALU.mult, op1=ALU.subtract),
                   reads=[("PS", s2b), "tmpB"], writes=["rs_t"])
                op(ACT, lambda: nc.scalar.activation(out=rs_t[:, 0:W], in_=rs_t[:, 0:W], func=AF.Ln, bias=eps_c[:, :], scale=1.0),
                   reads=["rs_t", "eps_c"], writes=["rs_t"])
                op(ACT, lambda: nc.scalar.activation(out=rs_t[:, 0:W], in_=rs_t[:, 0:W], func=AF.Exp, scale=-0.5),
                   reads=["rs_t"], writes=["rs_t"])
                for ct in range(CT):
                    op(DVE, lambda: nc.vector.tensor_tensor(out=tmpA[:, 0:W], in0=cf[:, ct, 0:W], in1=mn_t[:, 0:W],
                                                            op=ALU.subtract), reads=[("cf", ct), "cfh", "mn_t"], writes=["tmpA"])
                    op(DVE, lambda: nc.vector.tensor_tensor(out=tmpA[:, 0:W], in0=tmpA[:, 0:W], in1=rs_t[:, 0:W],
                                                            op=ALU.mult), reads=["tmpA", "rs_t"], writes=["tmpA"])
                    op(ACT, lambda: nc.scalar.activation(out=XT[:, HP + ct, c0:c0 + W], in_=tmpA[:, 0:W], func=AF.Silu,
                                                         bias=cvp_c[:, 2, ct:ct + 1], scale=cvp_c[:, 1, ct:ct + 1]),
                       reads=["tmpA", "cvp_c"], writes=[("XT", i) for i in g])

            load_ln(1)
            osl = [wget() for j in range(c.NOC)]

            def outproj_group(g):
                for i in g:
                    t = tiles[i]
                    n = t["n"]
                    for j in range(c.NOC):
                        slot, wk = osl[j]
                        Wt = wview(slot, KC, c.OCW)
                        b = psum_next()
                        pe([lambda kc=kc: nc.tensor.matmul(PS[b][:n, 0:c.OCW], lhsT=XT[:, kc, t["c0"]:t["c0"] + n],
                                                           rhs=Wt[:, kc, :], start=(kc == 0), stop=(kc == KC - 1))
                            for kc in range(KC)], reads=[wk, ("XT", i)], writes=[("PS", b)])
                        hsl = Hs[:n, i, j * c.OCW:(j + 1) * c.OCW]
                        op(DVE, lambda: nc.vector.scalar_tensor_tensor(out=hsl, in0=hsl, scalar=c.ALPHA,
                                                                       in1=PS[b][:n, 0:c.OCW], op0=ALU.mult, op1=ALU.add),
                           reads=[("PS", b), ("H", i)], writes=[("H", i)])
                ln_multi(p, tiles, g)
                for i in g:
                    transpose_tile(p, i, tiles[i]["n"], tiles[i]["c0"])

            conv_group(lgroups[0])
            for gi, g in enumerate(lgroups):
                if gi + 1 < len(lgroups):
                    conv_group(lgroups[gi + 1])
                outproj_group(g)

            if c.stop == "out":
                return
            load_ln(2)
            for j in range(c.NFC):
                s1, wk1 = wget()
                s2, wk2 = wget()
                W1 = wview(s1, KC, 512)
                W2 = wview(s2, 4, D)

                def ffn1(g, W1=W1, wk1=wk1):
                    c0, W = gcols(p, g)
                    hb = hid_rr[0] % 2
                    hid_rr[0] += 1
                    for fc in range(4):
                        b = psum_next()
                        pe([lambda kc=kc: nc.tensor.matmul(PS[b][:, 0:W], lhsT=W1[:, kc, fc * 128:(fc + 1) * 128],
                                                           rhs=XT[:, kc, c0:c0 + W], start=(kc == 0), stop=(kc == KC - 1))
                            for kc in range(KC)], reads=[wk1] + xtk(g), writes=[("PS", b)])
                        op(ACT, lambda: nc.scalar.activation(out=e_t[fc % 2][:, 0:W], in_=PS[b][:, 0:W], func=AF.Relu),
                           reads=[("PS", b)], writes=[("e", fc % 2)])
                        op(DVE, lambda: nc.vector.tensor_tensor(out=hidT[hb][:, fc, 0:W], in0=e_t[fc % 2][:, 0:W],
                                                                in1=e_t[fc % 2][:, 0:W], op=ALU.mult),
                           reads=[("e", fc % 2)], writes=[("hid", hb), "cfh"])
                    return hb

                def ffn2(g, hb, W2=W2, wk2=wk2, j=j):
                    c0, W = gcols(p, g)
                    for i in g:
                        t = tiles[i]
                        n = t["n"]
                        off = t["c0"] - c0
                        for oc in range(c.NOC):
                            b = psum_next()
                            pe([lambda fc=fc: nc.tensor.matmul(PS[b][:n, 0:c.OCW], lhsT=hidT[hb][:, fc, off:off + n],
                                                               rhs=W2[:, fc, oc * c.OCW:(oc + 1) * c.OCW], start=(fc == 0),
                                                               stop=(fc == 3)) for fc in range(4)],
                               reads=[wk2, ("hid", hb), "cfh"], writes=[("PS", b)])
                            hsl = Hs[:n, i, oc * c.OCW:(oc + 1) * c.OCW]
                            if j == 0:
                                op(DVE, lambda: nc.vector.scalar_tensor_tensor(out=hsl, in0=hsl, scalar=c.ALPHA,
                                                                               in1=PS[b][:n, 0:c.OCW], op0=ALU.mult, op1=ALU.add),
                                   reads=[("PS", b), ("H", i)], writes=[("H", i)])
                            else:
                                op(DVE, lambda: nc.vector.tensor_tensor(out=hsl, in0=hsl, in1=PS[b][:n, 0:c.OCW], op=ALU.add),
                                   reads=[("PS", b), ("H", i)], writes=[("H", i)])
                    if j == c.NFC - 1:
                        ln_multi(p, tiles, g)
                        for i in g:
                            t = tiles[i]
                            n = t["n"]
                            if t["y"] is not None:
                                dma(SP, t["y"], Hs[:n, i, :], ("Hst", i), reads=[("H", i)])
                        for i in g:
                            nxt = p.get("next")
                            if nxt is not None and i < len(nxt["tiles"]):
                                tn = nxt["tiles"][i]
                                dma(SP, Hs[:tn["n"], i, :], tn["src"], ("Hld", i), writes=[("H", i)])
                                tn["preloaded"] = True

                hbs = {}
                hbs[0] = ffn1(lgroups[0])
                for gi, g in enumerate(lgroups):
                    if gi + 1 < len(lgroups):
                        hbs[gi + 1] = ffn1(lgroups[gi + 1])
                    ffn2(g, hbs[gi])
            nxt = p.get("next")
            for g in groups:
                if g not in lgroups:
                    for i in g:
                        if nxt is not None and i < len(nxt["tiles"]) and not nxt["tiles"][i].get("preloaded"):
                            tn = nxt["tiles"][i]
                            dma(SP, Hs[:tn["n"], i, :], tn["src"], ("Hld", i), writes=[("H", i)])
                            tn["preloaded"] = True

        for pi_, p in enumerate(passes):
            p["next"] = passes[pi_ + 1] if pi_ + 1 < len(passes) else None
        for p in passes:
            run_pass(p)
        assert c.stop or wstate["cur"] == len(chunks)
        for k, v in ctx.dcnt.items():
            if v:
                SP.wait(k, v)
        SP.e.nop() if hasattr(SP.e, "nop") else None
    return nc


def make_consts():
    cst = np.zeros((128, NCONST * 128), np.float32)
    j = np.arange(128)[:, None]
    s = np.arange(128)[None, :]
    cst[:, 0:128] = (j == s)
    cst[:, 128:256] = 1.0
    cst[:, 256:384] = (j >= s)
    cst[:, 384:512] = BIG * (j >= s)
    cst[0, 512:640] = 1.0
    cst[:, 640:768] = (j < s)
    return cst


def core_inputs(c, core, inp):
    f = lambda a: np.ascontiguousarray(a, dtype=np.float32)
    ps = slice(core * c.NPS, (core + 1) * c.NPS)
    ss = slice(core * c.NSS, (core + 1) * c.NSS)
    return {
        "xp": f(inp["x_prompt"][ps]),
        "xs": f(inp["x_sample"][ss].reshape(c.NSS * c.DSEQ, c.D)),
        "ck": f(inp["cache_k"][0, ss].reshape(c.NSS, c.PAST, c.AW)),
        "cv": f(inp["cache_v"][0, ss].reshape(c.NSS, c.PAST, c.AW)),
        "sc": f(inp["state_conv"][0, ss]),
        "meta": f(inp["meta"]),
        "lnp": f(np.stack([inp["g_in"], inp["b_in"], inp["g_ln1"][0], inp["b_ln1"][0], inp["g_ln2"][0], inp["b_ln2"][0]])),
        "w_in": f(inp["w_in"][0]),
        "w_dw": f(inp["w_dw"][0]),
        "cvp": f(np.stack([inp["b_dw"][0], inp["g_conv"][0], inp["b_conv"][0]])),
        "w_out": f(inp["w_out"][0]),
        "w_ff1": f(inp["w_ff1"][0]),
        "w_ff2": f(inp["w_ff2"][0]),
        "consts": make_consts(),
    }


def assemble(c, res, ncores):
    cat = lambda k: np.concatenate([np.asarray(r[k]) for r in res], axis=0)
    B = ncores * c.NPS
    DB = ncores * c.NSS
    yp = cat("yp").reshape(B, c.SEQ, c.D)
    ys = cat("ys").reshape(DB, c.DSEQ, c.D)
    kp = cat("kp").reshape(1, B, c.LP, c.H, 64)
    vp = cat("vp").reshape(1, B, c.LP, c.H, 64)
    cp = cat("cp").reshape(1, B, c.HW, c.CC)
    ksm = cat("ksm").reshape(1, DB, c.DSEQ, c.H, 64)
    vsm = cat("vsm").reshape(1, DB, c.DSEQ, c.H, 64)
    cs = cat("cs").reshape(1, DB, c.HW, c.CC)
    return tuple(np.ascontiguousarray(a, dtype=np.float32) for a in (yp, ys, kp, vp, cp, ksm, vsm, cs))


_NC_CACHE = {}


def kernel(**inputs):
    ncores = 8
    c = Cfg()
    if "nc" not in _NC_CACHE:
        _NC_CACHE["nc"] = build(c)
    nc = _NC_CACHE["nc"]
    in_maps = [core_inputs(c, i, inputs) for i in range(ncores)]
    res = run_bass_kernel_spmd(nc, in_maps, core_ids=list(range(ncores)))
    return assemble(c, res.results, ncores)
```

```python
import numpy as np
from contextlib import ExitStack
import concourse.bass as bass
import concourse.mybir as mybir
from concourse.bass_utils import run_bass_kernel_spmd

F32 = mybir.dt.float32
BF16 = mybir.dt.bfloat16
ALU = mybir.AluOpType
AF = mybir.ActivationFunctionType

LN_EPS = 1e-5
BIG = 30000.0
NCONST = 6


class Cfg:
    def __init__(s, D=1024, H=8, DFF=4096, SEQ=2048, NPS=2, NSS=4, DSEQ=32, PAST=2048,
                 NMETA=16, CK=31, NHALF=2, depth=1, stop=None, merge=True):
        s.stop = stop
        s.D, s.H, s.DFF, s.SEQ, s.NPS, s.NSS, s.DSEQ, s.PAST = D, H, DFF, SEQ, NPS, NSS, DSEQ, PAST
        s.NMETA, s.CK, s.NHALF = NMETA, CK, NHALF
        s.AW = H * 64
        s.CC = D - s.AW
        s.KC = D // 128
        s.HP = H // 2
        s.CT = s.CC // 128
        s.NFC = DFF // 512
        s.NF = SEQ // 128
        s.LP = NMETA + SEQ
        s.INC = 3 * s.AW + 2 * s.CC
        s.ALPHA = (2.0 * depth) ** 0.25
        s.OCW = min(512, D)
        s.NOC = D // s.OCW
        assert s.AW <= 512 and s.CC <= 512 and s.KC * 512 <= 4096
        assert s.NF % NHALF == 0
        s.TPH = s.NF // NHALF
        s.MERGE = merge
        s.TPMAX = NMETA + s.TPH * 128
        s.TMAX = max(s.TPMAX, s.TPH * 128 + NSS * DSEQ if merge else NSS * DSEQ)
        s.NTMAX = s.TPH + 1
        s.LMAX = max(s.LP, PAST + DSEQ)
        s.NKB = max(1 + s.NF, PAST // 128 + 1)
        s.HW = CK - 1


class Ctx:
    def __init__(s, nc, stack):
        s.nc, s.stack = nc, stack
        s.sems = {}
        s.dcnt = {}

    def sem(s, key):
        if key not in s.sems:
            s.sems[key] = s.stack.enter_context(s.nc.semaphore("s%d" % len(s.sems)))
            s.dcnt[key] = 0
        return s.sems[key]


class Eng:
    def __init__(s, ctx, name, e):
        s.ctx, s.name, s.e = ctx, name, e
        s.key = ("eng", name)
        ctx.sem(s.key)
        s.cnt = 0
        s.seen = {}

    def wait(s, k, v):
        if s.seen.get(k, 0) >= v:
            return
        s.e.wait_ge(s.ctx.sems[k], v)
        s.seen[k] = v

    def sig(s, inst):
        inst.then_inc(s.ctx.sems[s.key], 1)
        s.cnt += 1
        return (s.key, s.cnt)


class Dep:
    def __init__(s):
        s.w = {}
        s.r = {}

    def pre(s, eng, reads, writes):
        if eng.name == "pe":
            return s.pre_pe(eng, reads, writes)
        for k in reads:
            for sk, v in s.w.get(k, {}).items():
                eng.wait(sk, v)
            if isinstance(k, tuple) and k[0] == "PS":
                for sk, v in s.r.get(k, {}).items():
                    if sk != eng.key:
                        eng.wait(sk, v)
        for k in writes:
            for sk, v in s.w.get(k, {}).items():
                eng.wait(sk, v)
            for sk, v in s.r.get(k, {}).items():
                eng.wait(sk, v)

    def pre_pe(s, eng, reads, writes):
        for k in reads:
            for sk, v in s.w.get(k, {}).items():
                if sk != eng.key:
                    eng.wait(sk, v)
        for k in writes:
            for d in (s.w.get(k, {}), s.r.get(k, {})):
                for sk, v in d.items():
                    if sk != eng.key:
                        eng.wait(sk, v)

    def post(s, ev, reads, writes):
        for k in reads:
            d = s.r.setdefault(k, {})
            d[ev[0]] = max(d.get(ev[0], 0), ev[1])
        for k in writes:
            d = s.w.setdefault(k, {})
            d[ev[0]] = max(d.get(ev[0], 0), ev[1])


def build(cfg):
    c = cfg
    nc = bass.Bass("TRN2", target_bir_lowering=False)
    D, KC, HP, CT, AW, CC, H = c.D, c.KC, c.HP, c.CT, c.AW, c.CC, c.H
    HWc = c.HW

    def din(name, shape):
        return nc.dram_tensor(name, list(shape), F32, kind="ExternalInput").ap()

    def dout(name, shape):
        return nc.dram_tensor(name, list(shape), F32, kind="ExternalOutput").ap()

    xp = din("xp", [c.NPS, c.SEQ, D])
    xs = din("xs", [c.NSS * c.DSEQ, D])
    ck = din("ck", [c.NSS, c.PAST, AW])
    cv = din("cv", [c.NSS, c.PAST, AW])
    sc = din("sc", [c.NSS, HWc, CC])
    meta = din("meta", [c.NMETA, D])
    lnp = din("lnp", [6, D])
    w_in = din("w_in", [D, c.INC])
    w_dw = din("w_dw", [c.CK, CC])
    cvp = din("cvp", [3, CC])
    w_out = din("w_out", [D, D])
    w_ff1 = din("w_ff1", [D, c.DFF])
    w_ff2 = din("w_ff2", [c.DFF, D])
    consts = din("consts", [128, NCONST * 128])

    yp = dout("yp", [c.NPS, c.SEQ, D])
    ys = dout("ys", [c.NSS * c.DSEQ, D])
    kp = dout("kp", [c.NPS, c.LP, AW])
    vp = dout("vp", [c.NPS, c.LP, AW])
    cp = dout("cp", [c.NPS, HWc, CC])
    ksm = dout("ksm", [c.NSS * c.DSEQ, AW])
    vsm = dout("vsm", [c.NSS * c.DSEQ, AW])
    cs = dout("cs", [c.NSS, HWc, CC])

    stack = ExitStack()
    with stack:
        ctx = Ctx(nc, stack)
        dep = Dep()
        PE = Eng(ctx, "pe", nc.tensor)
        ACT = Eng(ctx, "act", nc.scalar)
        DVE = Eng(ctx, "dve", nc.vector)
        POOL = Eng(ctx, "pool", nc.gpsimd)
        SP = Eng(ctx, "sp", nc.sync)

        def sb(name, shape, dt):
            return stack.enter_context(nc.sbuf_tensor(name, list(shape), dt))

        def op(eng, fn, reads=(), writes=()):
            dep.pre(eng, reads, writes)
            inst = fn()
            ev = eng.sig(inst)
            dep.post(ev, reads, writes)
            return inst

        def pe(fns, reads=(), writes=()):
            dep.pre(PE, reads, writes)
            inst = None
            for f in fns:
                inst = f()
            ev = PE.sig(inst)
            dep.post(ev, reads, writes)

        def dma(q, out, in_, semkey, reads=(), writes=()):
            dmas(q, [(out, in_)], semkey, reads, writes)

        def dmas(q, pairs, semkey, reads=(), writes=()):
            dep.pre(q, reads, writes)
            sem = ctx.sem(("dma", semkey))
            for out, in_ in pairs:
                q.e.dma_start(out=out, in_=in_).then_inc(sem, 16)
                ctx.dcnt[("dma", semkey)] += 16
            ev = (("dma", semkey), ctx.dcnt[("dma", semkey)])
            dep.post(ev, reads, writes)

        cst_f = sb("cst_f", [128, 2 * 128], F32)
        cst_b = sb("cst_b", [128, NCONST * 128], BF16)
        ident_f = cst_f[:, 0:128]
        ones_f = cst_f[:, 128:256]
        ident_b = cst_b[:, 0:128]
        ones_b = cst_b[:, 128:256]
        tri_b = cst_b[:, 256:384]
        mbig_b = cst_b[:, 384:512]
        e0_b = cst_b[:, 512:640]
        u_b = cst_b[:, 640:768]

        gtile = sb("gtile", [128, D], F32)
        btile = sb("btile", [128, D], F32)
        wdw_c = sb("wdw_c", [128, CT, c.CK], F32)
        cvp_c = sb("cvp_c", [128, 3, CT], F32)
        diag = [sb("diag%d" % i, [128, c.CK, 128], BF16) for i in range(2)]
        dg_rr = [0]
        hid_rr = [0]
        tmpH = sb("tmpH", [128, CT, HWc], BF16)

        Hs = sb("Hs", [128, c.NTMAX, D], F32)
        XT = sb("XT", [128, KC, c.TMAX], BF16)
        qz = sb("qz", [128, H, c.TMAX], BF16)
        kTn = sb("kTn", [128, HP, c.LMAX], BF16)
        vh = sb("vh", [128, c.NKB, AW], BF16)
        UBS = HWc + c.TPMAX
        UW = UBS + c.NSS * (HWc + c.DSEQ)
        ub = sb("ub", [128, CT, UW], BF16)
        NRING = 4
        ring = [sb("ring%d" % i, [128, 4096], BF16) for i in range(NRING)]
        cfh = sb("cfh", [128, max(CT * 1024, 4096)], BF16)
        hidT = [cfh[:, i * 2048:(i + 1) * 2048].rearrange("p (k t) -> p k t", k=4) for i in range(2)]
        cf = cfh[:, 0:CT * 1024].bitcast(F32).rearrange("p (k t) -> p k t", k=CT)
        e_t = [sb("e_t%d" % i, [128, 512], F32) for i in range(2)]
        sp_t = [sb("sp_t%d" % i, [128, 512], BF16) for i in range(2)]
        a_t = [sb("a_t%d" % i, [128, 512], BF16) for i in range(2)]
        NSTG = 3
        stg = [sb("stg%d" % i, [128, 512], F32) for i in range(NSTG)]
        tmpA = sb("tmpA", [128, 512], F32)
        tmpB = sb("tmpB", [128, 512], F32)
        rs_t = sb("rs_t", [128, 512], F32)
        mn_t = sb("mn_t", [128, 512], F32)
        st6 = sb("st6", [128, 2, 6], F32)
        mv = sb("mv", [128, 2], F32)
        rstd = sb("rstd", [128, 1], F32)
        nbias = sb("nbias", [128, 1], F32)
        kTnew = sb("kTnew", [128, HP, 128], BF16)
        vnew = sb("vnew", [128, AW], BF16)

        cf32 = cfh[:, 0:CT * 1024].bitcast(F32)
        kstage = [(stg[i], [("stg", i)]) for i in range(NSTG)]
        kstage += [(cf32[:, i * 512:(i + 1) * 512], [("cf", i), "cfh"]) for i in range(CT)]
        kst_rr = [0]
        PS = [stack.enter_context(nc.psum_tensor("ps%d" % i, [128, 512], F32)) for i in range(8)]
        ps_rr = [0]

        def psum_next(lo=0, hi=8):
            i = lo + ps_rr[0] % (hi - lo)
            ps_rr[0] += 1
            return i

        dma(SP, cst_f[:, :], consts[:, 0:256], "cst", writes=["cst_f"])
        dma(POOL, cst_b[:, :], consts[:, :], "cstb", writes=["cst_b"])
        prw, prc = stg[0], stg[1]
        dma(SP, prw[0:c.CK, 0:CC], w_dw[:, :], "prm_w", writes=[("stg", 0)])
        dma(SP, prc[0:3, 0:CC], cvp[:, :], "prm_c", writes=[("stg", 1)])
        for ct in range(CT):
            b = psum_next()
            pe([lambda: nc.tensor.transpose(out=PS[b][:, 0:c.CK], in_=prw[0:c.CK, ct * 128:(ct + 1) * 128],
                                            identity=ident_f[0:c.CK, 0:c.CK]),
                lambda: nc.tensor.transpose(out=PS[b][:, 32:35], in_=prc[0:3, ct * 128:(ct + 1) * 128],
                                            identity=ident_f[0:3, 0:3])],
               reads=[("stg", 0), ("stg", 1), "cst_f"], writes=[("PS", b)])
            op(DVE, lambda: nc.vector.tensor_copy(out=wdw_c[:, ct, :], in_=PS[b][:, 0:c.CK]), reads=[("PS", b)],
               writes=["wdw_c"])
            op(DVE, lambda: nc.vector.tensor_copy(out=cvp_c[:, :, ct], in_=PS[b][:, 32:35]), reads=[("PS", b)],
               writes=["cvp_c"])
        eps_c = sb("eps_c", [128, 1], F32)
        op(POOL, lambda: nc.gpsimd.memset(eps_c[:, :], LN_EPS), writes=["eps_c"])

        lnp_loaded = [None]

        def load_ln(which):
            if lnp_loaded[0] == which:
                return
            lnp_loaded[0] = which
            dma(ACT, gtile[:, :], lnp[2 * which, :].partition_broadcast(128), "gt", writes=["gtile"])
            dma(ACT, btile[:, :], lnp[2 * which + 1, :].partition_broadcast(128), "bt", writes=["btile"])

        chunks = []
        wstate = {"issued": 0, "cur": 0}

        def wsrc(desc):
            kind = desc[0]
            if kind in ("q", "k", "v"):
                c0 = {"q": 0, "k": AW, "v": 2 * AW}[kind]
                return [(0, KC, AW, w_in[:, c0:c0 + AW].rearrange("(k p) c -> p k c", p=128))]
            if kind == "ag":
                j = desc[1]
                cts = [t for t in (2 * j, 2 * j + 1) if t < CT]
                n = len(cts) * 128
                a0 = 3 * AW + cts[0] * 128
                g0 = 3 * AW + CC + cts[0] * 128
                return [("ag", n, w_in[:, a0:a0 + n].rearrange("(k p) c -> p k c", p=128),
                         w_in[:, g0:g0 + n].rearrange("(k p) c -> p k c", p=128))]
            if kind == "o":
                j = desc[1]
                return [(0, KC, c.OCW, w_out[:, j * c.OCW:(j + 1) * c.OCW].rearrange("(k p) c -> p k c", p=128))]
            if kind == "f1":
                j = desc[1]
                return [(0, KC, 512, w_ff1[:, j * 512:(j + 1) * 512].rearrange("(k p) c -> p k c", p=128))]
            if kind == "f2":
                j = desc[1]
                return [(0, 4, D, w_ff2[j * 512:(j + 1) * 512, :].rearrange("(k p) c -> p k c", p=128))]
            raise ValueError(kind)

        def wview(slot, a, b):
            return ring[slot][:, 0:a * b].rearrange("p (k c) -> p k c", k=a)

        def issue_chunk(m):
            slot = m % NRING
            desc = chunks[m]
            src = wsrc(desc)[0]
            key = ("W", slot)
            if src[0] == "ag":
                n = src[1]
                v = wview(slot, KC, 2 * n)
                dmas(POOL, [(v[:, :, 0:n], src[2]), (v[:, :, n:2 * n], src[3])], ("W", slot), writes=[key])
            else:
                _, a, b, ap = src
                dma(POOL, wview(slot, a, b), ap, ("W", slot), writes=[key])

        def wget():
            m = wstate["cur"]
            wstate["cur"] += 1
            while wstate["issued"] < min(len(chunks), m + NRING - 1):
                issue_chunk(wstate["issued"])
                wstate["issued"] += 1
            return m % NRING, ("W", m % NRING)

        passes = []
        for sq in range(c.NPS):
            for hf in range(c.NHALF):
                tiles = []
                col = 0
                if hf == 0:
                    tiles.append(dict(n=c.NMETA, c0=0, kb=0, pos=0, src=meta[:, :], y=None))
                    col = c.NMETA
                for f in range(hf * c.TPH, (hf + 1) * c.TPH):
                    tiles.append(dict(n=128, c0=col, kb=1 + f, pos=c.NMETA + 128 * f,
                                      src=xp[sq, 128 * f:128 * (f + 1), :], y=yp[sq, 128 * f:128 * (f + 1), :]))
                    col += 128
                groups = []
                i0 = 0
                if hf == 0:
                    groups.append([0])
                    i0 = 1
                fr = list(range(i0, len(tiles)))
                for g in range(0, len(fr), 4):
                    groups.append(fr[g:g + 4])
                passes.append(dict(kind="p", seq=sq, half=hf, tiles=tiles, groups=groups, T=col,
                                   last=(hf == c.NHALF - 1)))
        if c.NSS > 0:
            n = c.NSS * c.DSEQ
            stile = dict(n=n, c0=0, kb=None, pos=None, src=xs[:, :], y=ys[:, :], kind="s")
            if c.MERGE and passes:
                lp = passes[-1]
                stile["c0"] = lp["T"]
                lp["tiles"].append(stile)
                lp["groups"].append([len(lp["tiles"]) - 1])
                lp["Ttot"] = lp["T"] + n
            else:
                passes.append(dict(kind="s", tiles=[stile], groups=[[0]], T=0, Ttot=n, last=True, half=0, seq=0))
        for p_ in passes:
            p_.setdefault("Ttot", p_["T"])
        for p in passes:
            chunks.extend([("q",), ("k",), ("v",)])
            chunks.extend([("ag", j) for j in range((CT + 1) // 2)])
            chunks.extend([("o", j) for j in range(c.NOC)])
            for j in range(c.NFC):
                chunks.extend([("f1", j), ("f2", j)])

        while wstate["issued"] < min(len(chunks), NRING - 1):
            issue_chunk(wstate["issued"])
            wstate["issued"] += 1
        op(POOL, lambda: nc.gpsimd.memset(qz[:, :, :], 0.0), writes=["qz_init"])

        def gcols(p, g):
            ts = [p["tiles"][i] for i in g]
            c0 = ts[0]["c0"]
            W = sum(t["n"] for t in ts)
            return c0, W

        def layer_norm(i, n):
            hk = ("H", i)
            for j in range(D // 512 if D >= 512 else 1):
                w = min(512, D)
                op(DVE, lambda j=j, w=w: nc.vector.bn_stats(out=st6[:n, j, :], in_=Hs[:n, i, j * w:(j + 1) * w]),
                   reads=[hk], writes=[("st6", j)])
            nj = D // 512 if D >= 512 else 1
            op(DVE, lambda: nc.vector.bn_aggr(out=mv[:n, :], in_=st6[:n, 0:nj, :]),
               reads=[("st6", j) for j in range(nj)], writes=["mv"])
            op(ACT, lambda: nc.scalar.activation(out=rstd[:n, :], in_=mv[:n, 1:2], func=AF.Ln, bias=eps_c[:n, :], scale=1.0),
               reads=["mv", "eps_c"], writes=["rstd"])
            op(ACT, lambda: nc.scalar.activation(out=rstd[:n, :], in_=rstd[:n, :], func=AF.Exp, scale=-0.5),
               reads=["rstd"], writes=["rstd"])
            op(DVE, lambda: nc.vector.scalar_tensor_tensor(out=nbias[:n, :], in0=mv[:n, 0:1], scalar=-1.0,
                                                           in1=rstd[:n, :], op0=ALU.mult, op1=ALU.mult),
               reads=["mv", "rstd"], writes=["nbias"])
            op(ACT, lambda: nc.scalar.activation(out=Hs[:n, i, :], in_=Hs[:n, i, :], func=AF.Identity,
                                                 bias=nbias[:n, :], scale=rstd[:n, :]),
               reads=[hk, "rstd", "nbias"], writes=[hk])
            op(DVE, lambda: nc.vector.tensor_tensor(out=Hs[:n, i, :], in0=Hs[:n, i, :], in1=gtile[:n, :], op=ALU.mult),
               reads=[hk, "gtile"], writes=[hk])
            op(DVE, lambda: nc.vector.tensor_tensor(out=Hs[:n, i, :], in0=Hs[:n, i, :], in1=btile[:n, :], op=ALU.add),
               reads=[hk, "btile"], writes=[hk])

        st6a = sb("st6a", [128, c.NTMAX, 2, 6], F32)
        mva = sb("mva", [128, c.NTMAX, 2], F32)
        rsa = sb("rsa", [128, c.NTMAX], F32)
        nba = sb("nba", [128, c.NTMAX], F32)
        op(DVE, lambda: nc.vector.memset(mva[:, :, :], 1.0), writes=["mva"])

        def ln_multi(p, tiles_all, ids):
            i0, NT = ids[0], ids[-1] + 1
            tiles = [(i, tiles_all[i]) for i in ids]
            nj = D // 512 if D >= 512 else 1
            w = min(512, D)
            for i, t in tiles:
                n = t["n"]
                for j in range(nj):
                    op(DVE, lambda j=j: nc.vector.bn_stats(out=st6a[:n, i, j, :], in_=Hs[:n, i, j * w:(j + 1) * w]),
                       reads=[("H", i)], writes=[("st6a", i, j)])
                op(DVE, lambda: nc.vector.bn_aggr(out=mva[:n, i, :], in_=st6a[:n, i, 0:nj, :]),
                   reads=[("st6a", i, j) for j in range(nj)], writes=["mva"])
            op(ACT, lambda: nc.scalar.activation(out=rsa[:, i0:NT], in_=mva[:, i0:NT, 1], func=AF.Ln, bias=eps_c[:, :], scale=1.0),
               reads=["mva", "eps_c"], writes=["rsa"])
            op(ACT, lambda: nc.scalar.activation(out=rsa[:, i0:NT], in_=rsa[:, i0:NT], func=AF.Exp, scale=-0.5),
               reads=["rsa"], writes=["rsa"])
            for i, t in tiles:
                n = t["n"]
                hk = ("H", i)
                op(DVE, lambda: nc.vector.scalar_tensor_tensor(out=Hs[:n, i, :], in0=Hs[:n, i, :], scalar=mva[:n, i, 0:1],
                                                               in1=gtile[:n, :], op0=ALU.subtract, op1=ALU.mult),
                   reads=[hk, "mva", "gtile"], writes=[hk])
                op(DVE, lambda: nc.vector.scalar_tensor_tensor(out=Hs[:n, i, :], in0=Hs[:n, i, :], scalar=rsa[:n, i:i + 1],
                                                               in1=btile[:n, :], op0=ALU.mult, op1=ALU.add),
                   reads=[hk, "rsa", "btile"], writes=[hk])

        def transpose_tile(p, i, n, c0):
            for k0 in range(0, KC, 4):
                kn = min(4, KC - k0)
                b = psum_next()
                pk = ("PS", b)
                pe([lambda j=j: nc.tensor.transpose(out=PS[b][:, j * 128:j * 128 + n],
                                                    in_=Hs[:n, i, (k0 + j) * 128:(k0 + j + 1) * 128],
                                                    identity=ident_f[:n, :n]) for j in range(kn)],
                   reads=[("H", i)], writes=[pk])
                src = PS[b][:, 0:kn * 128].rearrange("p (k t) -> p k t", k=kn)[:, :, 0:n]
                op(ACT, lambda src=src, k0=k0, kn=kn: nc.scalar.copy(out=XT[:, k0:k0 + kn, c0:c0 + n], in_=src),
                   reads=[pk], writes=[("XT", i)])

        stg_rr = [0]

        def stg_next():
            i = stg_rr[0] % NSTG
            stg_rr[0] += 1
            return i

        def run_pass(p):
            tiles, groups = p["tiles"], p["groups"]
            sq = p.get("seq", 0)
            T = p["T"]
            isS = lambda i: tiles[i].get("kind") == "s"
            gS = lambda g: isS(g[0])
            has_s = any(isS(i) for i in range(len(tiles)))
            has_p = any(not isS(i) for i in range(len(tiles)))
            SW = HWc + c.DSEQ
            xtk = lambda g: [("XT", i) for i in g]

            for i, t in enumerate(tiles):
                if not t.get("preloaded"):
                    dma(SP, Hs[:t["n"], i, :], t["src"], ("Hld", i), writes=[("H", i)])
            load_ln(0)
            for g in groups:
                ln_multi(p, tiles, g)

            if c.stop == "ln":
                return
            slot, wk = wget()
            Wt = wview(slot, KC, AW)
            for g in groups:
                c0, W = gcols(p, g)
                for i in g:
                    transpose_tile(p, i, tiles[i]["n"], tiles[i]["c0"])
                for hp in range(HP):
                    b = psum_next()
                    pe([lambda kc=kc: nc.tensor.matmul(PS[b][:, 0:W], lhsT=Wt[:, kc, hp * 128:(hp + 1) * 128],
                                                       rhs=XT[:, kc, c0:c0 + W], start=(kc == 0), stop=(kc == KC - 1))
                        for kc in range(KC)], reads=[wk] + xtk(g), writes=[("PS", b)])
                    op(ACT, lambda: nc.scalar.copy(out=qz[0:64, 2 * hp, c0:c0 + W], in_=PS[b][0:64, 0:W]),
                       reads=[("PS", b), "qz_init"], writes=[("qz", 2 * hp, g[0])])
                    op(DVE, lambda: nc.vector.tensor_copy(out=qz[64:128, 2 * hp + 1, c0:c0 + W], in_=PS[b][64:128, 0:W]),
                       reads=[("PS", b), "qz_init"], writes=[("qz", 2 * hp + 1, g[0])])
            if c.stop == "q":
                return
            slot, wk = wget()
            Wt = wview(slot, KC, AW)
            for g in groups:
                c0, W = gcols(p, g)
                for hp in range(HP):
                    b = psum_next()
                    pe([lambda kc=kc: nc.tensor.matmul(PS[b][:, 0:W], lhsT=Wt[:, kc, hp * 128:(hp + 1) * 128],
                                                       rhs=XT[:, kc, c0:c0 + W], start=(kc == 0), stop=(kc == KC - 1))
                        for kc in range(KC)], reads=[wk] + xtk(g), writes=[("PS", b)])
                    if gS(g):
                        dst, dk = kTnew[:, hp, 0:W], [("kTnew",)]
                    else:
                        pos = tiles[g[0]]["pos"]
                        dst, dk = kTn[:, hp, pos:pos + W], [("kT", tiles[i_]["kb"]) for i_ in g]
                    op(ACT, lambda dst=dst: nc.scalar.mul(out=dst, in_=PS[b][:, 0:W], mul=-0.125),
                       reads=[("PS", b)], writes=dk)
                for i in g:
                    t = tiles[i]
                    n = t["n"]
                    b = psum_next()
                    pe([lambda kc=kc: nc.tensor.matmul(PS[b][:n, 0:AW], lhsT=XT[:, kc, t["c0"]:t["c0"] + n],
                                                       rhs=Wt[:, kc, :], start=(kc == 0), stop=(kc == KC - 1))
                        for kc in range(KC)], reads=[wk, ("XT", i)], writes=[("PS", b)])
                    s_ = stg_next()
                    op(DVE, lambda: nc.vector.tensor_copy(out=stg[s_][:n, 0:AW], in_=PS[b][:n, 0:AW]),
                       reads=[("PS", b)], writes=[("stg", s_)])
                    dst = ksm[0:n, :] if isS(i) else kp[sq, t["pos"]:t["pos"] + n, :]
                    dma(SP, dst, stg[s_][:n, 0:AW], ("stg", s_), reads=[("stg", s_)])
            if c.stop == "k":
                return
            slot, wk = wget()
            Wt = wview(slot, KC, AW)
            for g in groups:
                for i in g:
                    t = tiles[i]
                    n = t["n"]
                    b = psum_next()
                    pe([lambda kc=kc: nc.tensor.matmul(PS[b][:n, 0:AW], lhsT=XT[:, kc, t["c0"]:t["c0"] + n],
                                                       rhs=Wt[:, kc, :], start=(kc == 0), stop=(kc == KC - 1))
                        for kc in range(KC)], reads=[wk, ("XT", i)], writes=[("PS", b)])
                    s_ = stg_next()
                    op(DVE, lambda: nc.vector.tensor_copy(out=stg[s_][:n, 0:AW], in_=PS[b][:n, 0:AW]),
                       reads=[("PS", b)], writes=[("stg", s_)])
                    dst = vsm[0:n, :] if isS(i) else vp[sq, t["pos"]:t["pos"] + n, :]
                    dma(SP, dst, stg[s_][:n, 0:AW], ("stg", s_), reads=[("stg", s_)])
                    if isS(i):
                        op(ACT, lambda: nc.scalar.copy(out=vnew[:n, :], in_=PS[b][:n, 0:AW]),
                           reads=[("PS", b)], writes=["vnew"])
                    else:
                        op(ACT, lambda: nc.scalar.copy(out=vh[:n, t["kb"], :], in_=PS[b][:n, 0:AW]),
                           reads=[("PS", b)], writes=[("vh", t["kb"])])
            if c.stop == "v":
                return
            tail_tiles = []
            ptiles = [i for i in range(len(tiles)) if not isS(i)]
            if has_p and p["last"]:
                tail_tiles.append(ptiles[-1])
            tail_tiles += [i for i in range(len(tiles)) if isS(i)]
            if has_p and p["half"] == 0:
                op(POOL, lambda: nc.gpsimd.memset(ub[:, :, 0:HWc], 0.0), writes=[("ubh",)])
            if has_s:
                for s in range(c.NSS):
                    s_ = stg_next()
                    dma(SP, stg[s_][:HWc, 0:CC], sc[s, :, :], ("stgl", s_), writes=[("stg", s_)])
                    for ct in range(CT):
                        b = psum_next()
                        pe([lambda: nc.tensor.transpose(out=PS[b][:, 0:HWc], in_=stg[s_][:HWc, ct * 128:(ct + 1) * 128],
                                                        identity=ident_f[:HWc, :HWc])],
                           reads=[("stg", s_)], writes=[("PS", b)])
                        op(ACT, lambda: nc.scalar.copy(out=ub[:, ct, UBS + s * SW:UBS + s * SW + HWc], in_=PS[b][:, 0:HWc]),
                           reads=[("PS", b)], writes=[("ubhs",)])
            ust = {}
            for j in range((CT + 1) // 2):
                slot, wk = wget()
                cts = [t_ for t_ in (2 * j, 2 * j + 1) if t_ < CT]
                nn = len(cts) * 128
                Wt = wview(slot, KC, 2 * nn)
                for g in groups:
                    c0, W = gcols(p, g)
                    for ci, ct in enumerate(cts):
                        ba = psum_next()
                        pe([lambda kc=kc: nc.tensor.matmul(PS[ba][:, 0:W], lhsT=Wt[:, kc, ci * 128:(ci + 1) * 128],
                                                           rhs=XT[:, kc, c0:c0 + W], start=(kc == 0), stop=(kc == KC - 1))
                            for kc in range(KC)], reads=[wk] + xtk(g), writes=[("PS", ba)])
                        bg = psum_next()
                        pe([lambda kc=kc: nc.tensor.matmul(PS[bg][:, 0:W], lhsT=Wt[:, kc, nn + ci * 128:nn + (ci + 1) * 128],
                                                           rhs=XT[:, kc, c0:c0 + W], start=(kc == 0), stop=(kc == KC - 1))
                            for kc in range(KC)], reads=[wk] + xtk(g), writes=[("PS", bg)])
                        op(ACT, lambda: nc.scalar.activation(out=tmpA[:, 0:W], in_=PS[bg][:, 0:W], func=AF.Sigmoid),
                           reads=[("PS", bg)], writes=["tmpA"])
                        if gS(g):
                            dst = ub[:, ct, UBS:UBS + c.NSS * SW].rearrange("p (s j) -> p s j", j=SW)[:, :, HWc:SW]
                            src0 = PS[ba][:, 0:W].rearrange("p (s j) -> p s j", j=c.DSEQ)
                            src1 = tmpA[:, 0:W].rearrange("p (s j) -> p s j", j=c.DSEQ)
                        else:
                            dst = ub[:, ct, HWc + c0:HWc + c0 + W]
                            src0, src1 = PS[ba][:, 0:W], tmpA[:, 0:W]
                        op(DVE, lambda dst=dst, src0=src0, src1=src1: nc.vector.tensor_tensor(out=dst, in0=src0, in1=src1,
                                                                                             op=ALU.mult),
                           reads=[("PS", ba), "tmpA", ("ubh",), ("ubhs",)], writes=[("ub", ct, g[0])])
                for i in tail_tiles:
                    t = tiles[i]
                    n = t["n"]
                    if i not in ust:
                        ust[i] = stg_next()
                    ba = psum_next()
                    pe([lambda kc=kc: nc.tensor.matmul(PS[ba][:n, 0:nn], lhsT=XT[:, kc, t["c0"]:t["c0"] + n],
                                                       rhs=Wt[:, kc, 0:nn], start=(kc == 0), stop=(kc == KC - 1))
                        for kc in range(KC)], reads=[wk, ("XT", i)], writes=[("PS", ba)])
                    bg = psum_next()
                    pe([lambda kc=kc: nc.tensor.matmul(PS[bg][:n, 0:nn], lhsT=XT[:, kc, t["c0"]:t["c0"] + n],
                                                       rhs=Wt[:, kc, nn:2 * nn], start=(kc == 0), stop=(kc == KC - 1))
                        for kc in range(KC)], reads=[wk, ("XT", i)], writes=[("PS", bg)])
                    op(ACT, lambda: nc.scalar.activation(out=tmpA[:n, 0:nn], in_=PS[bg][:n, 0:nn], func=AF.Sigmoid),
                       reads=[("PS", bg)], writes=["tmpA"])
                    cc0 = cts[0] * 128
                    op(DVE, lambda: nc.vector.tensor_tensor(out=stg[ust[i]][:n, cc0:cc0 + nn], in0=PS[ba][:n, 0:nn],
                                                            in1=tmpA[:n, 0:nn], op=ALU.mult),
                       reads=[("PS", ba), "tmpA"], writes=[("stg", ust[i])])
            for i in tail_tiles:
                u_ = ust[i]
                if isS(i):
                    dmas(SP, [(cs[s, :, :], stg[u_][s * c.DSEQ + (c.DSEQ - HWc):(s + 1) * c.DSEQ, 0:CC])
                              for s in range(c.NSS)], ("stg", u_), reads=[("stg", u_)])
                else:
                    n = tiles[i]["n"]
                    dma(SP, cp[sq, :, :], stg[u_][n - HWc:n, 0:CC], ("stg", u_), reads=[("stg", u_)])

            if c.stop == "inproj":
                return
            def attn(streams, kbl, mode="all", next_prologue=None):
                nkb = len(kbl)
                ofirst = {}
                xt = [tmpA, tmpB]
                xk = ["tmpA", "tmpB"]
                eb = [[e_t[0], rs_t], [e_t[1], mn_t]]
                ek = [[("e", 0), "rs_t"], [("e", 1), "mn_t"]]

                def ctxs(bi, st):
                    kb = kbl[bi]
                    si = st["si"]
                    return kb, si, kb["nk"], kb["lo"], bi == 0, bi == nkb - 1, st["W"], st["segs"], st["qk"]

                def emit_Z(bi):
                    for st in streams:
                        kb, si, nk, lo, first, last, Wd, segs, rq = ctxs(bi, st)
                        zb = si
                        fns = []
                        for (h, qc0, zc0, w) in segs:
                            lhs = kb["kT"](h // 2)
                            fns.append(lambda lhs=lhs, h=h, qc0=qc0, zc0=zc0, w=w, nk=nk, lo=lo, kb=kb, zb=zb:
                                       nc.tensor.matmul(PS[zb][:nk, zc0 + lo:zc0 + w], lhsT=lhs,
                                                        rhs=qz[:, h, qc0 + lo:qc0 + w], start=True, stop=(kb["dw"] == 0)))
                            if kb["dw"]:
                                fns.append(lambda zc0=zc0, dw=kb["dw"], nk=nk, lo=lo, zb=zb:
                                           nc.tensor.matmul(PS[zb][:nk, zc0 + lo:zc0 + lo + dw], lhsT=ident_b[:nk, :nk],
                                                            rhs=mbig_b[:nk, 0:dw], start=False, stop=True))
                        pe(fns, reads=rq + kb["rk"] + ["cst_b"], writes=[("PS", zb)])

                def emit_e(bi):
                    for st in streams:
                        kb, si, nk, lo, first, last, Wd, segs, rq = ctxs(bi, st)
                        et, etk = eb[si][bi % 2], ek[si][bi % 2]
                        op(ACT, lambda: nc.scalar.activation(out=et[:nk, lo:Wd], in_=PS[si][:nk, lo:Wd], func=AF.Exp,
                                                             scale=-1.0), reads=[("PS", si)], writes=[etk])

                def emit_sp(bi):
                    for st in streams:
                        kb, si, nk, lo, first, last, Wd, segs, rq = ctxs(bi, st)
                        et, etk = eb[si][bi % 2], ek[si][bi % 2]
                        op(ACT, lambda: nc.scalar.activation(out=sp_t[si][:nk, lo:Wd], in_=et[:nk, lo:Wd], func=AF.Ln,
                                                             bias=1.0), reads=[etk], writes=[("sp", si)])

                def emit_tri(bi):
                    for st in streams:
                        kb, si, nk, lo, first, last, Wd, segs, rq = ctxs(bi, st)
                        rb = 2 + si
                        pe([lambda: nc.tensor.matmul(PS[rb][:, lo:Wd], lhsT=tri_b[:nk, :], rhs=sp_t[si][:nk, lo:Wd],
                                                     start=first, stop=True, skip_group_check=True)],
                           reads=[("sp", si), "cst_b"], writes=[("PS", rb)])

                def emit_X(bi):
                    for st in streams:
                        kb, si, nk, lo, first, last, Wd, segs, rq = ctxs(bi, st)
                        rb = 2 + si
                        op(ACT, lambda: nc.scalar.activation(out=xt[si][:nk, lo:Wd], in_=PS[rb][:nk, lo:Wd], func=AF.Exp,
                                                             scale=-1.0), reads=[("PS", rb)], writes=[xk[si]])

                def emit_U(bi):
                    for st in streams:
                        kb, si, nk, lo, first, last, Wd, segs, rq = ctxs(bi, st)
                        rb = 2 + si
                        if not last:
                            pe([lambda: nc.tensor.matmul(PS[rb][:, lo:Wd], lhsT=u_b[:nk, :], rhs=sp_t[si][:nk, lo:Wd],
                                                         start=False, stop=True, skip_group_check=True)],
                               reads=[("sp", si), "cst_b"], writes=[("PS", rb)])

                def emit_a(bi):
                    for st in streams:
                        kb, si, nk, lo, first, last, Wd, segs, rq = ctxs(bi, st)
                        et, etk = eb[si][bi % 2], ek[si][bi % 2]
                        op(DVE, lambda: nc.vector.tensor_tensor(out=a_t[si][:nk, lo:Wd], in0=et[:nk, lo:Wd],
                                                                in1=xt[si][:nk, lo:Wd], op=ALU.mult),
                           reads=[etk, xk[si]], writes=[("a", si)])

                def emit_O(bi):
                    for st in streams:
                        kb, si, nk, lo, first, last, Wd, segs, rq = ctxs(bi, st)
                        fns = []
                        wr = []
                        for sgi, (h, qc0, zc0, w) in enumerate(segs):
                            ob, oc0 = st["o"][sgi]
                            vl = kb["v"](h // 2)
                            stt = ob not in ofirst
                            ofirst[ob] = True
                            if ("PS", ob) not in wr:
                                wr.append(("PS", ob))
                            fns.append(lambda vl=vl, zc0=zc0, w=w, ob=ob, oc0=oc0, stt=stt, nk=nk, lo=lo, si=si:
                                       nc.tensor.matmul(PS[ob][:, oc0 + lo:oc0 + w], lhsT=vl, rhs=a_t[si][:nk, zc0 + lo:zc0 + w],
                                                        start=stt, stop=True, skip_group_check=True))
                        pe(fns, reads=[("a", si)] + kb["vk"], writes=wr)

                if mode in ("all", "prologue"):
                    emit_Z(0)
                    emit_e(0)
                    emit_sp(0)
                if mode == "prologue":
                    return
                for bi in range(nkb):
                    emit_tri(bi)
                    emit_X(bi)
                    if bi + 1 < nkb:
                        emit_Z(bi + 1)
                        emit_e(bi + 1)
                    emit_a(bi)
                    emit_U(bi)
                    if bi + 1 < nkb:
                        emit_sp(bi + 1)
                    elif next_prologue is not None:
                        next_prologue()
                    emit_O(bi)

            def attn1(st, kbl, hook=None):
                nkb = len(kbl)
                Wd = st["W"]
                segs = st["segs"]
                rq = st["qk"]
                assert Wd <= 256
                E4 = [(e_t[0], ("e", 0)), (rs_t, "rs_t"), (e_t[1], ("e", 1)), (mn_t, "mn_t")]
                XA = [(tmpA, "tmpA"), (tmpB, "tmpB")]
                ofirst = {}

                def spv(bi):
                    j = bi % 4
                    return sp_t[j // 2][:, (j % 2) * 256:(j % 2) * 256 + Wd], ("sp1", j)

                def front(bi):
                    kb = kbl[bi]
                    nk = kb["nk"]
                    zb = bi % 2
                    fns = []
                    for (h, qc0, zc0, w) in segs:
                        lhs = kb["kT"](h // 2)
                        fns.append(lambda lhs=lhs, h=h, qc0=qc0, zc0=zc0, w=w:
                                   nc.tensor.matmul(PS[zb][:nk, zc0:zc0 + w], lhsT=lhs, rhs=qz[:, h, qc0:qc0 + w],
                                                    start=True, stop=(kb["dw"] == 0)))
                        if kb["dw"]:
                            fns.append(lambda zc0=zc0, dw=kb["dw"]:
                                       nc.tensor.matmul(PS[zb][:nk, zc0:zc0 + dw], lhsT=ident_b[:nk, :nk],
                                                        rhs=mbig_b[:nk, 0:dw], start=False, stop=True))
                    pe(fns, reads=rq + kb["rk"] + ["cst_b"], writes=[("PS", zb)])
                    et, etk = E4[bi % 4]
                    spa, spk = spv(bi)
                    op(ACT, lambda: nc.scalar.activation(out=et[:nk, 0:Wd], in_=PS[zb][:nk, 0:Wd], func=AF.Exp, scale=-1.0),
                       reads=[("PS", zb)], writes=[etk])
                    op(ACT, lambda: nc.scalar.activation(out=spa[:nk, :], in_=et[:nk, 0:Wd], func=AF.Ln, bias=1.0),
                       reads=[etk], writes=[spk, ("sp", 0), ("sp", 1)])

                def emit_O(bi):
                    kb = kbl[bi]
                    nk = kb["nk"]
                    at = a_t[bi % 2]
                    fns, wr = [], []
                    for sgi, (h, qc0, zc0, w) in enumerate(segs):
                        ob, oc0 = st["o"][sgi]
                        vl = kb["v"](h // 2)
                        stt = ob not in ofirst
                        ofirst[ob] = True
                        if ("PS", ob) not in wr:
                            wr.append(("PS", ob))
                        fns.append(lambda vl=vl, zc0=zc0, w=w, ob=ob, oc0=oc0, stt=stt:
                                   nc.tensor.matmul(PS[ob][:, oc0:oc0 + w], lhsT=vl, rhs=at[:nk, zc0:zc0 + w],
                                                    start=stt, stop=True, skip_group_check=True))
                    pe(fns, reads=[("a", bi % 2)] + kb["vk"], writes=wr)

                front(0)
                if nkb > 1:
                    front(1)
                for bi in range(nkb):
                    kb = kbl[bi]
                    nk = kb["nk"]
                    first, last = bi == 0, bi == nkb - 1
                    spa, spk = spv(bi)
                    et, etk = E4[bi % 4]
                    xa, xak = XA[bi % 2]
                    pe([lambda: nc.tensor.matmul(PS[2][:, 0:Wd], lhsT=tri_b[:nk, :], rhs=spa[:nk, :], start=first, stop=True,
                                                 skip_group_check=True)], reads=[spk, "cst_b"], writes=[("PS", 2)])
                    if bi > 0:
                        emit_O(bi - 1)
                    op(ACT, lambda: nc.scalar.activation(out=xa[:nk, 0:Wd], in_=PS[2][:nk, 0:Wd], func=AF.Exp, scale=-1.0),
                       reads=[("PS", 2)], writes=[xak])
                    if bi + 2 < nkb:
                        front(bi + 2)
                    op(DVE, lambda: nc.vector.tensor_tensor(out=a_t[bi % 2][:nk, 0:Wd], in0=et[:nk, 0:Wd], in1=xa[:nk, 0:Wd],
                                                            op=ALU.mult), reads=[etk, xak], writes=[("a", bi % 2)])
                    if not last:
                        pe([lambda: nc.tensor.matmul(PS[2][:, 0:Wd], lhsT=u_b[:nk, :], rhs=spa[:nk, :], start=False, stop=True,
                                                     skip_group_check=True)], reads=[spk, "cst_b"], writes=[("PS", 2)])
                    if hook is not None:
                        hook(bi)
                emit_O(nkb - 1)

            def prompt_attn(g):
                c0, W = gcols(p, g)
                kb_hi = tiles[g[-1]]["kb"]
                kb_lo = tiles[g[0]]["kb"]
                kbl = []
                for kbx in range(kb_hi, -1, -1):
                    nk = c.NMETA if kbx == 0 else 128
                    kpos = 0 if kbx == 0 else c.NMETA + 128 * (kbx - 1)
                    if kbx >= kb_lo:
                        lo, dw = (kbx - kb_lo) * 128, nk
                    else:
                        lo, dw = 0, 0
                    kbl.append(dict(nk=nk, lo=lo, dw=dw,
                                    kT=(lambda hp, kpos=kpos, nk=nk: kTn[:, hp, kpos:kpos + nk]),
                                    v=(lambda hp, kbx=kbx, nk=nk: vh[:nk, kbx, hp * 128:(hp + 1) * 128]),
                                    rk=[("kT", kbx)], vk=[("vh", kbx)]))
                for hp in range(HP):
                    streams = [dict(si=si, segs=[(2 * hp + si, c0, 0, W)], W=W, qk=[("qz", 2 * hp + si, g[0])],
                                    o=[(6 + si, 0)]) for si in range(2)]

                    def evac(hp=hp):
                        op(ACT, lambda: nc.scalar.copy(out=XT[0:64, hp, c0:c0 + W], in_=PS[6][0:64, 0:W]),
                           reads=[("PS", 6)], writes=[("XT", i) for i in g])
                        op(DVE, lambda: nc.vector.tensor_copy(out=XT[64:128, hp, c0:c0 + W], in_=PS[7][64:128, 0:W]),
                           reads=[("PS", 7)], writes=[("XT", i) for i in g])
                    attn_jobs.append((streams, kbl, evac))

            def sample_attn(g):
                ti = g[0]
                cs0 = tiles[ti]["c0"]
                NCB = c.PAST // 128
                DS = c.DSEQ
                VG = 4 if NCB % 4 == 0 else 1
                allk = [("kT", x) for x in range(c.NKB + 1)]

                def prep_new(s):
                    dma(SP, vh[0:DS, NCB, :], vnew[s * DS:(s + 1) * DS, :], ("vnl",), reads=["vnew"],
                        writes=[("vh", NCB)])

                def prep_block(s, kb):
                    if kb % VG == 0:
                        dma(POOL, vh[:, kb:kb + VG, :], cv[s, kb * 128:(kb + VG) * 128, :].rearrange("(k p) c -> p k c", p=128),
                            ("cvl", (kb // VG) % 4), writes=[("vh", x) for x in range(kb, kb + VG)])
                    ks_ = kst_rr[0] % len(kstage)
                    kst_rr[0] += 1
                    kbuf, kkey = kstage[ks_]
                    dma(SP, kbuf[:, 0:AW], ck[s, kb * 128:(kb + 1) * 128, :], ("kstl", ks_), writes=kkey)
                    b = psum_next(4, 6)
                    pe([lambda hp=hp: nc.tensor.transpose(out=PS[b][:, hp * 128:(hp + 1) * 128],
                                                          in_=kbuf[:, hp * 128:(hp + 1) * 128], identity=ident_f[:, :])
                        for hp in range(HP)], reads=kkey + ["cst_f"], writes=[("PS", b)])
                    op(ACT, lambda: nc.scalar.mul(out=kTn[:, 0:HP, kb * 128:(kb + 1) * 128],
                                                  in_=PS[b][:, 0:HP * 128].rearrange("p (k t) -> p k t", k=HP), mul=-0.125),
                       reads=[("PS", b)], writes=[("kTs", kb)] + (allk if s == 0 else []))

                prep_new(0)
                for kb in range(NCB - 1, -1, -1):
                    prep_block(0, kb)
                for s in range(c.NSS):
                    kbl = [dict(nk=DS, lo=0, dw=DS, kT=(lambda hp, s=s: kTnew[:, hp, s * DS:(s + 1) * DS]),
                                v=(lambda hp: vh[:DS, NCB, hp * 128:(hp + 1) * 128]), rk=[("kTnew",)], vk=[("vh", NCB)])]
                    for kb in range(NCB - 1, -1, -1):
                        kbl.append(dict(nk=128, lo=0, dw=0, kT=(lambda hp, kb=kb: kTn[:, hp, kb * 128:(kb + 1) * 128]),
                                        v=(lambda hp, kb=kb: vh[:, kb, hp * 128:(hp + 1) * 128]),
                                        rk=[("kTs", kb)], vk=[("vh", kb)]))
                    st = dict(si=0, segs=[(h, cs0 + s * DS, h * DS, DS) for h in range(H)], W=H * DS,
                              qk=[("qz", h, ti) for h in range(H)], o=[(6 + (h % 2), (h // 2) * DS) for h in range(H)])
                    todo = list(range(NCB - 1, -1, -1)) if s + 1 < c.NSS else []
                    state = {"new": s + 1 < c.NSS}

                    def hook(bi, s=s, todo=todo, state=state):
                        if state["new"] and bi >= 2:
                            prep_new(s + 1)
                            state["new"] = False
                        while todo:
                            kb = todo[0]
                            need = (NCB - (kb - (kb % VG))) + 1 if VG > 1 else (NCB - kb) + 1
                            if bi < need:
                                break
                            prep_block(s + 1, todo.pop(0))
                    attn1(st, kbl, hook)
                    for kb in todo[:]:
                        prep_block(s + 1, todo.pop(0))
                    if state["new"]:
                        prep_new(s + 1)
                    op(ACT, lambda: nc.scalar.copy(out=XT[0:64, 0:HP, cs0 + s * DS:cs0 + (s + 1) * DS],
                                                   in_=PS[6][0:64, 0:HP * DS].rearrange("p (k t) -> p k t", k=HP)),
                       reads=[("PS", 6)], writes=[("XT", ti)])
                    op(DVE, lambda: nc.vector.tensor_copy(out=XT[64:128, 0:HP, cs0 + s * DS:cs0 + (s + 1) * DS],
                                                          in_=PS[7][64:128, 0:HP * DS].rearrange("p (k t) -> p k t", k=HP)),
                       reads=[("PS", 7)], writes=[("XT", ti)])

            lgroups = [g for g in groups if any(tiles[i]["y"] is not None for i in g)]
            attn_jobs = []
            for g in lgroups:
                if not gS(g):
                    prompt_attn(g)
            for n_, (st_, kb_, ev_) in enumerate(attn_jobs):
                if n_ == 0:
                    attn(st_, kb_, mode="prologue")
                nxt_ = attn_jobs[n_ + 1] if n_ + 1 < len(attn_jobs) else None
                attn(st_, kb_, mode="body",
                     next_prologue=(lambda nxt_=nxt_: attn(nxt_[0], nxt_[1], mode="prologue")) if nxt_ else None)
                ev_()
            for g in lgroups:
                if gS(g):
                    sample_attn(g)

            if c.stop == "attn":
                return
            def conv_group(g):
                c0, W = gcols(p, g)
                if gS(g):
                    csegs = [(UBS + s * SW, c.DSEQ, s * c.DSEQ) for s in range(c.NSS)]
                else:
                    csegs = [(c0, W, 0)]
                s1b, s2b = psum_next(), psum_next()
                while s2b == s1b:
                    s2b = psum_next()
                ubk = [("ub", ct_, g_[0]) for ct_ in range(CT) for g_ in groups] + [("ubh",), ("ubhs",)]
                pend_stats = [None]
                for ct in range(CT):
                    db = dg_rr[0] % 2
                    dg_rr[0] += 1
                    op(DVE, lambda: nc.vector.tensor_tensor(
                        out=diag[db][:, :, :], in0=ident_b.unsqueeze(1).to_broadcast([128, c.CK, 128]),
                        in1=wdw_c[:, ct, :].unsqueeze(2).to_broadcast([128, c.CK, 128]), op=ALU.mult),
                       reads=["cst_b", "wdw_c"], writes=[("diag", db, 0)])
                    b = psum_next()
                    while b in (s1b, s2b):
                        b = psum_next()
                    fns = []
                    for (u0, w, o0) in csegs:
                        for k in range(c.CK):
                            fns.append(lambda u0=u0, w=w, o0=o0, k=k:
                                       nc.tensor.matmul(PS[b][:, o0:o0 + w], lhsT=diag[db][:, k, :], rhs=ub[:, ct, u0 + k:u0 + k + w],
                                                        start=(k == 0), stop=(k == c.CK - 1)))
                    pe(fns, reads=[("diag", db, 0)] + ubk, writes=[("PS", b)])
                    op(ACT, lambda: nc.scalar.activation(out=cf[:, ct, 0:W], in_=PS[b][:, 0:W], func=AF.Identity,
                                                         bias=cvp_c[:, 0, ct:ct + 1], scale=1.0),
                       reads=[("PS", b), "cvp_c"], writes=[("cf", ct), "cfh"])
                    sqb = [(e_t[0], ("e", 0)), (e_t[1], ("e", 1))][ct % 2]
                    op(ACT, lambda: nc.scalar.activation(out=sqb[0][:, 0:W], in_=cf[:, ct, 0:W], func=AF.Square),
                       reads=[("cf", ct), "cfh"], writes=[sqb[1]])
                    def stats(ct=ct, sqb=sqb):
                        pe([lambda: nc.tensor.matmul(PS[s1b][:, 0:W], lhsT=ones_f[:, :], rhs=cf[:, ct, 0:W], start=(ct == 0),
                                                     stop=(ct == CT - 1), skip_group_check=True)],
                           reads=[("cf", ct), "cfh", "cst_f"], writes=[("PS", s1b)])
                        pe([lambda: nc.tensor.matmul(PS[s2b][:, 0:W], lhsT=ones_f[:, :], rhs=sqb[0][:, 0:W], start=(ct == 0),
                                                     stop=(ct == CT - 1), skip_group_check=True)],
                           reads=[sqb[1], "cst_f"], writes=[("PS", s2b)])
                    if pend_stats[0] is not None:
                        pend_stats[0]()
                    pend_stats[0] = stats
                pend_stats[0]()
                pend_stats[0] = None
                if has_p and g is [g_ for g_ in groups if not gS(g_)][-1] and not p["last"]:
                    op(POOL, lambda: nc.gpsimd.tensor_copy(out=tmpH[:, :, :], in_=ub[:, :, T:T + HWc]),
                       reads=ubk, writes=["tmpH"])
                    op(POOL, lambda: nc.gpsimd.tensor_copy(out=ub[:, :, 0:HWc], in_=tmpH[:, :, :]),
                       reads=["tmpH"], writes=[("ubh",)])
                inv = 1.0 / CC
                op(DVE, lambda: nc.vector.tensor_scalar(out=mn_t[:, 0:W], in0=PS[s1b][:, 0:W], scalar1=inv, scalar2=None,
                                                        op0=ALU.mult), reads=[("PS", s1b)], writes=["mn_t"])
                op(DVE, lambda: nc.vector.tensor_tensor(out=tmpB[:, 0:W], in0=mn_t[:, 0:W], in1=mn_t[:, 0:W], op=ALU.mult),
                   reads=["mn_t"], writes=["tmpB"])
                op(DVE, lambda: nc.vector.scalar_tensor_tensor(out=rs_t[:, 0:W], in0=PS[s2b][:, 0:W], scalar=inv,
                                                               in1=tmpB[:, 0:W], op0=ALU.mult, op1=ALU.subtract),
                   reads=[("PS", s2b), "tmpB"], writes=["rs_t"])
                op(ACT, lambda: nc.scalar.activation(out=rs_t[:, 0:W], in_=rs_t[:, 0:W], func=AF.Ln, bias=eps_c[:, :], scale=1.0),
                   reads=["rs_t", "eps_c"], writes=["rs_t"])
                op(ACT, lambda: nc.scalar.activation(out=rs_t[:, 0:W], in_=rs_t[:, 0:W], func=AF.Exp, scale=-0.5),
                   reads=["rs_t"], writes=["rs_t"])
                for ct in range(CT):
                    op(DVE, lambda: nc.vector.tensor_tensor(out=tmpA[:, 0:W], in0=cf[:, ct, 0:W], in1=mn_t[:, 0:W],
                                                            op=ALU.subtract), reads=[("cf", ct), "cfh", "mn_t"], writes=["tmpA"])
                    op(DVE, lambda: nc.vector.tensor_tensor(out=tmpA[:, 0:W], in0=tmpA[:, 0:W], in1=rs_t[:, 0:W],
                                                            op=ALU.mult), reads=["tmpA", "rs_t"], writes=["tmpA"])
                    op(ACT, lambda: nc.scalar.activation(out=XT[:, HP + ct, c0:c0 + W], in_=tmpA[:, 0:W], func=AF.Silu,
                                                         bias=cvp_c[:, 2, ct:ct + 1], scale=cvp_c[:, 1, ct:ct + 1]),
                       reads=["tmpA", "cvp_c"], writes=[("XT", i) for i in g])

            load_ln(1)
            osl = [wget() for j in range(c.NOC)]

            def outproj_group(g):
                for i in g:
                    t = tiles[i]
                    n = t["n"]
                    for j in range(c.NOC):
                        slot, wk = osl[j]
                        Wt = wview(slot, KC, c.OCW)
                        b = psum_next()
                        pe([lambda kc=kc: nc.tensor.matmul(PS[b][:n, 0:c.OCW], lhsT=XT[:, kc, t["c0"]:t["c0"] + n],
                                                           rhs=Wt[:, kc, :], start=(kc == 0), stop=(kc == KC - 1))
                            for kc in range(KC)], reads=[wk, ("XT", i)], writes=[("PS", b)])
                        hsl = Hs[:n, i, j * c.OCW:(j + 1) * c.OCW]
                        op(DVE, lambda: nc.vector.scalar_tensor_tensor(out=hsl, in0=hsl, scalar=c.ALPHA,
                                                                       in1=PS[b][:n, 0:c.OCW], op0=ALU.mult, op1=ALU.add),
                           reads=[("PS", b), ("H", i)], writes=[("H", i)])
                ln_multi(p, tiles, g)
                for i in g:
                    transpose_tile(p, i, tiles[i]["n"], tiles[i]["c0"])

            conv_group(lgroups[0])
            for gi, g in enumerate(lgroups):
                if gi + 1 < len(lgroups):
                    conv_group(lgroups[gi + 1])
                outproj_group(g)

            if c.stop == "out":
                return
            load_ln(2)
            for j in range(c.NFC):
                s1, wk1 = wget()
                s2, wk2 = wget()
                W1 = wview(s1, KC, 512)
                W2 = wview(s2, 4, D)

                def ffn1(g, W1=W1, wk1=wk1):
                    c0, W = gcols(p, g)
                    hb = hid_rr[0] % 2
                    hid_rr[0] += 1
                    for fc in range(4):
                        b = psum_next()
                        pe([lambda kc=kc: nc.tensor.matmul(PS[b][:, 0:W], lhsT=W1[:, kc, fc * 128:(fc + 1) * 128],
                                                           rhs=XT[:, kc, c0:c0 + W], start=(kc == 0), stop=(kc == KC - 1))
                            for kc in range(KC)], reads=[wk1] + xtk(g), writes=[("PS", b)])
                        op(ACT, lambda: nc.scalar.activation(out=e_t[fc % 2][:, 0:W], in_=PS[b][:, 0:W], func=AF.Relu),
                           reads=[("PS", b)], writes=[("e", fc % 2)])
                        op(DVE, lambda: nc.vector.tensor_tensor(out=hidT[hb][:, fc, 0:W], in0=e_t[fc % 2][:, 0:W],
                                                                in1=e_t[fc % 2][:, 0:W], op=ALU.mult),
                           reads=[("e", fc % 2)], writes=[("hid", hb), "cfh"])
                    return hb

                def ffn2(g, hb, W2=W2, wk2=wk2, j=j):
                    c0, W = gcols(p, g)
                    for i in g:
                        t = tiles[i]
                        n = t["n"]
                        off = t["c0"] - c0
                        for oc in range(c.NOC):
                            b = psum_next()
                            pe([lambda fc=fc: nc.tensor.matmul(PS[b][:n, 0:c.OCW], lhsT=hidT[hb][:, fc, off:off + n],
                                                               rhs=W2[:, fc, oc * c.OCW:(oc + 1) * c.OCW], start=(fc == 0),
                                                               stop=(fc == 3)) for fc in range(4)],
                               reads=[wk2, ("hid", hb), "cfh"], writes=[("PS", b)])
                            hsl = Hs[:n, i, oc * c.OCW:(oc + 1) * c.OCW]
                            if j == 0:
                                op(DVE, lambda: nc.vector.scalar_tensor_tensor(out=hsl, in0=hsl, scalar=c.ALPHA,
                                                                               in1=PS[b][:n, 0:c.OCW], op0=ALU.mult, op1=ALU.add),
                                   reads=[("PS", b), ("H", i)], writes=[("H", i)])
                            else:
                                op(DVE, lambda: nc.vector.tensor_tensor(out=hsl, in0=hsl, in1=PS[b][:n, 0:c.OCW], op=ALU.add),
                                   reads=[("PS", b), ("H", i)], writes=[("H", i)])
                        if j == c.NFC - 1 and g is lgroups[-1]:
                            ln_multi(p, tiles, [i])
                    if j == c.NFC - 1:
                        if g is not lgroups[-1]:
                            ln_multi(p, tiles, g)
                        for i in g:
                            t = tiles[i]
                            n = t["n"]
                            if t["y"] is not None:
                                dma(SP, t["y"], Hs[:n, i, :], ("Hst", i), reads=[("H", i)])
                        for i in g:
                            nxt = p.get("next")
                            if nxt is not None and i < len(nxt["tiles"]):
                                tn = nxt["tiles"][i]
                                dma(SP, Hs[:tn["n"], i, :], tn["src"], ("Hld", i), writes=[("H", i)])
                                tn["preloaded"] = True

                hbs = {}
                hbs[0] = ffn1(lgroups[0])
                for gi, g in enumerate(lgroups):
                    if gi + 1 < len(lgroups):
                        hbs[gi + 1] = ffn1(lgroups[gi + 1])
                    ffn2(g, hbs[gi])
            nxt = p.get("next")
            for g in groups:
                if g not in lgroups:
                    for i in g:
                        if nxt is not None and i < len(nxt["tiles"]) and not nxt["tiles"][i].get("preloaded"):
                            tn = nxt["tiles"][i]
                            dma(SP, Hs[:tn["n"], i, :], tn["src"], ("Hld", i), writes=[("H", i)])
                            tn["preloaded"] = True

        for pi_, p in enumerate(passes):
            p["next"] = passes[pi_ + 1] if pi_ + 1 < len(passes) else None
        for p in passes:
            run_pass(p)
        assert c.stop or wstate["cur"] == len(chunks)
        for k, v in ctx.dcnt.items():
            if v:
                SP.wait(k, v)
        SP.e.nop() if hasattr(SP.e, "nop") else None
    return nc


def make_consts():
    cst = np.zeros((128, NCONST * 128), np.float32)
    j = np.arange(128)[:, None]
    s = np.arange(128)[None, :]
    cst[:, 0:128] = (j == s)
    cst[:, 128:256] = 1.0
    cst[:, 256:384] = (j >= s)
    cst[:, 384:512] = BIG * (j >= s)
    cst[0, 512:640] = 1.0
    cst[:, 640:768] = (j < s)
    return cst


def core_inputs(c, core, inp):
    f = lambda a: np.ascontiguousarray(a, dtype=np.float32)
    ps = slice(core * c.NPS, (core + 1) * c.NPS)
    ss = slice(core * c.NSS, (core + 1) * c.NSS)
    return {
        "xp": f(inp["x_prompt"][ps]),
        "xs": f(inp["x_sample"][ss].reshape(c.NSS * c.DSEQ, c.D)),
        "ck": f(inp["cache_k"][0, ss].reshape(c.NSS, c.PAST, c.AW)),
        "cv": f(inp["cache_v"][0, ss].reshape(c.NSS, c.PAST, c.AW)),
        "sc": f(inp["state_conv"][0, ss]),
        "meta": f(inp["meta"]),
        "lnp": f(np.stack([inp["g_in"], inp["b_in"], inp["g_ln1"][0], inp["b_ln1"][0], inp["g_ln2"][0], inp["b_ln2"][0]])),
        "w_in": f(inp["w_in"][0]),
        "w_dw": f(inp["w_dw"][0]),
        "cvp": f(np.stack([inp["b_dw"][0], inp["g_conv"][0], inp["b_conv"][0]])),
        "w_out": f(inp["w_out"][0]),
        "w_ff1": f(inp["w_ff1"][0]),
        "w_ff2": f(inp["w_ff2"][0]),
        "consts": make_consts(),
    }


def assemble(c, res, ncores):
    cat = lambda k: np.concatenate([np.asarray(r[k]) for r in res], axis=0)
    B = ncores * c.NPS
    DB = ncores * c.NSS
    yp = cat("yp").reshape(B, c.SEQ, c.D)
    ys = cat("ys").reshape(DB, c.DSEQ, c.D)
    kp = cat("kp").reshape(1, B, c.LP, c.H, 64)
    vp = cat("vp").reshape(1, B, c.LP, c.H, 64)
    cp = cat("cp").reshape(1, B, c.HW, c.CC)
    ksm = cat("ksm").reshape(1, DB, c.DSEQ, c.H, 64)
    vsm = cat("vsm").reshape(1, DB, c.DSEQ, c.H, 64)
    cs = cat("cs").reshape(1, DB, c.HW, c.CC)
    return tuple(np.ascontiguousarray(a, dtype=np.float32) for a in (yp, ys, kp, vp, cp, ksm, vsm, cs))


_NC_CACHE = {}


def kernel(**inputs):
    ncores = 8
    c = Cfg()
    if "nc" not in _NC_CACHE:
        _NC_CACHE["nc"] = build(c)
    nc = _NC_CACHE["nc"]
    in_maps = [core_inputs(c, i, inputs) for i in range(ncores)]
    res = run_bass_kernel_spmd(nc, in_maps, core_ids=list(range(ncores)))
    return assemble(c, res.results, ncores)
```
